# Optimizing a Trainium2 kernel written in Bass

```python
import math
import jax, jax.numpy as jnp
from jax import lax
import numpy as np

D_MODEL = 1024
BATCH = 2
SEQ = 16384
DEPTH = 4
DEC_BATCH = 16
DEC_SEQ = 4096
PAST_LEN = 128

N_MIXERS = 3
N_A = (DEPTH + 2) // 3
N_B = (DEPTH + 1) // 3
N_C = DEPTH // 3
DA_HEADS = 8
DA_DK = 64
DA_DV = 2 * DA_DK
Q_BLOCK = 128
WA_HEADS = 16
WA_KV_HEADS = 4
WA_GROUP = WA_HEADS // WA_KV_HEADS
WA_DH = 64
WINDOW = 128
BLOCK = 128
D_RNN = 1536
RG_BLOCKS = 16
RG_BW = D_RNN // RG_BLOCKS
RG_C = 8.0
RG_CONV_W = 4
RG_CONV_LEFT = 2
D_FF = 2816
FFN_CONV_W = 3
FFN_CONV_LEFT = 1
ROPE_THETA = 500000.0
ROT_FRAC = 4
EPS = 1e-6

kernel_name = "hybrid_bidir_diffattn_swa_rglru_convffn"


def rmsnorm(x, g):
    xf = x.astype(jnp.float32)
    y = xf * lax.rsqrt(jnp.mean(xf * xf, axis=-1, keepdims=True) + EPS)
    return (y * g.astype(jnp.float32)).astype(x.dtype)


def rope_partial(x):
    S, dh = x.shape[1], x.shape[-1]
    rot = dh // ROT_FRAC
    half = rot // 2
    inv = ROPE_THETA ** (-jnp.arange(half, dtype=jnp.float32) * 2.0 / rot)
    ang = jnp.arange(S, dtype=jnp.float32)[:, None] * inv[None, :]
    bshape = (1, S) + (1,) * (x.ndim - 3) + (half,)
    cos = jnp.cos(ang).reshape(bshape)
    sin = jnp.sin(ang).reshape(bshape)
    xf = x.astype(jnp.float32)
    x1, x2 = xf[..., :half], xf[..., half:rot]
    out = jnp.concatenate([x1 * cos - x2 * sin, x2 * cos + x1 * sin, xf[..., rot:]], axis=-1)
    return out.astype(x.dtype)


def dwconv(x, w, b, left):
    K, C = w.shape
    y = lax.conv_general_dilated(x, w.reshape(K, 1, C).astype(x.dtype), window_strides=(1,),
                                 padding=[(left, K - 1 - left)],
                                 dimension_numbers=('NWC', 'WIO', 'NWC'),
                                 feature_group_count=C)
    return y + b.astype(x.dtype)


def diff_lambda_init(layer_idx):
    return 0.8 - 0.6 * math.exp(-0.3 * layer_idx)


def diff_attention(h, w_qkv, q_g, k_g, lq1, lk1, lq2, lk2, sub_g, w_o, lambda_init):
    B, S, _ = h.shape
    nq = S // Q_BLOCK
    qk_w = DA_HEADS * 2 * DA_DK
    qkv = h @ w_qkv
    q = qkv[..., :qk_w].reshape(B, S, DA_HEADS, 2, DA_DK)
    k = qkv[..., qk_w:2 * qk_w].reshape(B, S, DA_HEADS, 2, DA_DK)
    v = qkv[..., 2 * qk_w:].reshape(B, S, DA_HEADS, DA_DV)
    q = rope_partial(rmsnorm(q, q_g)) * (DA_DK ** -0.5)
    k = rope_partial(rmsnorm(k, k_g))
    f32 = jnp.float32
    lam = (jnp.exp(jnp.sum(lq1.astype(f32) * lk1.astype(f32)))
           - jnp.exp(jnp.sum(lq2.astype(f32) * lk2.astype(f32))) + lambda_init)
    q_blocks = jnp.moveaxis(q.reshape(B, nq, Q_BLOCK, DA_HEADS, 2, DA_DK), 1, 0)

    def attend(qb):
        s = jnp.einsum('bqhmd,bkhmd->bhmqk', qb, k, preferred_element_type=f32)
        p = jax.nn.softmax(s, axis=-1)
        w = p[:, :, 0] - lam * p[:, :, 1]
        return jnp.einsum('bhqk,bkhd->bqhd', w.astype(v.dtype), v)

    o = lax.map(attend, q_blocks)
    o = jnp.moveaxis(o, 0, 1).reshape(B, S, DA_HEADS, DA_DV)
    o = rmsnorm(o, sub_g) * (1.0 - lambda_init)
    return o.reshape(B, S, DA_HEADS * DA_DV) @ w_o


def window_attention(h, w_qkv, q_g, k_g, sink, w_o):
    B, S, _ = h.shape
    nb = S // BLOCK
    q_w = WA_HEADS * WA_DH
    kv_w = WA_KV_HEADS * WA_DH
    qkv = h @ w_qkv
    q = qkv[..., :q_w].reshape(B, S, WA_HEADS, WA_DH)
    k = qkv[..., q_w:q_w + kv_w].reshape(B, S, WA_KV_HEADS, WA_DH)
    v = qkv[..., q_w + kv_w:].reshape(B, S, WA_KV_HEADS, WA_DH)
    q = rope_partial(rmsnorm(q, q_g)) * (WA_DH ** -0.5)
    k = rope_partial(rmsnorm(k, k_g))
    pad = ((0, 0), (BLOCK, BLOCK), (0, 0), (0, 0))
    kp = jnp.pad(k, pad)
    vp = jnp.pad(v, pad)
    q_blocks = jnp.moveaxis(q.reshape(B, nb, BLOCK, WA_KV_HEADS, WA_GROUP, WA_DH), 1, 0)
    sink_b = sink.astype(jnp.float32).reshape(1, WA_KV_HEADS, WA_GROUP, 1, 1)
    q_off = jnp.arange(BLOCK)
    k_off = jnp.arange(3 * BLOCK)

    def attend(args):
        qb, j = args
        kb = lax.dynamic_slice_in_dim(kp, j * BLOCK, 3 * BLOCK, axis=1)
        vb = lax.dynamic_slice_in_dim(vp, j * BLOCK, 3 * BLOCK, axis=1)
        qpos = j * BLOCK + q_off
        kpos = (j - 1) * BLOCK + k_off
        valid = ((kpos >= 0) & (kpos < S))[None, :] & (jnp.abs(qpos[:, None] - kpos[None, :]) <= WINDOW)
        s = jnp.einsum('bqkgd,bskd->bkgqs', qb, kb, preferred_element_type=jnp.float32)
        s = jnp.where(valid, s, -jnp.inf)
        m = jnp.maximum(jnp.max(s, axis=-1, keepdims=True), sink_b)
        p = jnp.exp(s - m)
        p = p / (jnp.sum(p, axis=-1, keepdims=True) + jnp.exp(sink_b - m))
        return jnp.einsum('bkgqs,bskd->bqkgd', p.astype(vb.dtype), vb)

    o = lax.map(attend, (q_blocks, jnp.arange(nb)))
    o = jnp.moveaxis(o, 0, 1).reshape(B, S, WA_HEADS * WA_DH)
    return o @ w_o


def _lru_combine(left, right):
    a_l, b_l = left
    a_r, b_r = right
    return a_l * a_r, a_r * b_l + b_r


def rglru_mixer(h, w_in, conv_w, conv_b, gate_w, gate_b, lam, w_out):
    B, S, _ = h.shape
    f32 = jnp.float32
    gu = h @ w_in
    gate, u = gu[..., :D_RNN], gu[..., D_RNN:]
    u = dwconv(u, conv_w, conv_b, RG_CONV_LEFT)
    uf = u.astype(f32)
    ub = uf.reshape(B, S, RG_BLOCKS, RG_BW)

    def scan_dir(d, reverse):
        ri = jnp.einsum('bsnc,nce->bsne', ub, gate_w[d].astype(f32)) + gate_b[d].astype(f32)
        r = jax.nn.sigmoid(ri[..., :RG_BW]).reshape(B, S, D_RNN)
        i_g = jax.nn.sigmoid(ri[..., RG_BW:]).reshape(B, S, D_RNN)
        log_a = -RG_C * r * jax.nn.softplus(-lam[d].astype(f32))
        a = jnp.exp(log_a)
        bx = jnp.sqrt(-jnp.expm1(2.0 * log_a)) * (i_g * uf)
        _, hs = lax.associative_scan(_lru_combine, (a, bx), axis=1, reverse=reverse)
        return hs

    y = (scan_dir(0, False) + scan_dir(1, True)) * jax.nn.gelu(gate.astype(f32))
    return y.astype(h.dtype) @ w_out


def conv_ffn(h, w_up, conv_w, conv_b, w_down):
    gu = h @ w_up
    g, u = gu[..., :D_FF], gu[..., D_FF:]
    g = dwconv(g, conv_w, conv_b, FFN_CONV_LEFT)
    return (jax.nn.silu(g) * u) @ w_down


def encoder_trunk(x, p):
    for i in range(DEPTH):
        kind, j = i % N_MIXERS, i // N_MIXERS
        h = rmsnorm(x, p['norm_mix'][i])
        if kind == 0:
            m = diff_attention(h, p['da_w_qkv'][j], p['da_q_norm'][j], p['da_k_norm'][j],
                               p['da_lambda_q1'][j], p['da_lambda_k1'][j],
                               p['da_lambda_q2'][j], p['da_lambda_k2'][j],
                               p['da_sub_norm'][j], p['da_w_o'][j], diff_lambda_init(i))
        elif kind == 1:
            m = window_attention(h, p['wa_w_qkv'][j], p['wa_q_norm'][j], p['wa_k_norm'][j],
                                 p['wa_sink'][j], p['wa_w_o'][j])
        else:
            m = rglru_mixer(h, p['rg_w_in'][j], p['rg_conv_w'][j], p['rg_conv_b'][j],
                            p['rg_gate_w'][j], p['rg_gate_b'][j], p['rg_lambda'][j], p['rg_w_out'][j])
        x = x + m
        x = x + conv_ffn(rmsnorm(x, p['norm_ffn'][i]), p['ffn_w_up'][i], p['ffn_conv_w'][i],
                         p['ffn_conv_b'][i], p['ffn_w_down'][i])
    return x


def setup_inputs(seed: int = 0) -> dict:
    key = jax.random.key(seed)
    k = jax.random.split(key, 32)
    f32 = jnp.float32

    def nrm(kk, shape, scale):
        return jax.random.normal(kk, shape, f32) * scale

    def gain(kk, shape):
        return 1.0 + nrm(kk, shape, 0.05)

    D, F = D_MODEL, D_FF
    da_qkv = 2 * DA_HEADS * 2 * DA_DK + DA_HEADS * DA_DV
    wa_qkv = (WA_HEADS + 2 * WA_KV_HEADS) * WA_DH
    u = jax.random.uniform(k[27], (N_C, 2, D_RNN), f32, minval=0.9, maxval=0.999)
    s = u ** (1.0 / RG_C)
    rg_lambda = jnp.log(s) - jnp.log1p(-s)
    return {
        'x_prompt': nrm(k[0], (BATCH, SEQ, D), 1.0),
        'x_sample': nrm(k[1], (DEC_BATCH, DEC_SEQ, D), 1.0),
        'norm_mix': gain(k[2], (DEPTH, D)),
        'norm_ffn': gain(k[3], (DEPTH, D)),
        'ffn_w_up': nrm(k[4], (DEPTH, D, 2 * F), D ** -0.5),
        'ffn_conv_w': nrm(k[5], (DEPTH, FFN_CONV_W, F), FFN_CONV_W ** -0.5),
        'ffn_conv_b': nrm(k[6], (DEPTH, F), 0.02),
        'ffn_w_down': nrm(k[7], (DEPTH, F, D), F ** -0.5),
        'da_w_qkv': nrm(k[8], (N_A, D, da_qkv), D ** -0.5),
        'da_q_norm': gain(k[9], (N_A, DA_DK)),
        'da_k_norm': gain(k[10], (N_A, DA_DK)),
        'da_lambda_q1': nrm(k[11], (N_A, DA_DK), 0.1),
        'da_lambda_k1': nrm(k[12], (N_A, DA_DK), 0.1),
        'da_lambda_q2': nrm(k[13], (N_A, DA_DK), 0.1),
        'da_lambda_k2': nrm(k[14], (N_A, DA_DK), 0.1),
        'da_sub_norm': gain(k[15], (N_A, DA_DV)),
        'da_w_o': nrm(k[16], (N_A, DA_HEADS * DA_DV, D), (DA_HEADS * DA_DV) ** -0.5),
        'wa_w_qkv': nrm(k[17], (N_B, D, wa_qkv), D ** -0.5),
        'wa_q_norm': gain(k[18], (N_B, WA_DH)),
        'wa_k_norm': gain(k[19], (N_B, WA_DH)),
        'wa_sink': nrm(k[20], (N_B, WA_HEADS), 0.5),
        'wa_w_o': nrm(k[21], (N_B, WA_HEADS * WA_DH, D), (WA_HEADS * WA_DH) ** -0.5),
        'rg_w_in': nrm(k[22], (N_C, D, 2 * D_RNN), D ** -0.5),
        'rg_conv_w': nrm(k[23], (N_C, RG_CONV_W, D_RNN), RG_CONV_W ** -0.5),
        'rg_conv_b': nrm(k[24], (N_C, D_RNN), 0.02),
        'rg_gate_w': nrm(k[25], (N_C, 2, RG_BLOCKS, RG_BW, 2 * RG_BW), RG_BW ** -0.5),
        'rg_gate_b': nrm(k[26], (N_C, 2, RG_BLOCKS, 2 * RG_BW), 0.02),
        'rg_lambda': rg_lambda,
        'rg_w_out': nrm(k[28], (N_C, D_RNN, D), D_RNN ** -0.5),
    }


def reference(x_prompt, x_sample, norm_mix, norm_ffn, ffn_w_up, ffn_conv_w, ffn_conv_b, ffn_w_down,
              da_w_qkv, da_q_norm, da_k_norm, da_lambda_q1, da_lambda_k1, da_lambda_q2, da_lambda_k2,
              da_sub_norm, da_w_o, wa_w_qkv, wa_q_norm, wa_k_norm, wa_sink, wa_w_o,
              rg_w_in, rg_conv_w, rg_conv_b, rg_gate_w, rg_gate_b, rg_lambda, rg_w_out):
    p = dict(norm_mix=norm_mix, norm_ffn=norm_ffn, ffn_w_up=ffn_w_up, ffn_conv_w=ffn_conv_w,
             ffn_conv_b=ffn_conv_b, ffn_w_down=ffn_w_down,
             da_w_qkv=da_w_qkv, da_q_norm=da_q_norm, da_k_norm=da_k_norm,
             da_lambda_q1=da_lambda_q1, da_lambda_k1=da_lambda_k1,
             da_lambda_q2=da_lambda_q2, da_lambda_k2=da_lambda_k2,
             da_sub_norm=da_sub_norm, da_w_o=da_w_o,
             wa_w_qkv=wa_w_qkv, wa_q_norm=wa_q_norm, wa_k_norm=wa_k_norm, wa_sink=wa_sink, wa_w_o=wa_w_o,
             rg_w_in=rg_w_in, rg_conv_w=rg_conv_w, rg_conv_b=rg_conv_b, rg_gate_w=rg_gate_w,
             rg_gate_b=rg_gate_b, rg_lambda=rg_lambda, rg_w_out=rg_w_out)
    y_prompt = encoder_trunk(x_prompt, p)
    y_sample = encoder_trunk(x_sample, p)
    return (y_prompt, y_sample)
```

```python
import math
import numpy as np
import concourse.bass as bass
import concourse.mybir as mybir
from concourse.bass_utils import run_bass_kernel_spmd

F32 = mybir.dt.float32
BF16 = mybir.dt.bfloat16
AF = mybir.ActivationFunctionType
ALU = mybir.AluOpType

D = 1024
DFF = 2816
DRNN = 1536
EPS = 1e-6
NT = 512
ROPE_THETA = 500000.0


class Sem:
    def __init__(self, h):
        self.h = h
        self.cnt = 0


class Buf:
    __slots__ = ("w", "r")

    def __init__(self):
        self.w = {}
        self.r = {}


class Tile:
    def __init__(self, kb, h, col0=None):
        self.kb = kb
        self.h = h
        self.buf = Buf()
        self._ld = None
        self._st = None
        self.persistent = False
        self.col0 = col0
        self.shared = None

    def __getitem__(self, k):
        if self.col0 is None:
            return self.h[k]
        if not isinstance(k, tuple):
            k = (k, slice(None))
        p, c = k
        a = 0 if c.start is None else c.start
        b = 512 if c.stop is None else c.stop
        return self.h[p, self.col0 + a:self.col0 + b]

    @property
    def ld(self):
        if self.shared is not None:
            if self.shared[1] is None:
                self.shared[1] = self.kb.dsem(False)
            return self.shared[1]
        if self._ld is None:
            self._ld = self.kb.dsem(self.persistent)
        return self._ld

    @property
    def st(self):
        if self.shared is not None:
            if self.shared[2] is None:
                self.shared[2] = self.kb.dsem(False)
            return self.shared[2]
        if self._st is None:
            self._st = self.kb.dsem(self.persistent)
        return self._st


class Eng:
    def __init__(self, kb, e, name, is_pe=False):
        self.e = e
        self.sem = kb.newsem("e_" + name)
        self.waited = {}
        self.is_pe = is_pe


def _bufs(xs):
    out = []
    for x in xs:
        if x is None:
            continue
        if isinstance(x, (list, tuple)):
            out.extend(_bufs(x))
        elif isinstance(x, Tile):
            out.append(x.buf)
        else:
            out.append(x)
    return out


class KB:
    def __init__(self, seqs, nlayers=4, debug=False):
        self.seqs = list(seqs)
        self.offs = [sum(self.seqs[:i]) for i in range(len(self.seqs))]
        self.T = sum(self.seqs)
        self.nlayers = nlayers
        self.nc = bass.Bass("TRN2", target_bir_lowering=False)
        self._ctx = []
        self._semn = 0
        self.sems = []
        nc = self.nc
        self.pe = Eng(self, nc.tensor, "pe", True)
        self.act = Eng(self, nc.scalar, "act")
        self.dve = Eng(self, nc.vector, "dve")
        self.pool = Eng(self, nc.gpsimd, "pool")
        self.sp = Eng(self, nc.sync, "sp")
        self.engs = [self.pe, self.act, self.dve, self.pool, self.sp]
        self.out_bufs = []
        self.phase_ctx = []
        self.free_dsems = []
        self.phase_dsems = []
        self.scope_tag = None
        self.scope_ctx = []
        self.scope_state = {}
        self.phase_id = 0

    def newsem(self, name):
        self._semn += 1
        g = self.nc.semaphore("%s%d" % (name, self._semn))
        h = g.__enter__()
        self._ctx.append(g)
        s = Sem(h)
        self.sems.append(s)
        return s

    def sb(self, shape, dt, name=None, persistent=False):
        self._semn += 1
        g = self.nc.sbuf_tensor("%s_%d" % (name or "t", self._semn), list(shape), dt)
        h = g.__enter__()
        t = Tile(self, h)
        t.persistent = persistent
        if persistent:
            self._ctx.append(g)
        elif self.scope_tag is not None:
            self.scope_ctx.append(g)
            key = (self.phase_id, self.scope_tag, self.scope_i)
            self.scope_i += 1
            st = self.scope_state.get(key)
            if st is None:
                st = [Buf(), None, None]
                self.scope_state[key] = st
            t.buf = st[0]
            t.shared = st
        else:
            self.phase_ctx.append(g)
        return t

    def begin_scope(self, tag):
        self.scope_tag = tag
        self.scope_i = 0
        self.scope_ctx = []

    def end_scope(self):
        for g in reversed(self.scope_ctx):
            g.__exit__(None, None, None)
        self.scope_ctx = []
        self.scope_tag = None

    def dsem(self, persistent):
        if self.free_dsems:
            s = self.free_dsems.pop()
        else:
            s = self.newsem("d")
        if not persistent:
            self.phase_dsems.append(s)
        return s

    def dram(self, name, shape, dt, kind="Internal"):
        return self.nc.dram_tensor(name, list(shape), dt, kind=kind).ap()

    def end_phase(self):
        self.barrier()
        for g in reversed(self.phase_ctx):
            g.__exit__(None, None, None)
        self.phase_ctx = []
        self.free_dsems.extend(self.phase_dsems)
        self.phase_dsems = []
        self.phase_id += 1

    def barrier(self):
        for E in self.engs:
            for s in self.sems:
                if s.cnt > 0 and not (s is E.sem):
                    if E.waited.get(s, 0) >= s.cnt:
                        continue
                    E.e.wait_ge(s.h, s.cnt)
                    E.waited[s] = s.cnt

    def _wait(self, E, s, v):
        if E.waited.get(s, 0) >= v:
            return
        E.e.wait_ge(s.h, v)
        E.waited[s] = v

    def _deps(self, E, reads, writes, no_self=False):
        deps = {}
        me = E.sem
        for b in reads:
            for s, v in b.w.items():
                if deps.get(s, 0) < v:
                    deps[s] = v
        for b in writes:
            for d in (b.w, b.r):
                for s, v in d.items():
                    if s is me:
                        continue
                    if deps.get(s, 0) < v:
                        deps[s] = v
        need = []
        for s, v in deps.items():
            if s is me and (E.is_pe or no_self):
                continue
            if E.waited.get(s, 0) >= v:
                continue
            E.waited[s] = v
            need.append((s, v))
        return need

    def _emit_waits(self, E, need, ins):
        for s, v in need[:-1]:
            E.e.wait_ge(s.h, v)
        if need:
            s, v = need[-1]
            ins._wait_ge(s.h, v)

    def op(self, E, fn, reads=(), writes=(), mark=True, no_self=False):
        reads = _bufs(reads)
        writes = _bufs(writes)
        need = self._deps(E, reads, writes, no_self)
        for s_, v_ in need[:-1]:
            E.e.wait_ge(s_.h, v_)
        ins = fn()
        if need:
            ins._wait_ge(need[-1][0].h, need[-1][1])
        s = E.sem
        if mark:
            s.cnt += 1
            ins.then_inc(s.h, 1)
            val = s.cnt
        else:
            val = s.cnt + 1
        for b in reads:
            if b.r.get(s, 0) < val:
                b.r[s] = val
        for b in writes:
            if b.w.get(s, 0) < val:
                b.w[s] = val
        return ins

    def dma(self, Q, out, in_, sem, reads=(), writes=(), **kw):
        reads = _bufs(reads)
        writes = _bufs(writes)
        need = self._deps(Q, reads, writes)
        for s_, v_ in need[:-1]:
            Q.e.wait_ge(s_.h, v_)
        ins = Q.e.dma_start(out=out, in_=in_, **kw)
        if need:
            ins._wait_ge(need[-1][0].h, need[-1][1])
        sem.cnt += 16
        ins.then_inc(sem.h, 16)
        for b in reads:
            b.r[sem] = sem.cnt
        for b in writes:
            b.w[sem] = sem.cnt
        return ins

    def load(self, tile, out_ap, in_ap, dsrc=(), **kw):
        return self.dma(self.sp, out_ap, in_ap, tile.ld, reads=dsrc, writes=[tile], **kw)

    def store(self, tile, out_ap, in_ap, ddst=(), **kw):
        return self.dma(self.pool, out_ap, in_ap, tile.st, reads=[tile], writes=ddst, **kw)

    def mm(self, ps, out_ap, lhsT, rhs, start, stop, reads, mark=None, **kw):
        if mark is None:
            mark = stop
        return self.op(self.pe, lambda: self.nc.tensor.matmul(out_ap, lhsT=lhsT, rhs=rhs, start=start, stop=stop, **kw),
                       reads=reads, writes=[ps], mark=mark)

    def finish(self):
        self.barrier()
        for g in reversed(self.phase_ctx):
            g.__exit__(None, None, None)
        for g in reversed(self._ctx):
            g.__exit__(None, None, None)


def splits(W):
    if W <= 512:
        return [(0, W)]
    h = (W + 1) // 2
    return [(0, h), (h, W)]


def build(seqs, nlayers=4):
    kb = KB(seqs, nlayers)
    nc = kb.nc
    T = kb.T
    NB = T // 128
    Lmax = max(seqs)
    pe, act, dve, pool, sp = kb.pe, kb.act, kb.dve, kb.pool, kb.sp
    V, A, P_ = nc.vector, nc.scalar, nc.gpsimd

    def ein(name, shape):
        return kb.dram(name, shape, F32, kind="ExternalInput")

    x_in = ein("x", [T, D])
    y_out = kb.dram("y", [T, D], F32, kind="ExternalOutput")
    w_in = {
        "norm_mix": ein("norm_mix", [4, D]), "norm_ffn": ein("norm_ffn", [4, D]),
        "ffn_w_up": ein("ffn_w_up", [4, D, 2 * DFF]), "ffn_conv_w": ein("ffn_conv_w", [4, 3, DFF]),
        "ffn_conv_b": ein("ffn_conv_b", [4, DFF]), "ffn_w_down": ein("ffn_w_down", [4, DFF, D]),
        "da_w_qkv": ein("da_w_qkv", [2, D, 3072]), "da_q_norm": ein("da_q_norm", [2, 64]),
        "da_k_norm": ein("da_k_norm", [2, 64]), "da_lambda_q1": ein("da_lambda_q1", [2, 64]),
        "da_lambda_k1": ein("da_lambda_k1", [2, 64]), "da_lambda_q2": ein("da_lambda_q2", [2, 64]),
        "da_lambda_k2": ein("da_lambda_k2", [2, 64]), "da_sub_norm": ein("da_sub_norm", [2, 128]),
        "da_w_o": ein("da_w_o", [2, D, D]), "wa_w_qkv": ein("wa_w_qkv", [1, D, 1536]),
        "wa_q_norm": ein("wa_q_norm", [1, 64]), "wa_k_norm": ein("wa_k_norm", [1, 64]),
        "wa_sink": ein("wa_sink", [1, 16]), "wa_w_o": ein("wa_w_o", [1, D, D]),
        "rg_w_in": ein("rg_w_in", [1, D, 2 * DRNN]), "rg_conv_w": ein("rg_conv_w", [1, 4, DRNN]),
        "rg_conv_b": ein("rg_conv_b", [1, DRNN]), "rg_gate_w": ein("rg_gate_w", [1, 2, 16, 96, 192]),
        "rg_gate_b": ein("rg_gate_b", [1, 2, 16, 192]), "rg_lambda": ein("rg_lambda", [1, 2, DRNN]),
        "rg_w_out": ein("rg_w_out", [1, DRNN, D]),
    }
    c_ident = ein("c_ident", [128, 128])
    c_rmat = ein("c_rmat", [128, 128])
    c_bones = ein("c_bones", [128, 128])
    c_mask = ein("c_mask", [128, 2, 512])
    c_cos = ein("c_cos", [128, Lmax])
    c_sin = ein("c_sin", [128, Lmax])

    XTS = [kb.dram("XT%d" % i, [8, 128, T + 4], F32) for i in range(2)]
    AOT = kb.dram("AOT", [16, 128, T + 4], BF16)
    QT = kb.dram("QT", [8, 128, T], BF16)
    KT = kb.dram("KT", [8, 128, T], BF16)
    VV = kb.dram("VV", [8 * 128 * NB * 128], BF16)
    HF = kb.dram("HF", [16, 96, T], F32)
    PADC = 2

    ntiles = T // NT
    xt_bs = [[Buf() for _ in range(ntiles)] for _ in range(2)]
    ao_b = [Buf() for _ in range(ntiles)]
    q_b = [Buf() for _ in range(ntiles)]
    k_b = [Buf() for _ in range(ntiles)]
    v_b = [Buf() for _ in range(ntiles)]
    hf_b = [Buf() for _ in range(ntiles)]
    y_b = [Buf() for _ in range(ntiles)]
    wdram_b = Buf()

    def seq_tiles(si):
        t0 = kb.offs[si] // NT
        return list(range(t0, t0 + kb.seqs[si] // NT))

    psg = []
    ps = []
    pspair = [None] * 4
    ps_bufs = [Buf() for _ in range(8)]
    ps_n = [0]
    for i in range(8):
        ps.append(None)

    def ps_alloc():
        for g in reversed(psg):
            g.__exit__(None, None, None)
        del psg[:]
        ps_n[0] += 1
        for i in range(4):
            g = nc.psum_tensor("pspair%d_%d" % (i, ps_n[0]), [128, 1024], F32)
            pspair[i] = g.__enter__()
            psg.append(g)
        for i in range(8):
            t = Tile(kb, pspair[i // 2], col0=(i % 2) * 512)
            t.buf = ps_bufs[i]
            ps[i] = t

    ps_alloc()

    def const_load(src, shape, dt=F32):
        t = kb.sb(shape, F32, "c", persistent=True)
        kb.load(t, t[:], src)
        if dt == F32:
            return t
        tb = kb.sb(shape, BF16, "cb", persistent=True)
        kb.op(dve, lambda: V.tensor_copy(out=tb[:], in_=t[:]), reads=[t], writes=[tb])
        return tb

    ident_f = const_load(c_ident, [128, 128])
    ident_b = const_load(c_ident, [128, 128], BF16)
    rmat_f = const_load(c_rmat, [128, 128])
    bones_b = const_load(c_bones, [128, 128], BF16)
    mask_b = const_load(c_mask.rearrange("p a b -> p (a b)"), [128, 1024], BF16)
    ones_b = kb.sb([128, 128], BF16, "ones", persistent=True)
    kb.op(dve, lambda: V.memset(ones_b[:], 1.0), writes=[ones_b])
    ones_f = kb.sb([128, 128], F32, "onesf", persistent=True)
    kb.op(dve, lambda: V.memset(ones_f[:], 1.0), writes=[ones_f])

    def small_load(src_ap, shape):
        t = kb.sb(shape, F32, "sp", persistent=True)
        kb.load(t, t[:], src_ap, allow_slow_non_contiguous=True)
        return t

    def load_T(src2d, R, C, csz=128, persistent=True, o=None):
        if o is None:
            o = kb.sb([csz, C, R], F32, "lt")
        stg_ = kb.sb([R, C * csz], F32, "ltstg")
        kb.load(stg_, stg_[:], src2d)
        for c in range(C):
            pt = ps[c % 8]
            kb.op(pe, lambda: nc.tensor.transpose(out=pt[0:csz, 0:R], in_=stg_[0:R, c * csz:(c + 1) * csz],
                                                  identity=ident_f[0:R, 0:R]), reads=[stg_, ident_f], writes=[pt])
            kb.op(dve, lambda: V.tensor_copy(out=o[:, c, :], in_=pt[0:csz, 0:R]), reads=[pt], writes=[o])
        return o

    fcw = kb.sb([128, 22, 12], F32, "fcw", persistent=True)
    fcb = kb.sb([128, 22, 4], F32, "fcb", persistent=True)
    load_T(w_in["ffn_conv_w"].rearrange("l k f -> (l k) f"), 12, 22, o=fcw)
    load_T(w_in["ffn_conv_b"], 4, 22, o=fcb)
    gmix = load_T(w_in["norm_mix"], 4, 8)
    gffn = load_T(w_in["norm_ffn"], 4, 8)

    wbf = {}
    pcnt = [0]

    def prep_weight(name, src, K, N, ksz, msz, gain=None):
        KC, MC = K // ksz, N // msz
        dst = kb.dram("wb_" + name, [MC, ksz, KC, msz], BF16)
        wbf[name] = (dst, KC, MC, ksz, msz)
        CW = 2816 if N > 3072 else N
        for kc in range(KC):
            for c0 in range(0, N, CW):
                cw = min(CW, N - c0)
                stg = stage[pcnt[0] % 2]
                stb = stageb[pcnt[0] % 2]
                pcnt[0] += 1
                kb.load(stg, stg[0:ksz, 0:cw], src[kc * ksz:(kc + 1) * ksz, c0:c0 + cw])
                if gain is not None:
                    gap = gain(kc)
                    kb.op(dve, lambda: V.tensor_scalar(out=stb[0:ksz, 0:cw], in0=stg[0:ksz, 0:cw], scalar1=gap,
                                                       scalar2=None, op0=ALU.mult),
                          reads=[stg], writes=[stb])
                else:
                    kb.op(act, lambda: A.copy(out=stb[0:ksz, 0:cw], in_=stg[0:ksz, 0:cw]), reads=[stg], writes=[stb])
                m0, m1 = c0 // msz, (c0 + cw) // msz
                kb.store(stb, dst[m0:m1, :, kc, :].rearrange("m p i -> p m i"),
                         stb[0:ksz, 0:cw].rearrange("p (m i) -> p m i", i=msz), ddst=[wdram_b])

    stage = [kb.sb([128, 3072], F32, "stg") for _ in range(2)]
    stageb = [kb.sb([128, 3072], BF16, "stgb") for _ in range(2)]
    layer_kind = [i % 3 for i in range(4)]
    for li in range(nlayers):
        kind, j = li % 3, li // 3
        g_of = (lambda li: (lambda kc: gmix[:, kc, li:li + 1]))(li)
        if kind == 0:
            prep_weight("mix_in%d" % li, w_in["da_w_qkv"][j], D, 3072, 128, 128, g_of)
            prep_weight("mix_out%d" % li, w_in["da_w_o"][j], D, D, 128, 128)
        elif kind == 1:
            prep_weight("mix_in%d" % li, w_in["wa_w_qkv"][j], D, 1536, 128, 128, g_of)
            prep_weight("mix_out%d" % li, w_in["wa_w_o"][j], D, D, 128, 128)
        else:
            prep_weight("mix_in%d" % li, w_in["rg_w_in"][j], D, 3072, 128, 96, g_of)
            prep_weight("mix_out%d" % li, w_in["rg_w_out"][j], DRNN, D, 96, 128)
        gf_of = (lambda li: (lambda kc: gffn[:, kc, li:li + 1]))(li)
        prep_weight("up%d" % li, w_in["ffn_w_up"][li], D, 2 * DFF, 128, 128, gf_of)
        prep_weight("down%d" % li, w_in["ffn_w_down"][li], DFF, D, 128, 128)
    kb.end_phase()

    WSLOT = 4096
    NWS = 4

    class WStream:
        def __init__(self):
            self.slots = [kb.sb([128, WSLOT], BF16, "wslot") for _ in range(NWS)]
            self.i = 0

        def get(self, name, m0, nm):
            dst, KC, MC, ksz, msz = wbf[name]
            assert nm * KC * msz <= WSLOT
            s = self.slots[self.i % NWS]
            self.i += 1
            view = s[0:ksz, 0:nm * KC * msz].rearrange("p (m k i) -> p m k i", m=nm, k=KC)
            kb.load(s, view, dst[m0:m0 + nm].rearrange("m p k i -> p m k i"), dsrc=[wdram_b])
            return s, view

    def rmsnorm_fm(xt, ht, W, sq, rinv, psb, KCH=8):
        sp_ = splits(W)
        for c in range(KCH):
            s = sq[c % 2]
            kb.op(act, lambda: A.activation(out=s[:, 0:W], in_=xt[:, c, 0:W], func=AF.Square, scale=1.0 / 32.0),
                  reads=[xt], writes=[s])
            for bi, (a, b) in enumerate(sp_):
                kb.mm(psb[bi], psb[bi][:, 0:b - a], ones_b[:], s[:, a:b], c == 0, c == KCH - 1, reads=[ones_b, s],
                      mark=True)
        for bi, (a, b) in enumerate(sp_):
            kb.op(dve, lambda: V.tensor_scalar(out=rinv[:, a:b], in0=psb[bi][:, 0:b - a], scalar1=EPS, scalar2=None,
                                               op0=ALU.add), reads=[psb[bi]], writes=[rinv])
        kb.op(act, lambda: A.activation(out=rinv[:, 0:W], in_=rinv[:, 0:W], func=AF.Sqrt), reads=[rinv], writes=[rinv])
        kb.op(dve, lambda: V.reciprocal(out=rinv[:, 0:W], in_=rinv[:, 0:W]), reads=[rinv], writes=[rinv])
        for c in range(KCH):
            E, EE = (dve, V) if c % 2 == 0 else (pool, P_)
            kb.op(E, lambda: EE.tensor_tensor(out=ht[:, c, 0:W], in0=xt[:, c, 0:W], in1=rinv[:, 0:W], op=ALU.mult),
                  reads=[xt, rinv], writes=[ht])

    def phase_transpose_in():
        XT, xt_b = XTS[0], xt_bs[0]
        xin = [kb.sb([128, D], F32, "xin") for _ in range(3)]
        stg_ = [kb.sb([128, 8, NT], F32, "tstg") for _ in range(2)]
        for ti in range(ntiles):
            st_ = stg_[ti % 2]
            for sbk in range(4):
                blk = ti * 4 + sbk
                xi = xin[blk % 3]
                kb.load(xi, xi[:], x_in[blk * 128:(blk + 1) * 128, :])
                for half in range(2):
                    pt = ps[(blk * 2 + half) % 8]
                    for c4 in range(4):
                        c = half * 4 + c4
                        kb.op(pe, lambda: nc.tensor.transpose(out=pt[:, c4 * 128:(c4 + 1) * 128],
                                                              in_=xi[:, c * 128:(c + 1) * 128], identity=ident_f[:]),
                              reads=[xi, ident_f], writes=[pt], mark=(c4 == 3))
                    dstv = st_[:, half * 4:(half + 1) * 4, sbk * 128:(sbk + 1) * 128]
                    srcv = pt[:, :].rearrange("p (c t) -> p c t", c=4)
                    if half == 0:
                        kb.op(act, lambda: A.copy(out=dstv, in_=srcv), reads=[pt], writes=[st_])
                    else:
                        kb.op(dve, lambda: V.tensor_copy(out=dstv, in_=srcv), reads=[pt], writes=[st_])
            kb.store(st_, XT[:, :, PADC + ti * NT:PADC + (ti + 1) * NT].rearrange("c p t -> p c t"), st_[:],
                     ddst=[xt_b[ti]])
        kb.end_phase()

    def phase_transpose_out():
        XT, xt_b = XTS[nlayers % 2], xt_bs[nlayers % 2]
        xl = [kb.sb([128, 8, NT], F32, "xl") for _ in range(2)]
        yo = [kb.sb([128, D], F32, "yo") for _ in range(3)]
        for ti in range(ntiles):
            xt = xl[ti % 2]
            kb.load(xt, xt[:], XT[:, :, PADC + ti * NT:PADC + (ti + 1) * NT].rearrange("c p t -> p c t"),
                    dsrc=[xt_b[ti]])
            for sbk in range(4):
                blk = ti * 4 + sbk
                yt = yo[blk % 3]
                for half in range(2):
                    pt = ps[(blk * 2 + half) % 8]
                    for c4 in range(4):
                        c = half * 4 + c4
                        kb.op(pe, lambda: nc.tensor.transpose(out=pt[:, c4 * 128:(c4 + 1) * 128],
                                                              in_=xt[:, c, sbk * 128:(sbk + 1) * 128],
                                                              identity=ident_f[:]),
                              reads=[xt, ident_f], writes=[pt], mark=(c4 == 3))
                    if half == 0:
                        kb.op(act, lambda: A.copy(out=yt[:, 0:512], in_=pt[:, :]), reads=[pt], writes=[yt])
                    else:
                        kb.op(dve, lambda: V.tensor_copy(out=yt[:, 512:1024], in_=pt[:, :]), reads=[pt], writes=[yt])
                kb.store(yt, y_out[blk * 128:(blk + 1) * 128, :], yt[:], ddst=[y_b[ti]])
        kb.end_phase()

    def phase_ffn(li, aksz, akc):
        XT, xt_b = XTS[li % 2], xt_bs[li % 2]
        XTO, xto_b = XTS[(li + 1) % 2], xt_bs[(li + 1) % 2]
        W = NT + 2
        sp_ = splits(W)
        pi = [0]

        def nextps():
            p = ps[pi[0] % 8]
            pi[0] += 1
            return p

        for si in range(len(kb.seqs)):
            tl = seq_tiles(si)
            for ti in tl:
                kb.begin_scope("ffn")
                ps_alloc()
                ws = WStream()
                xt_s = [kb.sb([128, 8, W], F32, "xt") for _ in range(2)]
                ao_s = [kb.sb([128, akc, W], BF16, "ao") for _ in range(2)]
                ht = kb.sb([128, 8, W], BF16, "ht")
                sq = [kb.sb([128, W], BF16, "sq") for _ in range(2)]
                rinv = kb.sb([128, W], F32, "rinv")
                actT = kb.sb([128, 22, NT], BF16, "actT")
                gev = [kb.sb([128, W], F32, "gev") for _ in range(2)]
                cv1 = [kb.sb([128, NT], F32, "cv1") for _ in range(2)]
                cv2 = [kb.sb([128, NT], F32, "cv2") for _ in range(2)]
                xt = xt_s[ti % 2]
                ao = ao_s[ti % 2]
                c0 = PADC + ti * NT - 1
                nb = [xt_b[t] for t in (ti - 1, ti, ti + 1) if 0 <= t < ntiles]
                nab = [ao_b[t] for t in (ti - 1, ti, ti + 1) if 0 <= t < ntiles]
                kb.load(xt, xt[:], XT[:, :, c0:c0 + W].rearrange("c p t -> p c t"), dsrc=nb)
                kb.load(ao, ao[0:aksz], AOT[0:akc, 0:aksz, c0:c0 + W].rearrange("c p t -> p c t"), dsrc=nab)
                for mc in range(8):
                    if mc % 2 == 0:
                        wsl, wv = ws.get("mix_out%d" % li, mc, 2)
                    pp = [nextps() for _ in sp_]
                    for kc in range(akc):
                        for bi, (a, b) in enumerate(sp_):
                            kb.mm(pp[bi], pp[bi][:, 0:b - a], wv[0:aksz, mc % 2, kc, :], ao[0:aksz, kc, a:b], kc == 0,
                                  kc == akc - 1, reads=[wsl, ao])
                    for bi, (a, b) in enumerate(sp_):
                        kb.op(dve, lambda: V.tensor_tensor(out=xt[:, mc, a:b], in0=xt[:, mc, a:b],
                                                           in1=pp[bi][:, 0:b - a], op=ALU.add),
                              reads=[pp[bi], xt], writes=[xt])
                pp = [nextps() for _ in sp_]
                rmsnorm_fm(xt, ht, W, sq, rinv, pp)
                if ti == tl[0]:
                    kb.op(pool, lambda: P_.memset(ht[:, :, 0:1], 0.0), writes=[ht])
                if ti == tl[-1]:
                    kb.op(pool, lambda: P_.memset(ht[:, :, W - 1:W], 0.0), writes=[ht])
                for fc in range(22):
                    if fc % 2 == 0:
                        wsg, wvg = ws.get("up%d" % li, fc, 2)
                        wsu, wvu = ws.get("up%d" % li, 22 + fc, 2)
                    pg = [nextps() for _ in sp_]
                    for kc in range(8):
                        for bi, (a, b) in enumerate(sp_):
                            kb.mm(pg[bi], pg[bi][:, 0:b - a], wvg[:, fc % 2, kc, :], ht[:, kc, a:b], kc == 0, kc == 7,
                                  reads=[wsg, ht])
                    pu = nextps()
                    for kc in range(8):
                        kb.mm(pu, pu[:, 0:NT], wvu[:, fc % 2, kc, :], ht[:, kc, 1:1 + NT], kc == 0, kc == 7,
                              reads=[wsu, ht])
                    ge = gev[fc % 2]
                    for bi, (a, b) in enumerate(sp_):
                        kb.op(act, lambda: A.copy(out=ge[:, a:b], in_=pg[bi][:, 0:b - a]), reads=[pg[bi]], writes=[ge])
                    t1 = cv1[fc % 2]
                    t2 = cv2[fc % 2]
                    kb.op(pool, lambda: P_.tensor_scalar(out=t1[:], in0=ge[:, 0:NT], scalar1=fcw[:, fc, li * 3:li * 3 + 1],
                                                         scalar2=fcb[:, fc, li:li + 1], op0=ALU.mult, op1=ALU.add),
                          reads=[ge, fcw, fcb], writes=[t1])
                    kb.op(dve, lambda: V.scalar_tensor_tensor(out=t2[:], in0=ge[:, 1:1 + NT],
                                                                scalar=fcw[:, fc, li * 3 + 1:li * 3 + 2], in1=t1[:],
                                                                op0=ALU.mult, op1=ALU.add),
                          reads=[ge, fcw, t1], writes=[t2])
                    kb.op(dve, lambda: V.scalar_tensor_tensor(out=t1[:], in0=ge[:, 2:2 + NT],
                                                              scalar=fcw[:, fc, li * 3 + 2:li * 3 + 3], in1=t2[:],
                                                              op0=ALU.mult, op1=ALU.add),
                          reads=[ge, fcw, t2], writes=[t1])
                    kb.op(act, lambda: A.activation(out=t2[:], in_=t1[:], func=AF.Silu), reads=[t1], writes=[t2])
                    kb.op(dve, lambda: V.tensor_tensor(out=actT[:, fc, :], in0=t2[:], in1=pu[:, 0:NT], op=ALU.mult),
                          reads=[t2, pu], writes=[actT])
                for mc in range(8):
                    wsl, wv = ws.get("down%d" % li, mc, 1)
                    pd = nextps()
                    for kc in range(22):
                        kb.mm(pd, pd[:, 0:NT], wv[:, 0, kc, :], actT[:, kc, :], kc == 0, kc == 21, reads=[wsl, actT])
                    kb.op(dve, lambda: V.tensor_tensor(out=xt[:, mc, 1:1 + NT], in0=xt[:, mc, 1:1 + NT], in1=pd[:, 0:NT],
                                                       op=ALU.add), reads=[pd, xt], writes=[xt])
                kb.store(xt, XTO[:, :, PADC + ti * NT:PADC + (ti + 1) * NT].rearrange("c p t -> p c t"),
                         xt[:, :, 1:1 + NT], ddst=[xto_b[ti]])
                kb.end_scope()
        kb.end_phase()

    def qk_epilogue(pq, gap, cos_t, sin_t, outv, tmp):
        sqb, t_s, qn, t1, t2, _ob = tmp
        psq = pq[1]
        prq = pq[2]
        p0 = pq[0]
        kb.op(act, lambda: A.activation(out=sqb[:], in_=p0[:, 0:NT], func=AF.Square, scale=0.125),
              reads=[p0], writes=[sqb])
        kb.mm(psq, psq[:, 0:NT], bones_b[:], sqb[:], True, True, reads=[bones_b, sqb])
        kb.op(dve, lambda: V.tensor_scalar(out=t_s[:], in0=psq[:, 0:NT], scalar1=EPS, scalar2=None, op0=ALU.add),
              reads=[psq], writes=[t_s])
        kb.op(act, lambda: A.activation(out=t_s[:], in_=t_s[:], func=AF.Sqrt), reads=[t_s], writes=[t_s])
        kb.op(dve, lambda: V.reciprocal(out=t_s[:], in_=t_s[:]), reads=[t_s], writes=[t_s])
        kb.op(dve, lambda: V.scalar_tensor_tensor(out=qn[:], in0=p0[:, 0:NT], scalar=gap, in1=t_s[:], op0=ALU.mult,
                                                  op1=ALU.mult), reads=[p0, t_s], writes=[qn])
        kb.mm(prq, prq[:, 0:NT], rmat_f[:], qn[:], True, True, reads=[rmat_f, qn])
        kb.op(pool, lambda: P_.tensor_tensor(out=t1[:], in0=qn[:], in1=cos_t[:], op=ALU.mult),
              reads=[qn, cos_t], writes=[t1])
        kb.op(dve, lambda: V.tensor_tensor(out=t2[:], in0=prq[:, 0:NT], in1=sin_t[:], op=ALU.mult),
              reads=[prq, sin_t], writes=[t2])
        kb.op(pool, lambda: P_.tensor_tensor(out=outv, in0=t1[:], in1=t2[:], op=ALU.add),
              reads=[t1, t2], writes=[tmp[5]])

    def phase_qkv(li, kind, j):
        XT, xt_b = XTS[li % 2], xt_bs[li % 2]
        nq = 8
        nk = 8 if kind == 0 else 2
        nvcols = 1024 if kind == 0 else 256
        qn_src = w_in["da_q_norm"] if kind == 0 else w_in["wa_q_norm"]
        kn_src = w_in["da_k_norm"] if kind == 0 else w_in["wa_k_norm"]
        gq = kb.sb([128, 1], F32, "gq")
        gk = kb.sb([128, 1], F32, "gk")
        for half in range(2):
            kb.load(gq, gq[half * 64:(half + 1) * 64, :], qn_src[j:j + 1, :].rearrange("a d -> d a"),
                    allow_slow_non_contiguous=True)
            kb.load(gk, gk[half * 64:(half + 1) * 64, :], kn_src[j:j + 1, :].rearrange("a d -> d a"),
                    allow_slow_non_contiguous=True)
        pi = [0]

        def nextps():
            p = ps[pi[0] % 8]
            pi[0] += 1
            return p

        wname = "mix_in%d" % li
        for si in range(len(kb.seqs)):
            for n_, ti in enumerate(seq_tiles(si)):
                kb.begin_scope("qkv")
                ps_alloc()
                ws = WStream()
                xt_s = [kb.sb([128, 8, NT], F32, "xt") for _ in range(2)]
                ht = kb.sb([128, 8, NT], BF16, "ht")
                sq = [kb.sb([128, NT], BF16, "sq") for _ in range(2)]
                rinv = kb.sb([128, NT], F32, "rinv")
                cos_s = [kb.sb([128, NT], F32, "cos") for _ in range(2)]
                sin_s = [kb.sb([128, NT], F32, "sin") for _ in range(2)]
                sqb = [kb.sb([128, NT], BF16, "sqb") for _ in range(2)]
                t_s = [kb.sb([128, NT], F32, "ts") for _ in range(2)]
                qn = [kb.sb([128, NT], F32, "qn") for _ in range(2)]
                t1 = [kb.sb([128, NT], F32, "t1") for _ in range(2)]
                t2 = [kb.sb([128, NT], F32, "t2") for _ in range(2)]
                qo_s = [kb.sb([128, nq, NT], BF16, "qo") for _ in range(2)]
                ko_s = [kb.sb([128, nk, NT], BF16, "ko") for _ in range(2)]
                vo_s = [kb.sb([128, 4, nvcols], BF16, "vo") for _ in range(2)]
                xt = xt_s[ti % 2]
                qo, ko, vo = qo_s[ti % 2], ko_s[ti % 2], vo_s[ti % 2]
                cs, sn = cos_s[ti % 2], sin_s[ti % 2]
                kb.load(xt, xt[:], XT[:, :, PADC + ti * NT:PADC + (ti + 1) * NT].rearrange("c p t -> p c t"),
                        dsrc=[xt_b[ti]])
                kb.load(cs, cs[:], c_cos[:, n_ * NT:(n_ + 1) * NT])
                kb.load(sn, sn[:], c_sin[:, n_ * NT:(n_ + 1) * NT])
                rmsnorm_fm(xt, ht, NT, sq, rinv, [nextps()])
                for mc in range(nq + nk):
                    if mc % 2 == 0:
                        wsl, wv = ws.get(wname, mc, 2)
                    pq = [nextps(), nextps(), nextps()]
                    for kc in range(8):
                        kb.mm(pq[0], pq[0][:, 0:NT], wv[:, mc % 2, kc, :], ht[:, kc, :], kc == 0, kc == 7,
                              reads=[wsl, ht])
                    isq = mc < nq
                    outv = qo[:, mc, :] if isq else ko[:, mc - nq, :]
                    tmp = (sqb[mc % 2], t_s[mc % 2], qn[mc % 2], t1[mc % 2], t2[mc % 2], qo if isq else ko)
                    qk_epilogue(pq, (gq if isq else gk)[:, 0:1], cs, sn, outv, tmp)
                vm0 = nq + nk
                nvg = nvcols // 256
                for vg in range(nvg):
                    wsl, wv = ws.get(wname, vm0 + vg * 2, 2)
                    for sbk in range(4):
                        pv = nextps()
                        for kc in range(8):
                            kb.mm(pv, pv[:, 0:256], ht[:, kc, sbk * 128:(sbk + 1) * 128], wv[:, :, kc, :], kc == 0,
                                  kc == 7, reads=[wsl, ht])
                        if sbk % 2 == 0:
                            kb.op(act, lambda: A.copy(out=vo[:, sbk, vg * 256:(vg + 1) * 256], in_=pv[:, 0:256]),
                                  reads=[pv], writes=[vo])
                        else:
                            kb.op(dve, lambda: V.tensor_copy(out=vo[:, sbk, vg * 256:(vg + 1) * 256], in_=pv[:, 0:256]),
                                  reads=[pv], writes=[vo])
                kb.store(qo, QT[0:nq, :, ti * NT:(ti + 1) * NT].rearrange("c p t -> p c t"), qo[:], ddst=[q_b[ti]])
                kb.store(ko, KT[0:nk, :, ti * NT:(ti + 1) * NT].rearrange("c p t -> p c t"), ko[:], ddst=[k_b[ti]])
                nh = nvcols // (128 if kind == 0 else 64)
                dv = nvcols // nh
                vview = VV[0:nh * 128 * NB * dv].rearrange("(h p b d) -> p h b d", h=nh, p=128, b=NB)
                for sbk in range(4):
                    kb.store(vo, vview[:, :, ti * 4 + sbk, :], vo[:, sbk, :].rearrange("p (h d) -> p h d", h=nh),
                             ddst=[v_b[ti]])
                kb.end_scope()
        kb.end_phase()

    def phase_da_attn(li, j):
        lam_init = 0.8 - 0.6 * math.exp(-0.3 * li)
        Lm = max(kb.seqs)
        lv = kb.sb([128, 4, 64], F32, "lv")
        for i_, nm in enumerate(["da_lambda_q1", "da_lambda_k1", "da_lambda_q2", "da_lambda_k2"]):
            kb.load(lv, lv[:, i_, :], w_in[nm][j:j + 1, :].partition_broadcast(128))
        lp = kb.sb([128, 2, 64], F32, "lp")
        ls = kb.sb([128, 4], F32, "ls")
        kb.op(dve, lambda: V.tensor_tensor(out=lp[:, 0, :], in0=lv[:, 0, :], in1=lv[:, 1, :], op=ALU.mult),
              reads=[lv], writes=[lp])
        kb.op(dve, lambda: V.tensor_tensor(out=lp[:, 1, :], in0=lv[:, 2, :], in1=lv[:, 3, :], op=ALU.mult),
              reads=[lv], writes=[lp])
        kb.op(dve, lambda: V.reduce_sum(out=ls[:, 0:2], in_=lp[:], axis=mybir.AxisListType.X), reads=[lp], writes=[ls])
        kb.op(act, lambda: A.activation(out=ls[:, 0:2], in_=ls[:, 0:2], func=AF.Exp), reads=[ls], writes=[ls])
        kb.op(dve, lambda: V.tensor_tensor(out=ls[:, 2:3], in0=ls[:, 1:2], in1=ls[:, 0:1], op=ALU.subtract),
              reads=[ls], writes=[ls])
        kb.op(dve, lambda: V.tensor_scalar(out=ls[:, 3:4], in0=ls[:, 2:3], scalar1=-lam_init, scalar2=None,
                                           op0=ALU.add), reads=[ls], writes=[ls])
        neg_lam = ls[:, 3:4]
        sg = kb.sb([128, 2], F32, "sg")
        kb.load(sg, sg[:, 0:1], w_in["da_sub_norm"][j:j + 1, :].rearrange("a d -> d a"), allow_slow_non_contiguous=True)
        kb.op(dve, lambda: V.tensor_scalar(out=sg[:, 1:2], in0=sg[:, 0:1], scalar1=(1.0 - lam_init), scalar2=None,
                                           op0=ALU.mult), reads=[sg], writes=[sg])
        it = 0
        qi = 0
        for si in range(len(kb.seqs)):
            L = kb.seqs[si]
            off = kb.offs[si]
            tl = seq_tiles(si)
            nkt = L // 128
            for h in range(8):
                vview = VV[0:8 * 128 * NB * 128].rearrange("(h p b d) -> p h b d", h=8, p=128, b=NB)
                for ti in tl:
                    kb.begin_scope("da")
                    ps_alloc()
                    kt_t = kb.sb([128, Lm], BF16, "ktt")
                    v_t = kb.sb([128, Lm // 128, 128], BF16, "vtt")
                    q_s = [kb.sb([128, NT], BF16, "qs") for _ in range(2)]
                    p_s = [kb.sb([128, 2, NT], BF16, "pts") for _ in range(3)]
                    r0 = kb.sb([128, NT], F32, "r0")
                    r1 = kb.sb([128, NT], F32, "r1")
                    o_t = kb.sb([128, NT], F32, "ot")
                    sqb = kb.sb([128, NT], BF16, "sqb")
                    rs = kb.sb([128, NT], F32, "rs")
                    on_s = [kb.sb([128, NT], BF16, "on") for _ in range(2)]
                    acc = [kb.sb([128, 2, NT], F32, "acc") for _ in range(2)]
                    S_ps = [[ps[0], ps[1]], [ps[2], ps[3]]]
                    O_ps = [ps[4], ps[5]]
                    L_ps = [ps[6], ps[7]]
                    if ti == tl[0]:
                        kb.load(kt_t, kt_t[:, 0:L], KT[h, :, off:off + L], dsrc=[k_b[t] for t in tl])
                        kb.load(v_t, v_t[:, 0:nkt, :], vview[:, h, off // 128:off // 128 + nkt, :],
                                dsrc=[v_b[t] for t in tl])
                    qt = q_s[qi % 2]
                    on = on_s[qi % 2]
                    qi += 1
                    kb.load(qt, qt[:], QT[h, :, ti * NT:(ti + 1) * NT], dsrc=[q_b[ti]])
                    for kt in range(nkt):
                        b0 = 2 * (it % 2)
                        sp2 = S_ps[it % 2]
                        pt = p_s[it % 3]
                        it += 1
                        for m in range(2):
                            kb.mm(sp2[m], sp2[m][:, 0:NT], kt_t[m * 64:(m + 1) * 64, kt * 128:(kt + 1) * 128],
                                  qt[m * 64:(m + 1) * 64, :], True, True, reads=[kt_t, qt])
                        kb.op(act, lambda: A.activation(out=pt[:].rearrange("p a b -> p (a b)"),
                                                        in_=pspair[b0 // 2][:, 0:1024], func=AF.Exp,
                                                        scale=0.125), reads=[sp2[0], sp2[1]], writes=[pt])
                        last = kt == nkt - 1
                        for m in range(2):
                            kb.mm(O_ps[m], O_ps[m][:, 0:NT], v_t[:, kt, :], pt[:, m, :], kt == 0, last,
                                  reads=[v_t, pt], mark=True)
                        ac = acc[kt % 2]
                        if kt < 2:
                            kb.op(dve, lambda: V.tensor_copy(out=ac[:], in_=pt[:]), reads=[pt], writes=[ac],
                                  no_self=True)
                        else:
                            kb.op(dve, lambda: V.tensor_tensor(out=ac[:], in0=ac[:], in1=pt[:], op=ALU.add),
                                  reads=[pt, ac], writes=[ac], no_self=True)
                    kb.op(dve, lambda: V.tensor_tensor(out=acc[0][:], in0=acc[0][:], in1=acc[1][:], op=ALU.add),
                          reads=[acc[0], acc[1]], writes=[acc[0]])
                    for m in range(2):
                        kb.mm(L_ps[m], L_ps[m][:, 0:NT], ones_f[:], acc[0][:, m, :], True, True,
                              reads=[ones_f, acc[0]], mark=True)
                    kb.op(dve, lambda: V.reciprocal(out=r0[:], in_=L_ps[0][:, 0:NT]), reads=[L_ps[0]], writes=[r0])
                    kb.op(dve, lambda: V.reciprocal(out=r1[:], in_=L_ps[1][:, 0:NT]), reads=[L_ps[1]], writes=[r1])
                    kb.op(dve, lambda: V.tensor_tensor(out=r0[:], in0=O_ps[0][:, 0:NT], in1=r0[:], op=ALU.mult),
                          reads=[O_ps[0], r0], writes=[r0])
                    kb.op(dve, lambda: V.tensor_tensor(out=r1[:], in0=O_ps[1][:, 0:NT], in1=r1[:], op=ALU.mult),
                          reads=[O_ps[1], r1], writes=[r1])
                    kb.op(dve, lambda: V.scalar_tensor_tensor(out=o_t[:], in0=r1[:], scalar=neg_lam, in1=r0[:],
                                                              op0=ALU.mult, op1=ALU.add),
                          reads=[r0, r1, ls], writes=[o_t])
                    kb.op(act, lambda: A.activation(out=sqb[:], in_=o_t[:], func=AF.Square,
                                                    scale=1.0 / math.sqrt(128.0)), reads=[o_t], writes=[sqb])
                    pss = S_ps[it % 2][0]
                    kb.mm(pss, pss[:, 0:NT], ones_b[:], sqb[:], True, True, reads=[ones_b, sqb])
                    kb.op(dve, lambda: V.tensor_scalar(out=rs[:], in0=pss[:, 0:NT], scalar1=EPS, scalar2=None,
                                                       op0=ALU.add), reads=[pss], writes=[rs])
                    kb.op(act, lambda: A.activation(out=rs[:], in_=rs[:], func=AF.Sqrt), reads=[rs], writes=[rs])
                    kb.op(dve, lambda: V.reciprocal(out=rs[:], in_=rs[:]), reads=[rs], writes=[rs])
                    kb.op(dve, lambda: V.scalar_tensor_tensor(out=on[:], in0=o_t[:], scalar=sg[:, 1:2], in1=rs[:],
                                                              op0=ALU.mult, op1=ALU.mult),
                          reads=[o_t, sg, rs], writes=[on])
                    kb.store(on, AOT[h, :, PADC + ti * NT:PADC + (ti + 1) * NT], on[:], ddst=[ao_b[ti]])
                    kb.end_scope()
        kb.end_phase()

    def phase_wa_attn(li, j):
        Lm = max(kb.seqs)
        sk = kb.sb([64, 16], F32, "sk")
        kb.load(sk, sk[:], w_in["wa_sink"][j:j + 1, :].partition_broadcast(64))
        kb.op(act, lambda: A.activation(out=sk[:], in_=sk[:], func=AF.Exp), reads=[sk], writes=[sk])
        es = kb.sb([64, 16, 128], F32, "es")
        kb.op(dve, lambda: V.memset(es[:], 0.0), writes=[es])
        for hq in range(16):
            kb.op(dve, lambda: V.tensor_scalar(out=es[:, hq, :], in0=es[:, hq, :], scalar1=sk[:, hq:hq + 1],
                                               scalar2=None, op0=ALU.add), reads=[sk, es], writes=[es])
        it = 0
        qi = 0
        vview = VV[0:4 * 128 * NB * 64].rearrange("(h p b d) -> p h b d", h=4, p=128, b=NB)
        for si in range(len(kb.seqs)):
            L = kb.seqs[si]
            off = kb.offs[si]
            tl = seq_tiles(si)
            nkb = L // 128
            for g in range(4):
                for n_, ti in enumerate(tl):
                    kb.begin_scope("wa")
                    ps_alloc()
                    kt_t = kb.sb([64, Lm], BF16, "ktt")
                    v_t = kb.sb([128, Lm // 128, 64], BF16, "vtt")
                    q_s = [kb.sb([64, 4, NT], BF16, "qs") for _ in range(2)]
                    p_s = [kb.sb([128, NT], BF16, "pts") for _ in range(4)]
                    lt = kb.sb([64, NT], F32, "lt")
                    o_s = [kb.sb([64, 4, NT], BF16, "os") for _ in range(2)]
                    if n_ == 0:
                        kb.load(kt_t, kt_t[:, 0:L], KT[g // 2, (g % 2) * 64:(g % 2) * 64 + 64, off:off + L],
                                dsrc=[k_b[t] for t in tl])
                        kb.load(v_t, v_t[:, 0:nkb, :], vview[:, g, off // 128:off // 128 + nkb, :],
                                dsrc=[v_b[t] for t in tl])
                    qt = q_s[qi % 2]
                    ot = o_s[qi % 2]
                    qi += 1
                    for c2 in range(2):
                        kb.load(qt, qt[:, c2 * 2:c2 * 2 + 2, :],
                                QT[g * 2 + c2, :, ti * NT:(ti + 1) * NT].rearrange("(a d) t -> d a t", a=2),
                                dsrc=[q_b[ti]])
                    for qb in range(4):
                        jb = n_ * 4 + qb
                        kbs = [b for b in (jb - 1, jb, jb + 1) if 0 <= b < nkb]
                        Ops = ps[4 + (it % 2)]
                        Lps = ps[6 + (it % 2)]
                        qrhs = qt[:, :, qb * 128:(qb + 1) * 128]
                        for n2, kblk in enumerate(kbs):
                            sp_ = ps[it % 4]
                            pt = p_s[it % 4]
                            it += 1
                            rel = kblk - jb
                            kb.mm(sp_, sp_[0:128, 0:NT], kt_t[:, kblk * 128:(kblk + 1) * 128], qrhs, True, rel == 0,
                                  reads=[kt_t, qt], mark=True)
                            if rel != 0:
                                mi = 0 if rel == 1 else 1
                                kb.mm(sp_, sp_[0:128, 0:NT], ident_b[:], mask_b[:, mi * 512:(mi + 1) * 512], False, True,
                                      reads=[ident_b, mask_b], mark=True)
                            kb.op(act, lambda: A.activation(out=pt[:], in_=sp_[:, 0:NT], func=AF.Exp, scale=0.125),
                                  reads=[sp_], writes=[pt])
                            first, last = n2 == 0, n2 == len(kbs) - 1
                            kb.mm(Ops, Ops[0:64, 0:NT], v_t[:, kblk, :], pt[:], first, last, reads=[v_t, pt], mark=True)
                            kb.mm(Lps, Lps[0:64, 0:NT], ones_b[:, 0:64], pt[:], first, last, reads=[ones_b, pt],
                                  mark=True)
                        kb.op(dve, lambda: V.tensor_tensor(out=lt[:], in0=Lps[0:64, 0:NT],
                                                           in1=es[:, g * 4:(g + 1) * 4, :].rearrange("p a b -> p (a b)"),
                                                           op=ALU.add), reads=[Lps, es], writes=[lt])
                        kb.op(dve, lambda: V.reciprocal(out=lt[:], in_=lt[:]), reads=[lt], writes=[lt])
                        kb.op(dve, lambda: V.tensor_tensor(out=ot[:, :, qb * 128:(qb + 1) * 128],
                                                           in0=Ops[0:64, 0:NT].rearrange("p (a b) -> p a b", a=4),
                                                           in1=lt[:].rearrange("p (a b) -> p a b", a=4), op=ALU.mult),
                              reads=[Ops, lt], writes=[ot])
                    for c2 in range(2):
                        kb.store(ot, AOT[g * 2 + c2, :, PADC + ti * NT:PADC + (ti + 1) * NT].rearrange(
                            "(a d) t -> d a t", a=2), ot[:, c2 * 2:c2 * 2 + 2, :], ddst=[ao_b[ti]])
                    kb.end_scope()
        kb.end_phase()

    def phase_rg(li, j):
        XT, xt_b = XTS[li % 2], xt_bs[li % 2]
        W = NT + 3
        sp_ = splits(W)
        wname = "mix_in%d" % li
        cw = load_T(w_in["rg_conv_w"][j], 4, 16, csz=96, persistent=False)
        cb = load_T(w_in["rg_conv_b"][j:j + 1, :], 1, 16, csz=96, persistent=False)
        gb = load_T(w_in["rg_gate_b"][j].rearrange("d n e -> (d n) e"), 32, 2, csz=96, persistent=False)
        lam = load_T(w_in["rg_lambda"][j], 2, 16, csz=96, persistent=False)
        kb.op(act, lambda: A.activation(out=lam[:], in_=lam[:], func=AF.Exp, scale=-1.0), reads=[lam], writes=[lam])
        kb.op(dve, lambda: V.tensor_scalar(out=lam[:], in0=lam[:], scalar1=1.0, scalar2=None, op0=ALU.add),
              reads=[lam], writes=[lam])
        kb.op(act, lambda: A.activation(out=lam[:], in_=lam[:], func=AF.Ln), reads=[lam], writes=[lam])
        nsp = kb.sb([96, 16, 2], F32, "nsp")
        kb.op(dve, lambda: V.tensor_scalar(out=nsp[:], in0=lam[:], scalar1=-8.0, scalar2=None, op0=ALU.mult),
              reads=[lam], writes=[nsp])
        nsp2 = kb.sb([96, 16, 2], F32, "nsp2")
        kb.op(dve, lambda: V.tensor_scalar(out=nsp2[:], in0=lam[:], scalar1=-16.0, scalar2=None, op0=ALU.mult),
              reads=[lam], writes=[nsp2])
        gwf = kb.sb([96, 16, 192], F32, "gwf")
        gw = kb.sb([96, 2, 16, 192], BF16, "gw")
        for d in range(2):
            kb.load(gwf, gwf[:], w_in["rg_gate_w"][j, d].rearrange("n c e -> c n e"))
            kb.op(dve, lambda: V.tensor_copy(out=gw[:, d], in_=gwf[:]), reads=[gwf], writes=[gw])

        carry = kb.sb([96, 16], F32, "carry")
        pi = [0]
        cnt = [0]

        def nextps():
            p = ps[pi[0] % 8]
            pi[0] += 1
            return p

        def rev(a, n):
            st = a.ap[-1][0]
            return bass.AP(a.tensor, a.offset + (n - 1) * st, [list(x) for x in a.ap[:-1]] + [[-st, n]])

        for d in range(2):
            for si in range(len(kb.seqs)):
                tl = seq_tiles(si)
                order = tl if d == 0 else tl[::-1]
                kb.op(dve, lambda: V.memset(carry[:], 0.0), reads=[], writes=[carry])
                for ti in order:
                    kb.begin_scope("rg")
                    ps_alloc()
                    ws = WStream()
                    xt_s = [kb.sb([128, 8, W], F32, "xt") for _ in range(2)]
                    ht = kb.sb([128, 8, W], BF16, "ht")
                    sq = [kb.sb([128, W], BF16, "sq") for _ in range(2)]
                    rinv = kb.sb([128, W], F32, "rinv")
                    uev = [kb.sb([96, W], F32, "uev") for _ in range(2)]
                    c1 = [kb.sb([96, NT], F32, "c1") for _ in range(2)]
                    c2 = [kb.sb([96, NT], F32, "c2") for _ in range(2)]
                    uc = [kb.sb([96, NT], F32, "uc") for _ in range(2)]
                    ucb = [kb.sb([96, NT], BF16, "ucb") for _ in range(2)]
                    rr = [kb.sb([96, NT], F32, "rr") for _ in range(2)]
                    ig = [kb.sb([96, NT], F32, "ig") for _ in range(2)]
                    aa = [kb.sb([96, NT], F32, "aa") for _ in range(2)]
                    bb = [kb.sb([96, NT], F32, "bb") for _ in range(2)]
                    hs_s = [kb.sb([96, NT], F32, "hs") for _ in range(3)]
                    hf_s = [kb.sb([96, NT], F32, "hfl") for _ in range(3)]
                    gg = [kb.sb([96, NT], F32, "gg") for _ in range(2)]
                    g2 = [kb.sb([96, NT], F32, "g2") for _ in range(2)]
                    yo_s = [kb.sb([96, NT], BF16, "yo") for _ in range(3)]
                    xt = xt_s[ti % 2]
                    c0 = PADC + ti * NT - 2
                    nb = [xt_b[t] for t in (ti - 1, ti, ti + 1) if 0 <= t < ntiles]
                    kb.load(xt, xt[:], XT[:, :, c0:c0 + W].rearrange("c p t -> p c t"), dsrc=nb)
                    pp = [nextps() for _ in sp_]
                    rmsnorm_fm(xt, ht, W, sq, rinv, pp)
                    if ti == tl[0]:
                        kb.op(pool, lambda: P_.memset(ht[:, :, 0:2], 0.0), writes=[ht])
                    if ti == tl[-1]:
                        kb.op(pool, lambda: P_.memset(ht[:, :, W - 1:W], 0.0), writes=[ht])
                    for n in range(16):
                        k2 = n % 2
                        k3 = cnt[0] % 3
                        cnt[0] += 1
                        hs = hs_s[k3]
                        dn = d * 16 + n
                        wsl, wv = ws.get(wname, 16 + n, 1)
                        pu = [nextps() for _ in sp_]
                        for kc in range(8):
                            for bi, (a, b) in enumerate(sp_):
                                kb.mm(pu[bi], pu[bi][0:96, 0:b - a], wv[:, 0, kc, :], ht[:, kc, a:b], kc == 0, kc == 7,
                                      reads=[wsl, ht])
                        ue = uev[k2]
                        for bi, (a, b) in enumerate(sp_):
                            kb.op(act, lambda: A.copy(out=ue[:, a:b], in_=pu[bi][0:96, 0:b - a]), reads=[pu[bi]],
                                  writes=[ue])
                        kb.op(pool, lambda: P_.tensor_scalar(out=c1[k2][:], in0=ue[:, 0:NT], scalar1=cw[:, n, 0:1],
                                                             scalar2=cb[:, n, 0:1], op0=ALU.mult, op1=ALU.add),
                              reads=[ue, cw, cb], writes=[c1[k2]])
                        kb.op(dve, lambda: V.scalar_tensor_tensor(out=c2[k2][:], in0=ue[:, 1:1 + NT],
                                                                    scalar=cw[:, n, 1:2], in1=c1[k2][:],
                                                                    op0=ALU.mult, op1=ALU.add),
                              reads=[ue, cw, c1[k2]], writes=[c2[k2]])
                        kb.op(dve, lambda: V.scalar_tensor_tensor(out=c1[k2][:], in0=ue[:, 2:2 + NT],
                                                                    scalar=cw[:, n, 2:3], in1=c2[k2][:],
                                                                    op0=ALU.mult, op1=ALU.add),
                              reads=[ue, cw, c2[k2]], writes=[c1[k2]])
                        kb.op(dve, lambda: V.scalar_tensor_tensor(out=uc[k2][:], in0=ue[:, 3:3 + NT],
                                                                  scalar=cw[:, n, 3:4], in1=c1[k2][:],
                                                                  op0=ALU.mult, op1=ALU.add),
                              reads=[ue, cw, c1[k2]], writes=[uc[k2]])
                        kb.op(act, lambda: A.copy(out=ucb[k2][:], in_=uc[k2][:]), reads=[uc[k2]], writes=[ucb[k2]])
                        pr = nextps()
                        kb.mm(pr, pr[0:96, 0:NT], gw[:, d, n, 0:96], ucb[k2][:], True, True, reads=[gw, ucb[k2]])
                        pg_ = nextps()
                        kb.mm(pg_, pg_[0:96, 0:NT], gw[:, d, n, 96:192], ucb[k2][:], True, True, reads=[gw, ucb[k2]])
                        kb.op(act, lambda: A.activation(out=rr[k2][:], in_=pr[0:96, 0:NT], func=AF.Sigmoid,
                                                        bias=gb[:, 0, dn:dn + 1]), reads=[pr, gb], writes=[rr[k2]])
                        kb.op(act, lambda: A.activation(out=ig[k2][:], in_=pg_[0:96, 0:NT], func=AF.Sigmoid,
                                                        bias=gb[:, 1, dn:dn + 1]), reads=[pg_, gb], writes=[ig[k2]])
                        kb.op(act, lambda: A.activation(out=aa[k2][:], in_=rr[k2][:], func=AF.Exp,
                                                        scale=nsp[:, n, d:d + 1]), reads=[rr[k2], nsp], writes=[aa[k2]])
                        kb.op(act, lambda: A.activation(out=bb[k2][:], in_=rr[k2][:], func=AF.Exp,
                                                        scale=nsp2[:, n, d:d + 1]), reads=[rr[k2], nsp2],
                              writes=[bb[k2]])
                        kb.op(dve, lambda: V.tensor_scalar(out=bb[k2][:], in0=bb[k2][:], scalar1=-1.0, scalar2=1.0,
                                                           op0=ALU.mult, op1=ALU.add), reads=[bb[k2]], writes=[bb[k2]])
                        kb.op(act, lambda: A.activation(out=bb[k2][:], in_=bb[k2][:], func=AF.Sqrt), reads=[bb[k2]],
                              writes=[bb[k2]])
                        kb.op(pool, lambda: P_.tensor_tensor(out=ig[k2][:], in0=ig[k2][:], in1=uc[k2][:], op=ALU.mult),
                              reads=[ig[k2], uc[k2]], writes=[ig[k2]])
                        kb.op(dve, lambda: V.tensor_tensor(out=bb[k2][:], in0=bb[k2][:], in1=ig[k2][:], op=ALU.mult),
                              reads=[bb[k2], ig[k2]], writes=[bb[k2]])
                        if d == 0:
                            kb.op(dve, lambda: V.tensor_tensor_scan(out=hs[:], data0=aa[k2][:], data1=bb[k2][:],
                                                                    initial=carry[:, n:n + 1], op0=ALU.mult,
                                                                    op1=ALU.add),
                                  reads=[aa[k2], bb[k2], carry], writes=[hs])
                            kb.op(dve, lambda: V.tensor_copy(out=carry[:, n:n + 1], in_=hs[:, NT - 1:NT]),
                                  reads=[hs], writes=[carry])
                            kb.store(hs, HF[n, :, ti * NT:(ti + 1) * NT], hs[:], ddst=[hf_b[ti]])
                        else:
                            hfl = hf_s[k3]
                            yo = yo_s[k3]
                            kb.load(hfl, hfl[:], HF[n, :, ti * NT:(ti + 1) * NT], dsrc=[hf_b[ti]])
                            kb.op(dve, lambda: V.tensor_tensor_scan(out=rev(hs[:], NT), data0=rev(aa[k2][:], NT),
                                                                    data1=rev(bb[k2][:], NT),
                                                                    initial=carry[:, n:n + 1], op0=ALU.mult,
                                                                    op1=ALU.add),
                                  reads=[aa[k2], bb[k2], carry], writes=[hs])
                            kb.op(dve, lambda: V.tensor_copy(out=carry[:, n:n + 1], in_=hs[:, 0:1]),
                                  reads=[hs], writes=[carry])
                            wsl2, wv2 = ws.get(wname, n, 1)
                            pgt = nextps()
                            for kc in range(8):
                                kb.mm(pgt, pgt[0:96, 0:NT], wv2[:, 0, kc, :], ht[:, kc, 2:2 + NT], kc == 0, kc == 7,
                                      reads=[wsl2, ht])
                            G_, G2 = gg[k2], g2[k2]
                            kb.op(act, lambda: A.copy(out=G_[:], in_=pgt[0:96, 0:NT]), reads=[pgt], writes=[G_])
                            kb.op(pool, lambda: P_.tensor_tensor(out=G2[:], in0=G_[:], in1=G_[:], op=ALU.mult),
                                  reads=[G_], writes=[G2])
                            kb.op(pool, lambda: P_.tensor_scalar(out=G2[:], in0=G2[:], scalar1=0.044715, scalar2=1.0,
                                                                 op0=ALU.mult, op1=ALU.add), reads=[G2], writes=[G2])
                            kb.op(pool, lambda: P_.tensor_tensor(out=G2[:], in0=G2[:], in1=G_[:], op=ALU.mult),
                                  reads=[G_, G2], writes=[G2])
                            kb.op(act, lambda: A.activation(out=G2[:], in_=G2[:], func=AF.Sigmoid,
                                                            scale=2.0 * math.sqrt(2.0 / math.pi)), reads=[G2],
                                  writes=[G2])
                            kb.op(pool, lambda: P_.tensor_tensor(out=G_[:], in0=G_[:], in1=G2[:], op=ALU.mult),
                                  reads=[G_, G2], writes=[G_])
                            kb.op(pool, lambda: P_.tensor_tensor(out=hfl[:], in0=hfl[:], in1=hs[:], op=ALU.add),
                                  reads=[hfl, hs], writes=[hfl])
                            kb.op(dve, lambda: V.tensor_tensor(out=yo[:], in0=hfl[:], in1=G_[:], op=ALU.mult),
                                  reads=[hfl, G_], writes=[yo])
                            kb.store(yo, AOT[n, 0:96, PADC + ti * NT:PADC + (ti + 1) * NT], yo[:], ddst=[ao_b[ti]])
                    kb.end_scope()
        kb.end_phase()

    phase_transpose_in()
    for li in range(nlayers):
        kind, j = li % 3, li // 3
        if kind == 0:
            phase_qkv(li, 0, j)
            phase_da_attn(li, j)
            phase_ffn(li, 128, 8)
        elif kind == 1:
            phase_qkv(li, 1, j)
            phase_wa_attn(li, j)
            phase_ffn(li, 128, 8)
        else:
            phase_rg(li, j)
            phase_ffn(li, 96, 16)
    phase_transpose_out()
    kb.finish()
    for g in reversed(psg):
        g.__exit__(None, None, None)
    return nc


def make_consts(Lmax):
    ident = np.eye(128, dtype=np.float32)
    rmat = np.zeros((128, 128), np.float32)
    for m in range(128):
        d = m % 64
        if d < 8:
            rmat[m + 8, m] = 1.0
        elif d < 16:
            rmat[m - 8, m] = 1.0
    bones = np.zeros((128, 128), np.float32)
    bones[:64, :64] = 1.0
    bones[64:, 64:] = 1.0
    i = np.arange(128)[:, None]
    c = np.arange(128)[None, :]
    m0 = np.where(i <= c, 0.0, -30000.0).astype(np.float32)
    m1 = np.where(c <= i, 0.0, -30000.0).astype(np.float32)
    mask = np.stack([np.tile(m0, (1, 4)), np.tile(m1, (1, 4))], axis=1).astype(np.float32)
    half = 8
    inv = (ROPE_THETA ** (-np.arange(half, dtype=np.float32) * 2.0 / 16.0)).astype(np.float32)
    pos = np.arange(Lmax, dtype=np.float32)
    ang = pos[None, :] * inv[:, None]
    cosv = np.cos(ang).astype(np.float32)
    sinv = np.sin(ang).astype(np.float32)
    cos_t = np.ones((128, Lmax), np.float32)
    sin_t = np.zeros((128, Lmax), np.float32)
    for p in range(128):
        d = p % 64
        if d < 8:
            cos_t[p] = cosv[d]
            sin_t[p] = -sinv[d]
        elif d < 16:
            cos_t[p] = cosv[d - 8]
            sin_t[p] = sinv[d - 8]
    return {"c_ident": ident, "c_rmat": rmat, "c_bones": bones, "c_mask": mask, "c_cos": cos_t, "c_sin": sin_t}


_WNAMES = ["norm_mix", "norm_ffn", "ffn_w_up", "ffn_conv_w", "ffn_conv_b", "ffn_w_down", "da_w_qkv", "da_q_norm",
           "da_k_norm", "da_lambda_q1", "da_lambda_k1", "da_lambda_q2", "da_lambda_k2", "da_sub_norm", "da_w_o",
           "wa_w_qkv", "wa_q_norm", "wa_k_norm", "wa_sink", "wa_w_o", "rg_w_in", "rg_conv_w", "rg_conv_b",
           "rg_gate_w", "rg_gate_b", "rg_lambda", "rg_w_out"]


def run(x_prompt, x_sample, weights, nlayers=4):
    Bp, Sp, _ = x_prompt.shape
    Bs, Ss, _ = x_sample.shape
    ncores = 8
    per = Bs // ncores
    seqs = [Sp] + [Ss] * per
    nc = build(seqs, nlayers)
    consts = make_consts(max(seqs))
    wts = {k: np.ascontiguousarray(np.asarray(weights[k], dtype=np.float32)) for k in _WNAMES}
    in_maps = []
    for c in range(ncores):
        xp = x_prompt[c % Bp]
        xs = [x_sample[c * per + i] for i in range(per)]
        xc = np.ascontiguousarray(np.concatenate([xp] + xs, axis=0).astype(np.float32))
        m = {"x": xc}
        m.update(wts)
        m.update(consts)
        in_maps.append(m)
    res = run_bass_kernel_spmd(nc, in_maps, core_ids=list(range(ncores)))
    yp = np.stack([res.results[c]["y"][:Sp] for c in range(Bp)], axis=0)
    ys = np.stack([res.results[c]["y"][Sp + i * Ss:Sp + (i + 1) * Ss] for c in range(ncores) for i in range(per)], axis=0)
    return yp.astype(np.float32), ys.astype(np.float32)


def kernel(**inputs):
    x_prompt = np.asarray(inputs["x_prompt"])
    x_sample = np.asarray(inputs["x_sample"])
    return run(x_prompt, x_sample, inputs, 4)
```

```python
import math
import numpy as np
import concourse.bass as bass
import concourse.mybir as mybir
from concourse.bass_utils import run_bass_kernel_spmd

F32 = mybir.dt.float32
BF16 = mybir.dt.bfloat16
AF = mybir.ActivationFunctionType
ALU = mybir.AluOpType

D = 1024
DFF = 2816
DRNN = 1536
EPS = 1e-6
NT = 512
ROPE_THETA = 500000.0


class Sem:
    def __init__(self, h):
        self.h = h
        self.cnt = 0


class Buf:
    __slots__ = ("w", "r")

    def __init__(self):
        self.w = {}
        self.r = {}


class Tile:
    def __init__(self, kb, h, col0=None):
        self.kb = kb
        self.h = h
        self.buf = Buf()
        self._ld = None
        self._st = None
        self.persistent = False
        self.col0 = col0
        self.shared = None

    def __getitem__(self, k):
        if self.col0 is None:
            return self.h[k]
        if not isinstance(k, tuple):
            k = (k, slice(None))
        p, c = k
        a = 0 if c.start is None else c.start
        b = 512 if c.stop is None else c.stop
        return self.h[p, self.col0 + a:self.col0 + b]

    @property
    def ld(self):
        if self.shared is not None:
            if self.shared[1] is None:
                self.shared[1] = self.kb.dsem(False)
            return self.shared[1]
        if self._ld is None:
            self._ld = self.kb.dsem(self.persistent)
        return self._ld

    @property
    def st(self):
        if self.shared is not None:
            if self.shared[2] is None:
                self.shared[2] = self.kb.dsem(False)
            return self.shared[2]
        if self._st is None:
            self._st = self.kb.dsem(self.persistent)
        return self._st


class Eng:
    def __init__(self, kb, e, name, is_pe=False):
        self.e = e
        self.sem = kb.newsem("e_" + name)
        self.waited = {}
        self.is_pe = is_pe


def _bufs(xs):
    out = []
    for x in xs:
        if x is None:
            continue
        if isinstance(x, (list, tuple)):
            out.extend(_bufs(x))
        elif isinstance(x, Tile):
            out.append(x.buf)
        else:
            out.append(x)
    return out


class KB:
    def __init__(self, seqs, nlayers=4, debug=False):
        self.seqs = list(seqs)
        self.offs = [sum(self.seqs[:i]) for i in range(len(self.seqs))]
        self.T = sum(self.seqs)
        self.nlayers = nlayers
        self.nc = bass.Bass("TRN2", target_bir_lowering=False)
        self._ctx = []
        self._semn = 0
        self.sems = []
        nc = self.nc
        self.pe = Eng(self, nc.tensor, "pe", True)
        self.act = Eng(self, nc.scalar, "act")
        self.dve = Eng(self, nc.vector, "dve")
        self.pool = Eng(self, nc.gpsimd, "pool")
        self.sp = Eng(self, nc.sync, "sp")
        self.engs = [self.pe, self.act, self.dve, self.pool, self.sp]
        self.out_bufs = []
        self.phase_ctx = []
        self.free_dsems = []
        self.phase_dsems = []
        self.scope_tag = None
        self.scope_ctx = []
        self.scope_state = {}
        self.phase_id = 0

    def newsem(self, name):
        self._semn += 1
        g = self.nc.semaphore("%s%d" % (name, self._semn))
        h = g.__enter__()
        self._ctx.append(g)
        s = Sem(h)
        self.sems.append(s)
        return s

    def sb(self, shape, dt, name=None, persistent=False):
        self._semn += 1
        g = self.nc.sbuf_tensor("%s_%d" % (name or "t", self._semn), list(shape), dt)
        h = g.__enter__()
        t = Tile(self, h)
        t.persistent = persistent
        if persistent:
            self._ctx.append(g)
        elif self.scope_tag is not None:
            self.scope_ctx.append(g)
            key = (self.phase_id, self.scope_tag, self.scope_i)
            self.scope_i += 1
            st = self.scope_state.get(key)
            if st is None:
                st = [Buf(), None, None]
                self.scope_state[key] = st
            t.buf = st[0]
            t.shared = st
        else:
            self.phase_ctx.append(g)
        return t

    def begin_scope(self, tag):
        self.scope_tag = tag
        self.scope_i = 0
        self.scope_ctx = []

    def end_scope(self):
        for g in reversed(self.scope_ctx):
            g.__exit__(None, None, None)
        self.scope_ctx = []
        self.scope_tag = None

    def dsem(self, persistent):
        if self.free_dsems:
            s = self.free_dsems.pop()
        else:
            s = self.newsem("d")
        if not persistent:
            self.phase_dsems.append(s)
        return s

    def dram(self, name, shape, dt, kind="Internal"):
        return self.nc.dram_tensor(name, list(shape), dt, kind=kind).ap()

    def end_phase(self):
        self.barrier()
        for g in reversed(self.phase_ctx):
            g.__exit__(None, None, None)
        self.phase_ctx = []
        self.free_dsems.extend(self.phase_dsems)
        self.phase_dsems = []
        self.phase_id += 1

    def barrier(self):
        for E in self.engs:
            for s in self.sems:
                if s.cnt > 0 and not (s is E.sem):
                    if E.waited.get(s, 0) >= s.cnt:
                        continue
                    E.e.wait_ge(s.h, s.cnt)
                    E.waited[s] = s.cnt

    def _wait(self, E, s, v):
        if E.waited.get(s, 0) >= v:
            return
        E.e.wait_ge(s.h, v)
        E.waited[s] = v

    def _deps(self, E, reads, writes, no_self=False):
        deps = {}
        me = E.sem
        for b in reads:
            for s, v in b.w.items():
                if deps.get(s, 0) < v:
                    deps[s] = v
        for b in writes:
            for d in (b.w, b.r):
                for s, v in d.items():
                    if s is me:
                        continue
                    if deps.get(s, 0) < v:
                        deps[s] = v
        need = []
        for s, v in deps.items():
            if s is me and (E.is_pe or no_self):
                continue
            if E.waited.get(s, 0) >= v:
                continue
            E.waited[s] = v
            need.append((s, v))
        return need

    def _emit_waits(self, E, need, ins):
        for s, v in need[:-1]:
            E.e.wait_ge(s.h, v)
        if need:
            s, v = need[-1]
            ins._wait_ge(s.h, v)

    def op(self, E, fn, reads=(), writes=(), mark=True, no_self=False):
        reads = _bufs(reads)
        writes = _bufs(writes)
        need = self._deps(E, reads, writes, no_self)
        for s_, v_ in need[:-1]:
            E.e.wait_ge(s_.h, v_)
        ins = fn()
        if need:
            ins._wait_ge(need[-1][0].h, need[-1][1])
        s = E.sem
        if mark:
            s.cnt += 1
            ins.then_inc(s.h, 1)
            val = s.cnt
        else:
            val = s.cnt + 1
        for b in reads:
            if b.r.get(s, 0) < val:
                b.r[s] = val
        for b in writes:
            if b.w.get(s, 0) < val:
                b.w[s] = val
        return ins

    def dma(self, Q, out, in_, sem, reads=(), writes=(), **kw):
        reads = _bufs(reads)
        writes = _bufs(writes)
        need = self._deps(Q, reads, writes)
        for s_, v_ in need[:-1]:
            Q.e.wait_ge(s_.h, v_)
        ins = Q.e.dma_start(out=out, in_=in_, **kw)
        if need:
            ins._wait_ge(need[-1][0].h, need[-1][1])
        sem.cnt += 16
        ins.then_inc(sem.h, 16)
        for b in reads:
            b.r[sem] = sem.cnt
        for b in writes:
            b.w[sem] = sem.cnt
        return ins

    def load(self, tile, out_ap, in_ap, dsrc=(), **kw):
        return self.dma(self.sp, out_ap, in_ap, tile.ld, reads=dsrc, writes=[tile], **kw)

    def store(self, tile, out_ap, in_ap, ddst=(), **kw):
        return self.dma(self.pool, out_ap, in_ap, tile.st, reads=[tile], writes=ddst, **kw)

    def mm(self, ps, out_ap, lhsT, rhs, start, stop, reads, mark=None, **kw):
        if mark is None:
            mark = stop
        return self.op(self.pe, lambda: self.nc.tensor.matmul(out_ap, lhsT=lhsT, rhs=rhs, start=start, stop=stop, **kw),
                       reads=reads, writes=[ps], mark=mark)

    def finish(self):
        self.barrier()
        for g in reversed(self.phase_ctx):
            g.__exit__(None, None, None)
        for g in reversed(self._ctx):
            g.__exit__(None, None, None)


def splits(W):
    if W <= 512:
        return [(0, W)]
    h = (W + 1) // 2
    return [(0, h), (h, W)]


def build(seqs, nlayers=4):
    kb = KB(seqs, nlayers)
    nc = kb.nc
    T = kb.T
    NB = T // 128
    Lmax = max(seqs)
    pe, act, dve, pool, sp = kb.pe, kb.act, kb.dve, kb.pool, kb.sp
    V, A, P_ = nc.vector, nc.scalar, nc.gpsimd

    def ein(name, shape):
        return kb.dram(name, shape, F32, kind="ExternalInput")

    x_in = ein("x", [T, D])
    y_out = kb.dram("y", [T, D], F32, kind="ExternalOutput")
    w_in = {
        "norm_mix": ein("norm_mix", [4, D]), "norm_ffn": ein("norm_ffn", [4, D]),
        "ffn_w_up": ein("ffn_w_up", [4, D, 2 * DFF]), "ffn_conv_w": ein("ffn_conv_w", [4, 3, DFF]),
        "ffn_conv_b": ein("ffn_conv_b", [4, DFF]), "ffn_w_down": ein("ffn_w_down", [4, DFF, D]),
        "da_w_qkv": ein("da_w_qkv", [2, D, 3072]), "da_q_norm": ein("da_q_norm", [2, 64]),
        "da_k_norm": ein("da_k_norm", [2, 64]), "da_lambda_q1": ein("da_lambda_q1", [2, 64]),
        "da_lambda_k1": ein("da_lambda_k1", [2, 64]), "da_lambda_q2": ein("da_lambda_q2", [2, 64]),
        "da_lambda_k2": ein("da_lambda_k2", [2, 64]), "da_sub_norm": ein("da_sub_norm", [2, 128]),
        "da_w_o": ein("da_w_o", [2, D, D]), "wa_w_qkv": ein("wa_w_qkv", [1, D, 1536]),
        "wa_q_norm": ein("wa_q_norm", [1, 64]), "wa_k_norm": ein("wa_k_norm", [1, 64]),
        "wa_sink": ein("wa_sink", [1, 16]), "wa_w_o": ein("wa_w_o", [1, D, D]),
        "rg_w_in": ein("rg_w_in", [1, D, 2 * DRNN]), "rg_conv_w": ein("rg_conv_w", [1, 4, DRNN]),
        "rg_conv_b": ein("rg_conv_b", [1, DRNN]), "rg_gate_w": ein("rg_gate_w", [1, 2, 16, 96, 192]),
        "rg_gate_b": ein("rg_gate_b", [1, 2, 16, 192]), "rg_lambda": ein("rg_lambda", [1, 2, DRNN]),
        "rg_w_out": ein("rg_w_out", [1, DRNN, D]),
    }
    c_ident = ein("c_ident", [128, 128])
    c_rmat = ein("c_rmat", [128, 128])
    c_bones = ein("c_bones", [128, 128])
    c_mask = ein("c_mask", [128, 2, 512])
    c_cos = ein("c_cos", [128, Lmax])
    c_sin = ein("c_sin", [128, Lmax])

    XTS = [kb.dram("XT%d" % i, [8, 128, T + 4], F32) for i in range(2)]
    AOT = kb.dram("AOT", [16, 128, T + 4], BF16)
    QT = kb.dram("QT", [8, 128, T], BF16)
    KT = kb.dram("KT", [8, 128, T], BF16)
    VV = kb.dram("VV", [8 * 128 * NB * 128], BF16)
    HF = kb.dram("HF", [16, 96, T], F32)
    PADC = 2

    ntiles = T // NT
    xt_bs = [[Buf() for _ in range(ntiles)] for _ in range(2)]
    ao_b = [Buf() for _ in range(ntiles)]
    q_b = [Buf() for _ in range(ntiles)]
    k_b = [Buf() for _ in range(ntiles)]
    v_b = [Buf() for _ in range(ntiles)]
    hf_b = [Buf() for _ in range(ntiles)]
    y_b = [Buf() for _ in range(ntiles)]
    wdram_b = Buf()

    def seq_tiles(si):
        t0 = kb.offs[si] // NT
        return list(range(t0, t0 + kb.seqs[si] // NT))

    psg = []
    ps = []
    pspair = [None] * 4
    ps_bufs = [Buf() for _ in range(8)]
    ps_n = [0]
    for i in range(8):
        ps.append(None)

    def ps_alloc():
        for g in reversed(psg):
            g.__exit__(None, None, None)
        del psg[:]
        ps_n[0] += 1
        for i in range(4):
            g = nc.psum_tensor("pspair%d_%d" % (i, ps_n[0]), [128, 1024], F32)
            pspair[i] = g.__enter__()
            psg.append(g)
        for i in range(8):
            t = Tile(kb, pspair[i // 2], col0=(i % 2) * 512)
            t.buf = ps_bufs[i]
            ps[i] = t

    ps_alloc()

    def const_load(src, shape, dt=F32):
        t = kb.sb(shape, F32, "c", persistent=True)
        kb.load(t, t[:], src)
        if dt == F32:
            return t
        tb = kb.sb(shape, BF16, "cb", persistent=True)
        kb.op(dve, lambda: V.tensor_copy(out=tb[:], in_=t[:]), reads=[t], writes=[tb])
        return tb

    ident_f = const_load(c_ident, [128, 128])
    ident_b = const_load(c_ident, [128, 128], BF16)
    rmat_f = const_load(c_rmat, [128, 128])
    bones_b = const_load(c_bones, [128, 128], BF16)
    mask_b = const_load(c_mask.rearrange("p a b -> p (a b)"), [128, 1024], BF16)
    ones_b = kb.sb([128, 128], BF16, "ones", persistent=True)
    kb.op(dve, lambda: V.memset(ones_b[:], 1.0), writes=[ones_b])
    ones_f = kb.sb([128, 128], F32, "onesf", persistent=True)
    kb.op(dve, lambda: V.memset(ones_f[:], 1.0), writes=[ones_f])

    def small_load(src_ap, shape):
        t = kb.sb(shape, F32, "sp", persistent=True)
        kb.load(t, t[:], src_ap, allow_slow_non_contiguous=True)
        return t

    def load_T(src2d, R, C, csz=128, persistent=True, o=None):
        if o is None:
            o = kb.sb([csz, C, R], F32, "lt")
        stg_ = kb.sb([R, C * csz], F32, "ltstg")
        kb.load(stg_, stg_[:], src2d)
        for c in range(C):
            pt = ps[c % 8]
            kb.op(pe, lambda: nc.tensor.transpose(out=pt[0:csz, 0:R], in_=stg_[0:R, c * csz:(c + 1) * csz],
                                                  identity=ident_f[0:R, 0:R]), reads=[stg_, ident_f], writes=[pt])
            kb.op(dve, lambda: V.tensor_copy(out=o[:, c, :], in_=pt[0:csz, 0:R]), reads=[pt], writes=[o])
        return o

    fcw = kb.sb([128, 22, 12], F32, "fcw", persistent=True)
    fcb = kb.sb([128, 22, 4], F32, "fcb", persistent=True)
    load_T(w_in["ffn_conv_w"].rearrange("l k f -> (l k) f"), 12, 22, o=fcw)
    load_T(w_in["ffn_conv_b"], 4, 22, o=fcb)
    gmix = load_T(w_in["norm_mix"], 4, 8)
    gffn = load_T(w_in["norm_ffn"], 4, 8)

    wbf = {}
    pcnt = [0]

    def prep_weight(name, src, K, N, ksz, msz, gain=None):
        KC, MC = K // ksz, N // msz
        dst = kb.dram("wb_" + name, [MC, ksz, KC, msz], BF16)
        wbf[name] = (dst, KC, MC, ksz, msz)
        CW = 2816 if N > 3072 else N
        for kc in range(KC):
            for c0 in range(0, N, CW):
                cw = min(CW, N - c0)
                stg = stage[pcnt[0] % 2]
                stb = stageb[pcnt[0] % 2]
                pcnt[0] += 1
                kb.load(stg, stg[0:ksz, 0:cw], src[kc * ksz:(kc + 1) * ksz, c0:c0 + cw])
                if gain is not None:
                    gap = gain(kc)
                    kb.op(dve, lambda: V.tensor_scalar(out=stb[0:ksz, 0:cw], in0=stg[0:ksz, 0:cw], scalar1=gap,
                                                       scalar2=None, op0=ALU.mult),
                          reads=[stg], writes=[stb])
                else:
                    kb.op(act, lambda: A.copy(out=stb[0:ksz, 0:cw], in_=stg[0:ksz, 0:cw]), reads=[stg], writes=[stb])
                m0, m1 = c0 // msz, (c0 + cw) // msz
                kb.store(stb, dst[m0:m1, :, kc, :].rearrange("m p i -> p m i"),
                         stb[0:ksz, 0:cw].rearrange("p (m i) -> p m i", i=msz), ddst=[wdram_b])

    stage = [kb.sb([128, 3072], F32, "stg") for _ in range(2)]
    stageb = [kb.sb([128, 3072], BF16, "stgb") for _ in range(2)]
    layer_kind = [i % 3 for i in range(4)]
    for li in range(nlayers):
        kind, j = li % 3, li // 3
        g_of = (lambda li: (lambda kc: gmix[:, kc, li:li + 1]))(li)
        if kind == 0:
            prep_weight("mix_in%d" % li, w_in["da_w_qkv"][j], D, 3072, 128, 128, g_of)
            prep_weight("mix_out%d" % li, w_in["da_w_o"][j], D, D, 128, 128)
        elif kind == 1:
            prep_weight("mix_in%d" % li, w_in["wa_w_qkv"][j], D, 1536, 128, 128, g_of)
            prep_weight("mix_out%d" % li, w_in["wa_w_o"][j], D, D, 128, 128)
        else:
            prep_weight("mix_in%d" % li, w_in["rg_w_in"][j], D, 3072, 128, 96, g_of)
            prep_weight("mix_out%d" % li, w_in["rg_w_out"][j], DRNN, D, 96, 128)
        gf_of = (lambda li: (lambda kc: gffn[:, kc, li:li + 1]))(li)
        prep_weight("up%d" % li, w_in["ffn_w_up"][li], D, 2 * DFF, 128, 128, gf_of)
        prep_weight("down%d" % li, w_in["ffn_w_down"][li], DFF, D, 128, 128)
    kb.end_phase()

    WSLOT = 4096
    NWS = 4

    class WStream:
        def __init__(self):
            self.slots = [kb.sb([128, WSLOT], BF16, "wslot") for _ in range(NWS)]
            self.i = 0

        def get(self, name, m0, nm):
            dst, KC, MC, ksz, msz = wbf[name]
            assert nm * KC * msz <= WSLOT
            s = self.slots[self.i % NWS]
            self.i += 1
            view = s[0:ksz, 0:nm * KC * msz].rearrange("p (m k i) -> p m k i", m=nm, k=KC)
            kb.load(s, view, dst[m0:m0 + nm].rearrange("m p k i -> p m k i"), dsrc=[wdram_b])
            return s, view

    def rmsnorm_fm(xt, ht, W, sq, rinv, psb, KCH=8):
        sp_ = splits(W)
        for c in range(KCH):
            s = sq[c % 2]
            kb.op(act, lambda: A.activation(out=s[:, 0:W], in_=xt[:, c, 0:W], func=AF.Square, scale=1.0 / 32.0),
                  reads=[xt], writes=[s])
            for bi, (a, b) in enumerate(sp_):
                kb.mm(psb[bi], psb[bi][:, 0:b - a], ones_b[:], s[:, a:b], c == 0, c == KCH - 1, reads=[ones_b, s],
                      mark=True)
        for bi, (a, b) in enumerate(sp_):
            kb.op(dve, lambda: V.tensor_scalar(out=rinv[:, a:b], in0=psb[bi][:, 0:b - a], scalar1=EPS, scalar2=None,
                                               op0=ALU.add), reads=[psb[bi]], writes=[rinv])
        kb.op(act, lambda: A.activation(out=rinv[:, 0:W], in_=rinv[:, 0:W], func=AF.Sqrt), reads=[rinv], writes=[rinv])
        kb.op(dve, lambda: V.reciprocal(out=rinv[:, 0:W], in_=rinv[:, 0:W]), reads=[rinv], writes=[rinv])
        for c in range(KCH):
            E, EE = (dve, V) if c % 2 == 0 else (pool, P_)
            kb.op(E, lambda: EE.tensor_tensor(out=ht[:, c, 0:W], in0=xt[:, c, 0:W], in1=rinv[:, 0:W], op=ALU.mult),
                  reads=[xt, rinv], writes=[ht])

    def phase_transpose_in():
        XT, xt_b = XTS[0], xt_bs[0]
        xin = [kb.sb([128, D], F32, "xin") for _ in range(3)]
        stg_ = [kb.sb([128, 8, NT], F32, "tstg") for _ in range(2)]
        for ti in range(ntiles):
            st_ = stg_[ti % 2]
            for sbk in range(4):
                blk = ti * 4 + sbk
                xi = xin[blk % 3]
                kb.load(xi, xi[:], x_in[blk * 128:(blk + 1) * 128, :])
                for half in range(2):
                    pt = ps[(blk * 2 + half) % 8]
                    for c4 in range(4):
                        c = half * 4 + c4
                        kb.op(pe, lambda: nc.tensor.transpose(out=pt[:, c4 * 128:(c4 + 1) * 128],
                                                              in_=xi[:, c * 128:(c + 1) * 128], identity=ident_f[:]),
                              reads=[xi, ident_f], writes=[pt], mark=(c4 == 3))
                    dstv = st_[:, half * 4:(half + 1) * 4, sbk * 128:(sbk + 1) * 128]
                    srcv = pt[:, :].rearrange("p (c t) -> p c t", c=4)
                    if half == 0:
                        kb.op(act, lambda: A.copy(out=dstv, in_=srcv), reads=[pt], writes=[st_])
                    else:
                        kb.op(dve, lambda: V.tensor_copy(out=dstv, in_=srcv), reads=[pt], writes=[st_])
            kb.store(st_, XT[:, :, PADC + ti * NT:PADC + (ti + 1) * NT].rearrange("c p t -> p c t"), st_[:],
                     ddst=[xt_b[ti]])
        kb.end_phase()

    def phase_transpose_out():
        XT, xt_b = XTS[nlayers % 2], xt_bs[nlayers % 2]
        xl = [kb.sb([128, 8, NT], F32, "xl") for _ in range(2)]
        yo = [kb.sb([128, D], F32, "yo") for _ in range(3)]
        for ti in range(ntiles):
            xt = xl[ti % 2]
            kb.load(xt, xt[:], XT[:, :, PADC + ti * NT:PADC + (ti + 1) * NT].rearrange("c p t -> p c t"),
                    dsrc=[xt_b[ti]])
            for sbk in range(4):
                blk = ti * 4 + sbk
                yt = yo[blk % 3]
                for half in range(2):
                    pt = ps[(blk * 2 + half) % 8]
                    for c4 in range(4):
                        c = half * 4 + c4
                        kb.op(pe, lambda: nc.tensor.transpose(out=pt[:, c4 * 128:(c4 + 1) * 128],
                                                              in_=xt[:, c, sbk * 128:(sbk + 1) * 128],
                                                              identity=ident_f[:]),
                              reads=[xt, ident_f], writes=[pt], mark=(c4 == 3))
                    if half == 0:
                        kb.op(act, lambda: A.copy(out=yt[:, 0:512], in_=pt[:, :]), reads=[pt], writes=[yt])
                    else:
                        kb.op(dve, lambda: V.tensor_copy(out=yt[:, 512:1024], in_=pt[:, :]), reads=[pt], writes=[yt])
                kb.store(yt, y_out[blk * 128:(blk + 1) * 128, :], yt[:], ddst=[y_b[ti]])
        kb.end_phase()

    def phase_ffn(li, aksz, akc):
        XT, xt_b = XTS[li % 2], xt_bs[li % 2]
        XTO, xto_b = XTS[(li + 1) % 2], xt_bs[(li + 1) % 2]
        W = NT + 2
        sp_ = splits(W)
        pi = [0]

        def nextps():
            p = ps[pi[0] % 8]
            pi[0] += 1
            return p

        for si in range(len(kb.seqs)):
            tl = seq_tiles(si)
            for ti in tl:
                kb.begin_scope("ffn")
                ps_alloc()
                ws = WStream()
                xt_s = [kb.sb([128, 8, W], F32, "xt") for _ in range(2)]
                ao_s = [kb.sb([128, akc, W], BF16, "ao") for _ in range(2)]
                ht = kb.sb([128, 8, W], BF16, "ht")
                sq = [kb.sb([128, W], BF16, "sq") for _ in range(2)]
                rinv = kb.sb([128, W], F32, "rinv")
                actT = kb.sb([128, 22, NT], BF16, "actT")
                gev = [kb.sb([128, W], F32, "gev") for _ in range(2)]
                cv1 = [kb.sb([128, NT], F32, "cv1") for _ in range(2)]
                cv2 = [kb.sb([128, NT], F32, "cv2") for _ in range(2)]
                xt = xt_s[ti % 2]
                ao = ao_s[ti % 2]
                c0 = PADC + ti * NT - 1
                nb = [xt_b[t] for t in (ti - 1, ti, ti + 1) if 0 <= t < ntiles]
                nab = [ao_b[t] for t in (ti - 1, ti, ti + 1) if 0 <= t < ntiles]
                kb.load(xt, xt[:], XT[:, :, c0:c0 + W].rearrange("c p t -> p c t"), dsrc=nb)
                kb.load(ao, ao[0:aksz], AOT[0:akc, 0:aksz, c0:c0 + W].rearrange("c p t -> p c t"), dsrc=nab)
                for mc in range(8):
                    if mc % 2 == 0:
                        wsl, wv = ws.get("mix_out%d" % li, mc, 2)
                    pp = [nextps() for _ in sp_]
                    for kc in range(akc):
                        for bi, (a, b) in enumerate(sp_):
                            kb.mm(pp[bi], pp[bi][:, 0:b - a], wv[0:aksz, mc % 2, kc, :], ao[0:aksz, kc, a:b], kc == 0,
                                  kc == akc - 1, reads=[wsl, ao])
                    for bi, (a, b) in enumerate(sp_):
                        kb.op(dve, lambda: V.tensor_tensor(out=xt[:, mc, a:b], in0=xt[:, mc, a:b],
                                                           in1=pp[bi][:, 0:b - a], op=ALU.add),
                              reads=[pp[bi], xt], writes=[xt])
                pp = [nextps() for _ in sp_]
                rmsnorm_fm(xt, ht, W, sq, rinv, pp)
                if ti == tl[0]:
                    kb.op(pool, lambda: P_.memset(ht[:, :, 0:1], 0.0), writes=[ht])
                if ti == tl[-1]:
                    kb.op(pool, lambda: P_.memset(ht[:, :, W - 1:W], 0.0), writes=[ht])
                for fc in range(22):
                    if fc % 2 == 0:
                        wsg, wvg = ws.get("up%d" % li, fc, 2)
                        wsu, wvu = ws.get("up%d" % li, 22 + fc, 2)
                    pg = [nextps() for _ in sp_]
                    for kc in range(8):
                        for bi, (a, b) in enumerate(sp_):
                            kb.mm(pg[bi], pg[bi][:, 0:b - a], wvg[:, fc % 2, kc, :], ht[:, kc, a:b], kc == 0, kc == 7,
                                  reads=[wsg, ht])
                    pu = nextps()
                    for kc in range(8):
                        kb.mm(pu, pu[:, 0:NT], wvu[:, fc % 2, kc, :], ht[:, kc, 1:1 + NT], kc == 0, kc == 7,
                              reads=[wsu, ht])
                    ge = gev[fc % 2]
                    for bi, (a, b) in enumerate(sp_):
                        kb.op(act, lambda: A.copy(out=ge[:, a:b], in_=pg[bi][:, 0:b - a]), reads=[pg[bi]], writes=[ge])
                    t1 = cv1[fc % 2]
                    t2 = cv2[fc % 2]
                    kb.op(pool, lambda: P_.tensor_scalar(out=t1[:], in0=ge[:, 0:NT], scalar1=fcw[:, fc, li * 3:li * 3 + 1],
                                                         scalar2=fcb[:, fc, li:li + 1], op0=ALU.mult, op1=ALU.add),
                          reads=[ge, fcw, fcb], writes=[t1])
                    kb.op(dve, lambda: V.scalar_tensor_tensor(out=t2[:], in0=ge[:, 1:1 + NT],
                                                                scalar=fcw[:, fc, li * 3 + 1:li * 3 + 2], in1=t1[:],
                                                                op0=ALU.mult, op1=ALU.add),
                          reads=[ge, fcw, t1], writes=[t2])
                    kb.op(dve, lambda: V.scalar_tensor_tensor(out=t1[:], in0=ge[:, 2:2 + NT],
                                                              scalar=fcw[:, fc, li * 3 + 2:li * 3 + 3], in1=t2[:],
                                                              op0=ALU.mult, op1=ALU.add),
                          reads=[ge, fcw, t2], writes=[t1])
                    kb.op(act, lambda: A.activation(out=t2[:], in_=t1[:], func=AF.Silu), reads=[t1], writes=[t2])
                    kb.op(dve, lambda: V.tensor_tensor(out=actT[:, fc, :], in0=t2[:], in1=pu[:, 0:NT], op=ALU.mult),
                          reads=[t2, pu], writes=[actT])
                for mc in range(8):
                    wsl, wv = ws.get("down%d" % li, mc, 1)
                    pd = nextps()
                    for kc in range(22):
                        kb.mm(pd, pd[:, 0:NT], wv[:, 0, kc, :], actT[:, kc, :], kc == 0, kc == 21, reads=[wsl, actT])
                    kb.op(dve, lambda: V.tensor_tensor(out=xt[:, mc, 1:1 + NT], in0=xt[:, mc, 1:1 + NT], in1=pd[:, 0:NT],
                                                       op=ALU.add), reads=[pd, xt], writes=[xt])
                kb.store(xt, XTO[:, :, PADC + ti * NT:PADC + (ti + 1) * NT].rearrange("c p t -> p c t"),
                         xt[:, :, 1:1 + NT], ddst=[xto_b[ti]])
                kb.end_scope()
        kb.end_phase()

    def qk_epilogue(pq, gap, cos_t, sin_t, outv, tmp):
        sqb, t_s, qn, t1, t2, _ob = tmp
        psq = pq[1]
        prq = pq[2]
        p0 = pq[0]
        kb.op(act, lambda: A.activation(out=sqb[:], in_=p0[:, 0:NT], func=AF.Square, scale=0.125),
              reads=[p0], writes=[sqb])
        kb.mm(psq, psq[:, 0:NT], bones_b[:], sqb[:], True, True, reads=[bones_b, sqb])
        kb.op(dve, lambda: V.tensor_scalar(out=t_s[:], in0=psq[:, 0:NT], scalar1=EPS, scalar2=None, op0=ALU.add),
              reads=[psq], writes=[t_s])
        kb.op(act, lambda: A.activation(out=t_s[:], in_=t_s[:], func=AF.Sqrt), reads=[t_s], writes=[t_s])
        kb.op(dve, lambda: V.reciprocal(out=t_s[:], in_=t_s[:]), reads=[t_s], writes=[t_s])
        kb.op(dve, lambda: V.scalar_tensor_tensor(out=qn[:], in0=p0[:, 0:NT], scalar=gap, in1=t_s[:], op0=ALU.mult,
                                                  op1=ALU.mult), reads=[p0, t_s], writes=[qn])
        kb.mm(prq, prq[:, 0:NT], rmat_f[:], qn[:], True, True, reads=[rmat_f, qn])
        kb.op(pool, lambda: P_.tensor_tensor(out=t1[:], in0=qn[:], in1=cos_t[:], op=ALU.mult),
              reads=[qn, cos_t], writes=[t1])
        kb.op(dve, lambda: V.tensor_tensor(out=t2[:], in0=prq[:, 0:NT], in1=sin_t[:], op=ALU.mult),
              reads=[prq, sin_t], writes=[t2])
        kb.op(pool, lambda: P_.tensor_tensor(out=outv, in0=t1[:], in1=t2[:], op=ALU.add),
              reads=[t1, t2], writes=[tmp[5]])

    def phase_qkv(li, kind, j):
        XT, xt_b = XTS[li % 2], xt_bs[li % 2]
        nq = 8
        nk = 8 if kind == 0 else 2
        nvcols = 1024 if kind == 0 else 256
        qn_src = w_in["da_q_norm"] if kind == 0 else w_in["wa_q_norm"]
        kn_src = w_in["da_k_norm"] if kind == 0 else w_in["wa_k_norm"]
        gq = kb.sb([128, 1], F32, "gq")
        gk = kb.sb([128, 1], F32, "gk")
        for half in range(2):
            kb.load(gq, gq[half * 64:(half + 1) * 64, :], qn_src[j:j + 1, :].rearrange("a d -> d a"),
                    allow_slow_non_contiguous=True)
            kb.load(gk, gk[half * 64:(half + 1) * 64, :], kn_src[j:j + 1, :].rearrange("a d -> d a"),
                    allow_slow_non_contiguous=True)
        pi = [0]

        def nextps():
            p = ps[pi[0] % 8]
            pi[0] += 1
            return p

        wname = "mix_in%d" % li
        for si in range(len(kb.seqs)):
            for n_, ti in enumerate(seq_tiles(si)):
                kb.begin_scope("qkv")
                ps_alloc()
                ws = WStream()
                xt_s = [kb.sb([128, 8, NT], F32, "xt") for _ in range(2)]
                ht = kb.sb([128, 8, NT], BF16, "ht")
                sq = [kb.sb([128, NT], BF16, "sq") for _ in range(2)]
                rinv = kb.sb([128, NT], F32, "rinv")
                cos_s = [kb.sb([128, NT], F32, "cos") for _ in range(2)]
                sin_s = [kb.sb([128, NT], F32, "sin") for _ in range(2)]
                sqb = [kb.sb([128, NT], BF16, "sqb") for _ in range(2)]
                t_s = [kb.sb([128, NT], F32, "ts") for _ in range(2)]
                qn = [kb.sb([128, NT], F32, "qn") for _ in range(2)]
                t1 = [kb.sb([128, NT], F32, "t1") for _ in range(2)]
                t2 = [kb.sb([128, NT], F32, "t2") for _ in range(2)]
                qo_s = [kb.sb([128, nq, NT], BF16, "qo") for _ in range(2)]
                ko_s = [kb.sb([128, nk, NT], BF16, "ko") for _ in range(2)]
                vo_s = [kb.sb([128, 4, nvcols], BF16, "vo") for _ in range(2)]
                xt = xt_s[ti % 2]
                qo, ko, vo = qo_s[ti % 2], ko_s[ti % 2], vo_s[ti % 2]
                cs, sn = cos_s[ti % 2], sin_s[ti % 2]
                kb.load(xt, xt[:], XT[:, :, PADC + ti * NT:PADC + (ti + 1) * NT].rearrange("c p t -> p c t"),
                        dsrc=[xt_b[ti]])
                kb.load(cs, cs[:], c_cos[:, n_ * NT:(n_ + 1) * NT])
                kb.load(sn, sn[:], c_sin[:, n_ * NT:(n_ + 1) * NT])
                rmsnorm_fm(xt, ht, NT, sq, rinv, [nextps()])
                for mc in range(nq + nk):
                    if mc % 2 == 0:
                        wsl, wv = ws.get(wname, mc, 2)
                    pq = [nextps(), nextps(), nextps()]
                    for kc in range(8):
                        kb.mm(pq[0], pq[0][:, 0:NT], wv[:, mc % 2, kc, :], ht[:, kc, :], kc == 0, kc == 7,
                              reads=[wsl, ht])
                    isq = mc < nq
                    outv = qo[:, mc, :] if isq else ko[:, mc - nq, :]
                    tmp = (sqb[mc % 2], t_s[mc % 2], qn[mc % 2], t1[mc % 2], t2[mc % 2], qo if isq else ko)
                    qk_epilogue(pq, (gq if isq else gk)[:, 0:1], cs, sn, outv, tmp)
                vm0 = nq + nk
                nvg = nvcols // 256
                for vg in range(nvg):
                    wsl, wv = ws.get(wname, vm0 + vg * 2, 2)
                    for sbk in range(4):
                        pv = nextps()
                        for kc in range(8):
                            kb.mm(pv, pv[:, 0:256], ht[:, kc, sbk * 128:(sbk + 1) * 128], wv[:, :, kc, :], kc == 0,
                                  kc == 7, reads=[wsl, ht])
                        if sbk % 2 == 0:
                            kb.op(act, lambda: A.copy(out=vo[:, sbk, vg * 256:(vg + 1) * 256], in_=pv[:, 0:256]),
                                  reads=[pv], writes=[vo])
                        else:
                            kb.op(dve, lambda: V.tensor_copy(out=vo[:, sbk, vg * 256:(vg + 1) * 256], in_=pv[:, 0:256]),
                                  reads=[pv], writes=[vo])
                kb.store(qo, QT[0:nq, :, ti * NT:(ti + 1) * NT].rearrange("c p t -> p c t"), qo[:], ddst=[q_b[ti]])
                kb.store(ko, KT[0:nk, :, ti * NT:(ti + 1) * NT].rearrange("c p t -> p c t"), ko[:], ddst=[k_b[ti]])
                nh = nvcols // (128 if kind == 0 else 64)
                dv = nvcols // nh
                vview = VV[0:nh * 128 * NB * dv].rearrange("(h p b d) -> p h b d", h=nh, p=128, b=NB)
                for sbk in range(4):
                    kb.store(vo, vview[:, :, ti * 4 + sbk, :], vo[:, sbk, :].rearrange("p (h d) -> p h d", h=nh),
                             ddst=[v_b[ti]])
                kb.end_scope()
        kb.end_phase()

    def phase_da_attn(li, j):
        lam_init = 0.8 - 0.6 * math.exp(-0.3 * li)
        Lm = max(kb.seqs)
        lv = kb.sb([128, 4, 64], F32, "lv")
        for i_, nm in enumerate(["da_lambda_q1", "da_lambda_k1", "da_lambda_q2", "da_lambda_k2"]):
            kb.load(lv, lv[:, i_, :], w_in[nm][j:j + 1, :].partition_broadcast(128))
        lp = kb.sb([128, 2, 64], F32, "lp")
        ls = kb.sb([128, 4], F32, "ls")
        kb.op(dve, lambda: V.tensor_tensor(out=lp[:, 0, :], in0=lv[:, 0, :], in1=lv[:, 1, :], op=ALU.mult),
              reads=[lv], writes=[lp])
        kb.op(dve, lambda: V.tensor_tensor(out=lp[:, 1, :], in0=lv[:, 2, :], in1=lv[:, 3, :], op=ALU.mult),
              reads=[lv], writes=[lp])
        kb.op(dve, lambda: V.reduce_sum(out=ls[:, 0:2], in_=lp[:], axis=mybir.AxisListType.X), reads=[lp], writes=[ls])
        kb.op(act, lambda: A.activation(out=ls[:, 0:2], in_=ls[:, 0:2], func=AF.Exp), reads=[ls], writes=[ls])
        kb.op(dve, lambda: V.tensor_tensor(out=ls[:, 2:3], in0=ls[:, 1:2], in1=ls[:, 0:1], op=ALU.subtract),
              reads=[ls], writes=[ls])
        kb.op(dve, lambda: V.tensor_scalar(out=ls[:, 3:4], in0=ls[:, 2:3], scalar1=-lam_init, scalar2=None,
                                           op0=ALU.add), reads=[ls], writes=[ls])
        neg_lam = ls[:, 3:4]
        sg = kb.sb([128, 2], F32, "sg")
        kb.load(sg, sg[:, 0:1], w_in["da_sub_norm"][j:j + 1, :].rearrange("a d -> d a"), allow_slow_non_contiguous=True)
        kb.op(dve, lambda: V.tensor_scalar(out=sg[:, 1:2], in0=sg[:, 0:1], scalar1=(1.0 - lam_init), scalar2=None,
                                           op0=ALU.mult), reads=[sg], writes=[sg])
        it = 0
        qi = 0
        for si in range(len(kb.seqs)):
            L = kb.seqs[si]
            off = kb.offs[si]
            tl = seq_tiles(si)
            nkt = L // 128
            for h in range(8):
                vview = VV[0:8 * 128 * NB * 128].rearrange("(h p b d) -> p h b d", h=8, p=128, b=NB)
                for ti in tl:
                    kb.begin_scope("da")
                    ps_alloc()
                    kt_t = kb.sb([128, Lm], BF16, "ktt")
                    v_t = kb.sb([128, Lm // 128, 128], BF16, "vtt")
                    q_s = [kb.sb([128, NT], BF16, "qs") for _ in range(2)]
                    p_s = [kb.sb([128, 2, NT], BF16, "pts") for _ in range(3)]
                    r0 = kb.sb([128, NT], F32, "r0")
                    r1 = kb.sb([128, NT], F32, "r1")
                    o_t = kb.sb([128, NT], F32, "ot")
                    sqb = kb.sb([128, NT], BF16, "sqb")
                    rs = kb.sb([128, NT], F32, "rs")
                    on_s = [kb.sb([128, NT], BF16, "on") for _ in range(2)]
                    acc = [kb.sb([128, NT], F32, "acc") for _ in range(2)]
                    S_ps = [[ps[0], ps[1]], [ps[2], ps[3]]]
                    O_ps = [ps[4], ps[5]]
                    L_ps = [ps[6], ps[7]]
                    if ti == tl[0]:
                        kb.load(kt_t, kt_t[:, 0:L], KT[h, :, off:off + L], dsrc=[k_b[t] for t in tl])
                        kb.load(v_t, v_t[:, 0:nkt, :], vview[:, h, off // 128:off // 128 + nkt, :],
                                dsrc=[v_b[t] for t in tl])
                    qt = q_s[qi % 2]
                    on = on_s[qi % 2]
                    qi += 1
                    kb.load(qt, qt[:], QT[h, :, ti * NT:(ti + 1) * NT], dsrc=[q_b[ti]])
                    it0 = it
                    for kk in range(nkt + 1):
                        if kk < nkt:
                            kt = kk
                            pr_ = (it0 + kt) % 2
                            sp2 = S_ps[pr_]
                            pt = p_s[(it0 + kt) % 3]
                            for m in range(2):
                                kb.mm(sp2[m], sp2[m][:, 0:NT], kt_t[m * 64:(m + 1) * 64, kt * 128:(kt + 1) * 128],
                                      qt[m * 64:(m + 1) * 64, :], True, True, reads=[kt_t, qt])
                            kb.op(act, lambda: A.activation(out=pt[:].rearrange("p a b -> p (a b)"),
                                                            in_=pspair[pr_][:, 0:1024], func=AF.Exp,
                                                            scale=0.125), reads=[sp2[0], sp2[1]], writes=[pt])
                        if kk >= 1:
                            kt = kk - 1
                            pt = p_s[(it0 + kt) % 3]
                            last = kt == nkt - 1
                            for m in range(2):
                                kb.mm(O_ps[m], O_ps[m][:, 0:NT], v_t[:, kt, :], pt[:, m, :], kt == 0, last,
                                      reads=[v_t, pt], mark=True)
                            kb.mm(L_ps[1], L_ps[1][:, 0:NT], ones_b[:], pt[:, 1, :], kt == 0, last,
                                  reads=[ones_b, pt], mark=True)
                            ac = acc[kt % 2]
                            if kt < 2:
                                kb.op(dve, lambda: V.tensor_copy(out=ac[:], in_=pt[:, 0, :]), reads=[pt], writes=[ac],
                                      no_self=True)
                            else:
                                kb.op(dve, lambda: V.tensor_tensor(out=ac[:], in0=ac[:], in1=pt[:, 0, :], op=ALU.add),
                                      reads=[pt, ac], writes=[ac], no_self=True)
                    it = it0 + nkt
                    kb.op(dve, lambda: V.tensor_tensor(out=acc[0][:], in0=acc[0][:], in1=acc[1][:], op=ALU.add),
                          reads=[acc[0], acc[1]], writes=[acc[0]])
                    kb.mm(L_ps[0], L_ps[0][:, 0:NT], ones_f[:], acc[0][:], True, True,
                          reads=[ones_f, acc[0]], mark=True)
                    kb.op(dve, lambda: V.reciprocal(out=r0[:], in_=L_ps[0][:, 0:NT]), reads=[L_ps[0]], writes=[r0])
                    kb.op(dve, lambda: V.reciprocal(out=r1[:], in_=L_ps[1][:, 0:NT]), reads=[L_ps[1]], writes=[r1])
                    kb.op(dve, lambda: V.tensor_tensor(out=r0[:], in0=O_ps[0][:, 0:NT], in1=r0[:], op=ALU.mult),
                          reads=[O_ps[0], r0], writes=[r0])
                    kb.op(dve, lambda: V.tensor_tensor(out=r1[:], in0=O_ps[1][:, 0:NT], in1=r1[:], op=ALU.mult),
                          reads=[O_ps[1], r1], writes=[r1])
                    kb.op(dve, lambda: V.scalar_tensor_tensor(out=o_t[:], in0=r1[:], scalar=neg_lam, in1=r0[:],
                                                              op0=ALU.mult, op1=ALU.add),
                          reads=[r0, r1, ls], writes=[o_t])
                    kb.op(act, lambda: A.activation(out=sqb[:], in_=o_t[:], func=AF.Square,
                                                    scale=1.0 / math.sqrt(128.0)), reads=[o_t], writes=[sqb])
                    pss = S_ps[it % 2][0]
                    kb.mm(pss, pss[:, 0:NT], ones_b[:], sqb[:], True, True, reads=[ones_b, sqb])
                    kb.op(dve, lambda: V.tensor_scalar(out=rs[:], in0=pss[:, 0:NT], scalar1=EPS, scalar2=None,
                                                       op0=ALU.add), reads=[pss], writes=[rs])
                    kb.op(act, lambda: A.activation(out=rs[:], in_=rs[:], func=AF.Sqrt), reads=[rs], writes=[rs])
                    kb.op(dve, lambda: V.reciprocal(out=rs[:], in_=rs[:]), reads=[rs], writes=[rs])
                    kb.op(dve, lambda: V.scalar_tensor_tensor(out=on[:], in0=o_t[:], scalar=sg[:, 1:2], in1=rs[:],
                                                              op0=ALU.mult, op1=ALU.mult),
                          reads=[o_t, sg, rs], writes=[on])
                    kb.store(on, AOT[h, :, PADC + ti * NT:PADC + (ti + 1) * NT], on[:], ddst=[ao_b[ti]])
                    kb.end_scope()
        kb.end_phase()

    def phase_wa_attn(li, j):
        Lm = max(kb.seqs)
        sk = kb.sb([64, 16], F32, "sk")
        kb.load(sk, sk[:], w_in["wa_sink"][j:j + 1, :].partition_broadcast(64))
        kb.op(act, lambda: A.activation(out=sk[:], in_=sk[:], func=AF.Exp), reads=[sk], writes=[sk])
        es = kb.sb([64, 16, 128], F32, "es")
        kb.op(dve, lambda: V.memset(es[:], 0.0), writes=[es])
        for hq in range(16):
            kb.op(dve, lambda: V.tensor_scalar(out=es[:, hq, :], in0=es[:, hq, :], scalar1=sk[:, hq:hq + 1],
                                               scalar2=None, op0=ALU.add), reads=[sk, es], writes=[es])
        it = 0
        qi = 0
        vview = VV[0:4 * 128 * NB * 64].rearrange("(h p b d) -> p h b d", h=4, p=128, b=NB)
        for si in range(len(kb.seqs)):
            L = kb.seqs[si]
            off = kb.offs[si]
            tl = seq_tiles(si)
            nkb = L // 128
            for g in range(4):
                for n_, ti in enumerate(tl):
                    kb.begin_scope("wa")
                    ps_alloc()
                    kt_t = kb.sb([64, Lm], BF16, "ktt")
                    v_t = kb.sb([128, Lm // 128, 64], BF16, "vtt")
                    q_s = [kb.sb([64, 4, NT], BF16, "qs") for _ in range(2)]
                    p_s = [kb.sb([128, NT], BF16, "pts") for _ in range(4)]
                    lt = kb.sb([64, NT], F32, "lt")
                    o_s = [kb.sb([64, 4, NT], BF16, "os") for _ in range(2)]
                    if n_ == 0:
                        kb.load(kt_t, kt_t[:, 0:L], KT[g // 2, (g % 2) * 64:(g % 2) * 64 + 64, off:off + L],
                                dsrc=[k_b[t] for t in tl])
                        kb.load(v_t, v_t[:, 0:nkb, :], vview[:, g, off // 128:off // 128 + nkb, :],
                                dsrc=[v_b[t] for t in tl])
                    qt = q_s[qi % 2]
                    ot = o_s[qi % 2]
                    qi += 1
                    for c2 in range(2):
                        kb.load(qt, qt[:, c2 * 2:c2 * 2 + 2, :],
                                QT[g * 2 + c2, :, ti * NT:(ti + 1) * NT].rearrange("(a d) t -> d a t", a=2),
                                dsrc=[q_b[ti]])
                    for qb in range(4):
                        jb = n_ * 4 + qb
                        kbs = [b for b in (jb - 1, jb, jb + 1) if 0 <= b < nkb]
                        Ops = ps[4 + (it % 2)]
                        Lps = ps[6 + (it % 2)]
                        qrhs = qt[:, :, qb * 128:(qb + 1) * 128]
                        for n2, kblk in enumerate(kbs):
                            sp_ = ps[it % 4]
                            pt = p_s[it % 4]
                            it += 1
                            rel = kblk - jb
                            kb.mm(sp_, sp_[0:128, 0:NT], kt_t[:, kblk * 128:(kblk + 1) * 128], qrhs, True, rel == 0,
                                  reads=[kt_t, qt], mark=True)
                            if rel != 0:
                                mi = 0 if rel == 1 else 1
                                kb.mm(sp_, sp_[0:128, 0:NT], ident_b[:], mask_b[:, mi * 512:(mi + 1) * 512], False, True,
                                      reads=[ident_b, mask_b], mark=True)
                            kb.op(act, lambda: A.activation(out=pt[:], in_=sp_[:, 0:NT], func=AF.Exp, scale=0.125),
                                  reads=[sp_], writes=[pt])
                            first, last = n2 == 0, n2 == len(kbs) - 1
                            kb.mm(Ops, Ops[0:64, 0:NT], v_t[:, kblk, :], pt[:], first, last, reads=[v_t, pt], mark=True)
                            kb.mm(Lps, Lps[0:64, 0:NT], ones_b[:, 0:64], pt[:], first, last, reads=[ones_b, pt],
                                  mark=True)
                        kb.op(dve, lambda: V.tensor_tensor(out=lt[:], in0=Lps[0:64, 0:NT],
                                                           in1=es[:, g * 4:(g + 1) * 4, :].rearrange("p a b -> p (a b)"),
                                                           op=ALU.add), reads=[Lps, es], writes=[lt])
                        kb.op(dve, lambda: V.reciprocal(out=lt[:], in_=lt[:]), reads=[lt], writes=[lt])
                        kb.op(dve, lambda: V.tensor_tensor(out=ot[:, :, qb * 128:(qb + 1) * 128],
                                                           in0=Ops[0:64, 0:NT].rearrange("p (a b) -> p a b", a=4),
                                                           in1=lt[:].rearrange("p (a b) -> p a b", a=4), op=ALU.mult),
                              reads=[Ops, lt], writes=[ot])
                    for c2 in range(2):
                        kb.store(ot, AOT[g * 2 + c2, :, PADC + ti * NT:PADC + (ti + 1) * NT].rearrange(
                            "(a d) t -> d a t", a=2), ot[:, c2 * 2:c2 * 2 + 2, :], ddst=[ao_b[ti]])
                    kb.end_scope()
        kb.end_phase()

    def phase_rg(li, j):
        XT, xt_b = XTS[li % 2], xt_bs[li % 2]
        W = NT + 3
        sp_ = splits(W)
        wname = "mix_in%d" % li
        cw = load_T(w_in["rg_conv_w"][j], 4, 16, csz=96, persistent=False)
        cb = load_T(w_in["rg_conv_b"][j:j + 1, :], 1, 16, csz=96, persistent=False)
        gb = load_T(w_in["rg_gate_b"][j].rearrange("d n e -> (d n) e"), 32, 2, csz=96, persistent=False)
        lam = load_T(w_in["rg_lambda"][j], 2, 16, csz=96, persistent=False)
        kb.op(act, lambda: A.activation(out=lam[:], in_=lam[:], func=AF.Exp, scale=-1.0), reads=[lam], writes=[lam])
        kb.op(dve, lambda: V.tensor_scalar(out=lam[:], in0=lam[:], scalar1=1.0, scalar2=None, op0=ALU.add),
              reads=[lam], writes=[lam])
        kb.op(act, lambda: A.activation(out=lam[:], in_=lam[:], func=AF.Ln), reads=[lam], writes=[lam])
        nsp = kb.sb([96, 16, 2], F32, "nsp")
        kb.op(dve, lambda: V.tensor_scalar(out=nsp[:], in0=lam[:], scalar1=-8.0, scalar2=None, op0=ALU.mult),
              reads=[lam], writes=[nsp])
        nsp2 = kb.sb([96, 16, 2], F32, "nsp2")
        kb.op(dve, lambda: V.tensor_scalar(out=nsp2[:], in0=lam[:], scalar1=-16.0, scalar2=None, op0=ALU.mult),
              reads=[lam], writes=[nsp2])
        gwf = kb.sb([96, 16, 192], F32, "gwf")
        gw = kb.sb([96, 2, 16, 192], BF16, "gw")
        for d in range(2):
            kb.load(gwf, gwf[:], w_in["rg_gate_w"][j, d].rearrange("n c e -> c n e"))
            kb.op(dve, lambda: V.tensor_copy(out=gw[:, d], in_=gwf[:]), reads=[gwf], writes=[gw])

        carry = kb.sb([96, 16], F32, "carry")
        pi = [0]
        cnt = [0]

        def nextps():
            p = ps[pi[0] % 8]
            pi[0] += 1
            return p

        def rev(a, n):
            st = a.ap[-1][0]
            return bass.AP(a.tensor, a.offset + (n - 1) * st, [list(x) for x in a.ap[:-1]] + [[-st, n]])

        for d in range(2):
            for si in range(len(kb.seqs)):
                tl = seq_tiles(si)
                order = tl if d == 0 else tl[::-1]
                kb.op(dve, lambda: V.memset(carry[:], 0.0), reads=[], writes=[carry])
                for ti in order:
                    kb.begin_scope("rg")
                    ps_alloc()
                    ws = WStream()
                    xt_s = [kb.sb([128, 8, W], F32, "xt") for _ in range(2)]
                    ht = kb.sb([128, 8, W], BF16, "ht")
                    sq = [kb.sb([128, W], BF16, "sq") for _ in range(2)]
                    rinv = kb.sb([128, W], F32, "rinv")
                    uev = [kb.sb([96, W], F32, "uev") for _ in range(2)]
                    c1 = [kb.sb([96, NT], F32, "c1") for _ in range(2)]
                    c2 = [kb.sb([96, NT], F32, "c2") for _ in range(2)]
                    uc = [kb.sb([96, NT], F32, "uc") for _ in range(2)]
                    ucb = [kb.sb([96, NT], BF16, "ucb") for _ in range(2)]
                    rr = [kb.sb([96, NT], F32, "rr") for _ in range(2)]
                    ig = [kb.sb([96, NT], F32, "ig") for _ in range(2)]
                    aa = [kb.sb([96, NT], F32, "aa") for _ in range(2)]
                    bb = [kb.sb([96, NT], F32, "bb") for _ in range(2)]
                    hs_s = [kb.sb([96, NT], F32, "hs") for _ in range(3)]
                    hf_s = [kb.sb([96, NT], F32, "hfl") for _ in range(3)]
                    gg = [kb.sb([96, NT], F32, "gg") for _ in range(2)]
                    g2 = [kb.sb([96, NT], F32, "g2") for _ in range(2)]
                    yo_s = [kb.sb([96, NT], BF16, "yo") for _ in range(3)]
                    xt = xt_s[ti % 2]
                    c0 = PADC + ti * NT - 2
                    nb = [xt_b[t] for t in (ti - 1, ti, ti + 1) if 0 <= t < ntiles]
                    kb.load(xt, xt[:], XT[:, :, c0:c0 + W].rearrange("c p t -> p c t"), dsrc=nb)
                    pp = [nextps() for _ in sp_]
                    rmsnorm_fm(xt, ht, W, sq, rinv, pp)
                    if ti == tl[0]:
                        kb.op(pool, lambda: P_.memset(ht[:, :, 0:2], 0.0), writes=[ht])
                    if ti == tl[-1]:
                        kb.op(pool, lambda: P_.memset(ht[:, :, W - 1:W], 0.0), writes=[ht])
                    for n in range(16):
                        k2 = n % 2
                        k3 = cnt[0] % 3
                        cnt[0] += 1
                        hs = hs_s[k3]
                        dn = d * 16 + n
                        wsl, wv = ws.get(wname, 16 + n, 1)
                        pu = [nextps() for _ in sp_]
                        for kc in range(8):
                            for bi, (a, b) in enumerate(sp_):
                                kb.mm(pu[bi], pu[bi][0:96, 0:b - a], wv[:, 0, kc, :], ht[:, kc, a:b], kc == 0, kc == 7,
                                      reads=[wsl, ht])
                        ue = uev[k2]
                        for bi, (a, b) in enumerate(sp_):
                            kb.op(act, lambda: A.copy(out=ue[:, a:b], in_=pu[bi][0:96, 0:b - a]), reads=[pu[bi]],
                                  writes=[ue])
                        kb.op(pool, lambda: P_.tensor_scalar(out=c1[k2][:], in0=ue[:, 0:NT], scalar1=cw[:, n, 0:1],
                                                             scalar2=cb[:, n, 0:1], op0=ALU.mult, op1=ALU.add),
                              reads=[ue, cw, cb], writes=[c1[k2]])
                        kb.op(dve, lambda: V.scalar_tensor_tensor(out=c2[k2][:], in0=ue[:, 1:1 + NT],
                                                                    scalar=cw[:, n, 1:2], in1=c1[k2][:],
                                                                    op0=ALU.mult, op1=ALU.add),
                              reads=[ue, cw, c1[k2]], writes=[c2[k2]])
                        kb.op(dve, lambda: V.scalar_tensor_tensor(out=c1[k2][:], in0=ue[:, 2:2 + NT],
                                                                    scalar=cw[:, n, 2:3], in1=c2[k2][:],
                                                                    op0=ALU.mult, op1=ALU.add),
                              reads=[ue, cw, c2[k2]], writes=[c1[k2]])
                        kb.op(dve, lambda: V.scalar_tensor_tensor(out=uc[k2][:], in0=ue[:, 3:3 + NT],
                                                                  scalar=cw[:, n, 3:4], in1=c1[k2][:],
                                                                  op0=ALU.mult, op1=ALU.add),
                              reads=[ue, cw, c1[k2]], writes=[uc[k2]])
                        kb.op(act, lambda: A.copy(out=ucb[k2][:], in_=uc[k2][:]), reads=[uc[k2]], writes=[ucb[k2]])
                        pr = nextps()
                        kb.mm(pr, pr[0:96, 0:NT], gw[:, d, n, 0:96], ucb[k2][:], True, True, reads=[gw, ucb[k2]])
                        pg_ = nextps()
                        kb.mm(pg_, pg_[0:96, 0:NT], gw[:, d, n, 96:192], ucb[k2][:], True, True, reads=[gw, ucb[k2]])
                        kb.op(act, lambda: A.activation(out=rr[k2][:], in_=pr[0:96, 0:NT], func=AF.Sigmoid,
                                                        bias=gb[:, 0, dn:dn + 1]), reads=[pr, gb], writes=[rr[k2]])
                        kb.op(act, lambda: A.activation(out=ig[k2][:], in_=pg_[0:96, 0:NT], func=AF.Sigmoid,
                                                        bias=gb[:, 1, dn:dn + 1]), reads=[pg_, gb], writes=[ig[k2]])
                        kb.op(act, lambda: A.activation(out=aa[k2][:], in_=rr[k2][:], func=AF.Exp,
                                                        scale=nsp[:, n, d:d + 1]), reads=[rr[k2], nsp], writes=[aa[k2]])
                        kb.op(act, lambda: A.activation(out=bb[k2][:], in_=rr[k2][:], func=AF.Exp,
                                                        scale=nsp2[:, n, d:d + 1]), reads=[rr[k2], nsp2],
                              writes=[bb[k2]])
                        kb.op(dve, lambda: V.tensor_scalar(out=bb[k2][:], in0=bb[k2][:], scalar1=-1.0, scalar2=1.0,
                                                           op0=ALU.mult, op1=ALU.add), reads=[bb[k2]], writes=[bb[k2]])
                        kb.op(act, lambda: A.activation(out=bb[k2][:], in_=bb[k2][:], func=AF.Sqrt), reads=[bb[k2]],
                              writes=[bb[k2]])
                        kb.op(pool, lambda: P_.tensor_tensor(out=ig[k2][:], in0=ig[k2][:], in1=uc[k2][:], op=ALU.mult),
                              reads=[ig[k2], uc[k2]], writes=[ig[k2]])
                        kb.op(dve, lambda: V.tensor_tensor(out=bb[k2][:], in0=bb[k2][:], in1=ig[k2][:], op=ALU.mult),
                              reads=[bb[k2], ig[k2]], writes=[bb[k2]])
                        if d == 0:
                            kb.op(dve, lambda: V.tensor_tensor_scan(out=hs[:], data0=aa[k2][:], data1=bb[k2][:],
                                                                    initial=carry[:, n:n + 1], op0=ALU.mult,
                                                                    op1=ALU.add),
                                  reads=[aa[k2], bb[k2], carry], writes=[hs])
                            kb.op(dve, lambda: V.tensor_copy(out=carry[:, n:n + 1], in_=hs[:, NT - 1:NT]),
                                  reads=[hs], writes=[carry])
                            kb.store(hs, HF[n, :, ti * NT:(ti + 1) * NT], hs[:], ddst=[hf_b[ti]])
                        else:
                            hfl = hf_s[k3]
                            yo = yo_s[k3]
                            kb.load(hfl, hfl[:], HF[n, :, ti * NT:(ti + 1) * NT], dsrc=[hf_b[ti]])
                            kb.op(dve, lambda: V.tensor_tensor_scan(out=rev(hs[:], NT), data0=rev(aa[k2][:], NT),
                                                                    data1=rev(bb[k2][:], NT),
                                                                    initial=carry[:, n:n + 1], op0=ALU.mult,
                                                                    op1=ALU.add),
                                  reads=[aa[k2], bb[k2], carry], writes=[hs])
                            kb.op(dve, lambda: V.tensor_copy(out=carry[:, n:n + 1], in_=hs[:, 0:1]),
                                  reads=[hs], writes=[carry])
                            wsl2, wv2 = ws.get(wname, n, 1)
                            pgt = nextps()
                            for kc in range(8):
                                kb.mm(pgt, pgt[0:96, 0:NT], wv2[:, 0, kc, :], ht[:, kc, 2:2 + NT], kc == 0, kc == 7,
                                      reads=[wsl2, ht])
                            G_, G2 = gg[k2], g2[k2]
                            kb.op(act, lambda: A.copy(out=G_[:], in_=pgt[0:96, 0:NT]), reads=[pgt], writes=[G_])
                            kb.op(pool, lambda: P_.tensor_tensor(out=G2[:], in0=G_[:], in1=G_[:], op=ALU.mult),
                                  reads=[G_], writes=[G2])
                            kb.op(pool, lambda: P_.tensor_scalar(out=G2[:], in0=G2[:], scalar1=0.044715, scalar2=1.0,
                                                                 op0=ALU.mult, op1=ALU.add), reads=[G2], writes=[G2])
                            kb.op(pool, lambda: P_.tensor_tensor(out=G2[:], in0=G2[:], in1=G_[:], op=ALU.mult),
                                  reads=[G_, G2], writes=[G2])
                            kb.op(act, lambda: A.activation(out=G2[:], in_=G2[:], func=AF.Sigmoid,
                                                            scale=2.0 * math.sqrt(2.0 / math.pi)), reads=[G2],
                                  writes=[G2])
                            kb.op(pool, lambda: P_.tensor_tensor(out=G_[:], in0=G_[:], in1=G2[:], op=ALU.mult),
                                  reads=[G_, G2], writes=[G_])
                            kb.op(pool, lambda: P_.tensor_tensor(out=hfl[:], in0=hfl[:], in1=hs[:], op=ALU.add),
                                  reads=[hfl, hs], writes=[hfl])
                            kb.op(dve, lambda: V.tensor_tensor(out=yo[:], in0=hfl[:], in1=G_[:], op=ALU.mult),
                                  reads=[hfl, G_], writes=[yo])
                            kb.store(yo, AOT[n, 0:96, PADC + ti * NT:PADC + (ti + 1) * NT], yo[:], ddst=[ao_b[ti]])
                    kb.end_scope()
        kb.end_phase()

    phase_transpose_in()
    for li in range(nlayers):
        kind, j = li % 3, li // 3
        if kind == 0:
            phase_qkv(li, 0, j)
            phase_da_attn(li, j)
            phase_ffn(li, 128, 8)
        elif kind == 1:
            phase_qkv(li, 1, j)
            phase_wa_attn(li, j)
            phase_ffn(li, 128, 8)
        else:
            phase_rg(li, j)
            phase_ffn(li, 96, 16)
    phase_transpose_out()
    kb.finish()
    for g in reversed(psg):
        g.__exit__(None, None, None)
    return nc


def make_consts(Lmax):
    ident = np.eye(128, dtype=np.float32)
    rmat = np.zeros((128, 128), np.float32)
    for m in range(128):
        d = m % 64
        if d < 8:
            rmat[m + 8, m] = 1.0
        elif d < 16:
            rmat[m - 8, m] = 1.0
    bones = np.zeros((128, 128), np.float32)
    bones[:64, :64] = 1.0
    bones[64:, 64:] = 1.0
    i = np.arange(128)[:, None]
    c = np.arange(128)[None, :]
    m0 = np.where(i <= c, 0.0, -30000.0).astype(np.float32)
    m1 = np.where(c <= i, 0.0, -30000.0).astype(np.float32)
    mask = np.stack([np.tile(m0, (1, 4)), np.tile(m1, (1, 4))], axis=1).astype(np.float32)
    half = 8
    inv = (ROPE_THETA ** (-np.arange(half, dtype=np.float32) * 2.0 / 16.0)).astype(np.float32)
    pos = np.arange(Lmax, dtype=np.float32)
    ang = pos[None, :] * inv[:, None]
    cosv = np.cos(ang).astype(np.float32)
    sinv = np.sin(ang).astype(np.float32)
    cos_t = np.ones((128, Lmax), np.float32)
    sin_t = np.zeros((128, Lmax), np.float32)
    for p in range(128):
        d = p % 64
        if d < 8:
            cos_t[p] = cosv[d]
            sin_t[p] = -sinv[d]
        elif d < 16:
            cos_t[p] = cosv[d - 8]
            sin_t[p] = sinv[d - 8]
    return {"c_ident": ident, "c_rmat": rmat, "c_bones": bones, "c_mask": mask, "c_cos": cos_t, "c_sin": sin_t}


_WNAMES = ["norm_mix", "norm_ffn", "ffn_w_up", "ffn_conv_w", "ffn_conv_b", "ffn_w_down", "da_w_qkv", "da_q_norm",
           "da_k_norm", "da_lambda_q1", "da_lambda_k1", "da_lambda_q2", "da_lambda_k2", "da_sub_norm", "da_w_o",
           "wa_w_qkv", "wa_q_norm", "wa_k_norm", "wa_sink", "wa_w_o", "rg_w_in", "rg_conv_w", "rg_conv_b",
           "rg_gate_w", "rg_gate_b", "rg_lambda", "rg_w_out"]


def run(x_prompt, x_sample, weights, nlayers=4):
    Bp, Sp, _ = x_prompt.shape
    Bs, Ss, _ = x_sample.shape
    ncores = 8
    per = Bs // ncores
    seqs = [Sp] + [Ss] * per
    nc = build(seqs, nlayers)
    consts = make_consts(max(seqs))
    wts = {k: np.ascontiguousarray(np.asarray(weights[k], dtype=np.float32)) for k in _WNAMES}
    in_maps = []
    for c in range(ncores):
        xp = x_prompt[c % Bp]
        xs = [x_sample[c * per + i] for i in range(per)]
        xc = np.ascontiguousarray(np.concatenate([xp] + xs, axis=0).astype(np.float32))
        m = {"x": xc}
        m.update(wts)
        m.update(consts)
        in_maps.append(m)
    res = run_bass_kernel_spmd(nc, in_maps, core_ids=list(range(ncores)))
    yp = np.stack([res.results[c]["y"][:Sp] for c in range(Bp)], axis=0)
    ys = np.stack([res.results[c]["y"][Sp + i * Ss:Sp + (i + 1) * Ss] for c in range(ncores) for i in range(per)], axis=0)
    return yp.astype(np.float32), ys.astype(np.float32)


def kernel(**inputs):
    x_prompt = np.asarray(inputs["x_prompt"])
    x_sample = np.asarray(inputs["x_sample"])
    return run(x_prompt, x_sample, inputs, 4)
```

```python
import math
import numpy as np
import concourse.bass as bass
import concourse.mybir as mybir
from concourse.bass_utils import run_bass_kernel_spmd

F32 = mybir.dt.float32
BF16 = mybir.dt.bfloat16
AF = mybir.ActivationFunctionType
ALU = mybir.AluOpType

D = 1024
DFF = 2816
DRNN = 1536
EPS = 1e-6
NT = 512
ROPE_THETA = 500000.0


class Sem:
    def __init__(self, h):
        self.h = h
        self.cnt = 0


class Buf:
    __slots__ = ("w", "r")

    def __init__(self):
        self.w = {}
        self.r = {}


class Tile:
    def __init__(self, kb, h, col0=None):
        self.kb = kb
        self.h = h
        self.buf = Buf()
        self._ld = None
        self._st = None
        self.persistent = False
        self.col0 = col0
        self.shared = None

    def __getitem__(self, k):
        if self.col0 is None:
            return self.h[k]
        if not isinstance(k, tuple):
            k = (k, slice(None))
        p, c = k
        a = 0 if c.start is None else c.start
        b = 512 if c.stop is None else c.stop
        return self.h[p, self.col0 + a:self.col0 + b]

    @property
    def ld(self):
        if self.shared is not None:
            if self.shared[1] is None:
                self.shared[1] = self.kb.dsem(False)
            return self.shared[1]
        if self._ld is None:
            self._ld = self.kb.dsem(self.persistent)
        return self._ld

    @property
    def st(self):
        if self.shared is not None:
            if self.shared[2] is None:
                self.shared[2] = self.kb.dsem(False)
            return self.shared[2]
        if self._st is None:
            self._st = self.kb.dsem(self.persistent)
        return self._st


class Eng:
    def __init__(self, kb, e, name, is_pe=False):
        self.e = e
        self.sem = kb.newsem("e_" + name)
        self.waited = {}
        self.is_pe = is_pe


def _bufs(xs):
    out = []
    for x in xs:
        if x is None:
            continue
        if isinstance(x, (list, tuple)):
            out.extend(_bufs(x))
        elif isinstance(x, Tile):
            out.append(x.buf)
        else:
            out.append(x)
    return out


class KB:
    def __init__(self, seqs, nlayers=4, debug=False):
        self.seqs = list(seqs)
        self.offs = [sum(self.seqs[:i]) for i in range(len(self.seqs))]
        self.T = sum(self.seqs)
        self.nlayers = nlayers
        self.nc = bass.Bass("TRN2", target_bir_lowering=False)
        self._ctx = []
        self._semn = 0
        self.sems = []
        nc = self.nc
        self.pe = Eng(self, nc.tensor, "pe", True)
        self.act = Eng(self, nc.scalar, "act")
        self.dve = Eng(self, nc.vector, "dve")
        self.pool = Eng(self, nc.gpsimd, "pool")
        self.sp = Eng(self, nc.sync, "sp")
        self.engs = [self.pe, self.act, self.dve, self.pool, self.sp]
        self.out_bufs = []
        self.phase_ctx = []
        self.free_dsems = []
        self.phase_dsems = []
        self.scope_tag = None
        self.scope_ctx = []
        self.scope_state = {}
        self.phase_id = 0

    def newsem(self, name):
        self._semn += 1
        g = self.nc.semaphore("%s%d" % (name, self._semn))
        h = g.__enter__()
        self._ctx.append(g)
        s = Sem(h)
        self.sems.append(s)
        return s

    def sb(self, shape, dt, name=None, persistent=False):
        self._semn += 1
        g = self.nc.sbuf_tensor("%s_%d" % (name or "t", self._semn), list(shape), dt)
        h = g.__enter__()
        t = Tile(self, h)
        t.persistent = persistent
        if persistent:
            self._ctx.append(g)
        elif self.scope_tag is not None:
            self.scope_ctx.append(g)
            key = (self.phase_id, self.scope_tag, self.scope_i)
            self.scope_i += 1
            st = self.scope_state.get(key)
            if st is None:
                st = [Buf(), None, None]
                self.scope_state[key] = st
            t.buf = st[0]
            t.shared = st
        else:
            self.phase_ctx.append(g)
        return t

    def begin_scope(self, tag):
        self.scope_tag = tag
        self.scope_i = 0
        self.scope_ctx = []

    def end_scope(self):
        for g in reversed(self.scope_ctx):
            g.__exit__(None, None, None)
        self.scope_ctx = []
        self.scope_tag = None

    def dsem(self, persistent):
        if self.free_dsems:
            s = self.free_dsems.pop()
        else:
            s = self.newsem("d")
        if not persistent:
            self.phase_dsems.append(s)
        return s

    def dram(self, name, shape, dt, kind="Internal"):
        return self.nc.dram_tensor(name, list(shape), dt, kind=kind).ap()

    def end_phase(self):
        self.barrier()
        for g in reversed(self.phase_ctx):
            g.__exit__(None, None, None)
        self.phase_ctx = []
        self.free_dsems.extend(self.phase_dsems)
        self.phase_dsems = []
        self.phase_id += 1

    def barrier(self):
        for E in self.engs:
            for s in self.sems:
                if s.cnt > 0 and not (s is E.sem):
                    if E.waited.get(s, 0) >= s.cnt:
                        continue
                    E.e.wait_ge(s.h, s.cnt)
                    E.waited[s] = s.cnt

    def _wait(self, E, s, v):
        if E.waited.get(s, 0) >= v:
            return
        E.e.wait_ge(s.h, v)
        E.waited[s] = v

    def _deps(self, E, reads, writes, no_self=False):
        deps = {}
        me = E.sem
        for b in reads:
            for s, v in b.w.items():
                if deps.get(s, 0) < v:
                    deps[s] = v
        for b in writes:
            for d in (b.w, b.r):
                for s, v in d.items():
                    if s is me:
                        continue
                    if deps.get(s, 0) < v:
                        deps[s] = v
        need = []
        for s, v in deps.items():
            if s is me and (E.is_pe or no_self):
                continue
            if E.waited.get(s, 0) >= v:
                continue
            E.waited[s] = v
            need.append((s, v))
        return need

    def _emit_waits(self, E, need, ins):
        for s, v in need[:-1]:
            E.e.wait_ge(s.h, v)
        if need:
            s, v = need[-1]
            ins._wait_ge(s.h, v)

    def op(self, E, fn, reads=(), writes=(), mark=True, no_self=False):
        reads = _bufs(reads)
        writes = _bufs(writes)
        need = self._deps(E, reads, writes, no_self)
        for s_, v_ in need[:-1]:
            E.e.wait_ge(s_.h, v_)
        ins = fn()
        if need:
            ins._wait_ge(need[-1][0].h, need[-1][1])
        s = E.sem
        if mark:
            s.cnt += 1
            ins.then_inc(s.h, 1)
            val = s.cnt
        else:
            val = s.cnt + 1
        for b in reads:
            if b.r.get(s, 0) < val:
                b.r[s] = val
        for b in writes:
            if b.w.get(s, 0) < val:
                b.w[s] = val
        return ins

    def dma(self, Q, out, in_, sem, reads=(), writes=(), **kw):
        reads = _bufs(reads)
        writes = _bufs(writes)
        need = self._deps(Q, reads, writes)
        for s_, v_ in need[:-1]:
            Q.e.wait_ge(s_.h, v_)
        ins = Q.e.dma_start(out=out, in_=in_, **kw)
        if need:
            ins._wait_ge(need[-1][0].h, need[-1][1])
        sem.cnt += 16
        ins.then_inc(sem.h, 16)
        for b in reads:
            b.r[sem] = sem.cnt
        for b in writes:
            b.w[sem] = sem.cnt
        return ins

    def load(self, tile, out_ap, in_ap, dsrc=(), **kw):
        return self.dma(self.sp, out_ap, in_ap, tile.ld, reads=dsrc, writes=[tile], **kw)

    def store(self, tile, out_ap, in_ap, ddst=(), **kw):
        return self.dma(self.pool, out_ap, in_ap, tile.st, reads=[tile], writes=ddst, **kw)

    def mm(self, ps, out_ap, lhsT, rhs, start, stop, reads, mark=None, **kw):
        if mark is None:
            mark = stop
        return self.op(self.pe, lambda: self.nc.tensor.matmul(out_ap, lhsT=lhsT, rhs=rhs, start=start, stop=stop, **kw),
                       reads=reads, writes=[ps], mark=mark)

    def finish(self):
        self.barrier()
        for g in reversed(self.phase_ctx):
            g.__exit__(None, None, None)
        for g in reversed(self._ctx):
            g.__exit__(None, None, None)


def interleave(gens):
    gens = list(gens)
    while gens:
        for g in list(gens):
            try:
                next(g)
            except StopIteration:
                gens.remove(g)


def splits(W):
    if W <= 512:
        return [(0, W)]
    h = (W + 1) // 2
    return [(0, h), (h, W)]


def build(seqs, nlayers=4):
    kb = KB(seqs, nlayers)
    nc = kb.nc
    T = kb.T
    NB = T // 128
    Lmax = max(seqs)
    pe, act, dve, pool, sp = kb.pe, kb.act, kb.dve, kb.pool, kb.sp
    V, A, P_ = nc.vector, nc.scalar, nc.gpsimd

    def ein(name, shape):
        return kb.dram(name, shape, F32, kind="ExternalInput")

    x_in = ein("x", [T, D])
    y_out = kb.dram("y", [T, D], F32, kind="ExternalOutput")
    w_in = {
        "norm_mix": ein("norm_mix", [4, D]), "norm_ffn": ein("norm_ffn", [4, D]),
        "ffn_w_up": ein("ffn_w_up", [4, D, 2 * DFF]), "ffn_conv_w": ein("ffn_conv_w", [4, 3, DFF]),
        "ffn_conv_b": ein("ffn_conv_b", [4, DFF]), "ffn_w_down": ein("ffn_w_down", [4, DFF, D]),
        "da_w_qkv": ein("da_w_qkv", [2, D, 3072]), "da_q_norm": ein("da_q_norm", [2, 64]),
        "da_k_norm": ein("da_k_norm", [2, 64]), "da_lambda_q1": ein("da_lambda_q1", [2, 64]),
        "da_lambda_k1": ein("da_lambda_k1", [2, 64]), "da_lambda_q2": ein("da_lambda_q2", [2, 64]),
        "da_lambda_k2": ein("da_lambda_k2", [2, 64]), "da_sub_norm": ein("da_sub_norm", [2, 128]),
        "da_w_o": ein("da_w_o", [2, D, D]), "wa_w_qkv": ein("wa_w_qkv", [1, D, 1536]),
        "wa_q_norm": ein("wa_q_norm", [1, 64]), "wa_k_norm": ein("wa_k_norm", [1, 64]),
        "wa_sink": ein("wa_sink", [1, 16]), "wa_w_o": ein("wa_w_o", [1, D, D]),
        "rg_w_in": ein("rg_w_in", [1, D, 2 * DRNN]), "rg_conv_w": ein("rg_conv_w", [1, 4, DRNN]),
        "rg_conv_b": ein("rg_conv_b", [1, DRNN]), "rg_gate_w": ein("rg_gate_w", [1, 2, 16, 96, 192]),
        "rg_gate_b": ein("rg_gate_b", [1, 2, 16, 192]), "rg_lambda": ein("rg_lambda", [1, 2, DRNN]),
        "rg_w_out": ein("rg_w_out", [1, DRNN, D]),
    }
    c_ident = ein("c_ident", [128, 128])
    c_rmat = ein("c_rmat", [128, 128])
    c_bones = ein("c_bones", [128, 128])
    c_mask = ein("c_mask", [128, 2, 512])
    c_cos = ein("c_cos", [128, Lmax])
    c_sin = ein("c_sin", [128, Lmax])

    XTS = [kb.dram("XT%d" % i, [8, 128, T + 4], F32) for i in range(2)]
    AOT = kb.dram("AOT", [16, 128, T + 4], BF16)
    QT = kb.dram("QT", [8, 128, T], BF16)
    KT = kb.dram("KT", [8, 128, T], BF16)
    VV = kb.dram("VV", [8 * 128 * NB * 128], BF16)
    HF = kb.dram("HF", [16, 96, T], F32)
    PADC = 2

    ntiles = T // NT
    xt_bs = [[Buf() for _ in range(ntiles)] for _ in range(2)]
    ao_b = [Buf() for _ in range(ntiles)]
    q_b = [Buf() for _ in range(ntiles)]
    k_b = [Buf() for _ in range(ntiles)]
    v_b = [Buf() for _ in range(ntiles)]
    hf_b = [Buf() for _ in range(ntiles)]
    y_b = [Buf() for _ in range(ntiles)]
    wdram_b = Buf()

    def seq_tiles(si):
        t0 = kb.offs[si] // NT
        return list(range(t0, t0 + kb.seqs[si] // NT))

    psg = []
    ps = []
    pspair = [None] * 4
    ps_bufs = [Buf() for _ in range(8)]
    ps_n = [0]
    for i in range(8):
        ps.append(None)

    def ps_alloc():
        for g in reversed(psg):
            g.__exit__(None, None, None)
        del psg[:]
        ps_n[0] += 1
        for i in range(4):
            g = nc.psum_tensor("pspair%d_%d" % (i, ps_n[0]), [128, 1024], F32)
            pspair[i] = g.__enter__()
            psg.append(g)
        for i in range(8):
            t = Tile(kb, pspair[i // 2], col0=(i % 2) * 512)
            t.buf = ps_bufs[i]
            ps[i] = t

    ps_alloc()

    def const_load(src, shape, dt=F32):
        t = kb.sb(shape, F32, "c", persistent=True)
        kb.load(t, t[:], src)
        if dt == F32:
            return t
        tb = kb.sb(shape, BF16, "cb", persistent=True)
        kb.op(dve, lambda: V.tensor_copy(out=tb[:], in_=t[:]), reads=[t], writes=[tb])
        return tb

    ident_f = const_load(c_ident, [128, 128])
    ident_b = const_load(c_ident, [128, 128], BF16)
    rmat_f = const_load(c_rmat, [128, 128])
    bones_b = const_load(c_bones, [128, 128], BF16)
    mask_b = const_load(c_mask.rearrange("p a b -> p (a b)"), [128, 1024], BF16)
    ones_b = kb.sb([128, 128], BF16, "ones", persistent=True)
    kb.op(dve, lambda: V.memset(ones_b[:], 1.0), writes=[ones_b])
    ones_f = kb.sb([128, 128], F32, "onesf", persistent=True)
    kb.op(dve, lambda: V.memset(ones_f[:], 1.0), writes=[ones_f])

    def small_load(src_ap, shape):
        t = kb.sb(shape, F32, "sp", persistent=True)
        kb.load(t, t[:], src_ap, allow_slow_non_contiguous=True)
        return t

    def load_T(src2d, R, C, csz=128, persistent=True, o=None):
        if o is None:
            o = kb.sb([csz, C, R], F32, "lt")
        stg_ = kb.sb([R, C * csz], F32, "ltstg")
        kb.load(stg_, stg_[:], src2d)
        for c in range(C):
            pt = ps[c % 8]
            kb.op(pe, lambda: nc.tensor.transpose(out=pt[0:csz, 0:R], in_=stg_[0:R, c * csz:(c + 1) * csz],
                                                  identity=ident_f[0:R, 0:R]), reads=[stg_, ident_f], writes=[pt])
            kb.op(dve, lambda: V.tensor_copy(out=o[:, c, :], in_=pt[0:csz, 0:R]), reads=[pt], writes=[o])
        return o

    fcw = kb.sb([128, 22, 12], F32, "fcw", persistent=True)
    fcb = kb.sb([128, 22, 4], F32, "fcb", persistent=True)
    load_T(w_in["ffn_conv_w"].rearrange("l k f -> (l k) f"), 12, 22, o=fcw)
    load_T(w_in["ffn_conv_b"], 4, 22, o=fcb)
    gmix = load_T(w_in["norm_mix"], 4, 8)
    gffn = load_T(w_in["norm_ffn"], 4, 8)

    wbf = {}
    pcnt = [0]

    def prep_weight(name, src, K, N, ksz, msz, gain=None):
        KC, MC = K // ksz, N // msz
        dst = kb.dram("wb_" + name, [MC, ksz, KC, msz], BF16)
        wbf[name] = (dst, KC, MC, ksz, msz)
        CW = 2816 if N > 3072 else N
        for kc in range(KC):
            for c0 in range(0, N, CW):
                cw = min(CW, N - c0)
                stg = stage[pcnt[0] % 2]
                stb = stageb[pcnt[0] % 2]
                pcnt[0] += 1
                kb.load(stg, stg[0:ksz, 0:cw], src[kc * ksz:(kc + 1) * ksz, c0:c0 + cw])
                if gain is not None:
                    gap = gain(kc)
                    kb.op(dve, lambda: V.tensor_scalar(out=stb[0:ksz, 0:cw], in0=stg[0:ksz, 0:cw], scalar1=gap,
                                                       scalar2=None, op0=ALU.mult),
                          reads=[stg], writes=[stb])
                else:
                    kb.op(act, lambda: A.copy(out=stb[0:ksz, 0:cw], in_=stg[0:ksz, 0:cw]), reads=[stg], writes=[stb])
                m0, m1 = c0 // msz, (c0 + cw) // msz
                kb.store(stb, dst[m0:m1, :, kc, :].rearrange("m p i -> p m i"),
                         stb[0:ksz, 0:cw].rearrange("p (m i) -> p m i", i=msz), ddst=[wdram_b])

    stage = [kb.sb([128, 3072], F32, "stg") for _ in range(2)]
    stageb = [kb.sb([128, 3072], BF16, "stgb") for _ in range(2)]
    layer_kind = [i % 3 for i in range(4)]
    for li in range(nlayers):
        kind, j = li % 3, li // 3
        g_of = (lambda li: (lambda kc: gmix[:, kc, li:li + 1]))(li)
        if kind == 0:
            prep_weight("mix_in%d" % li, w_in["da_w_qkv"][j], D, 3072, 128, 128, g_of)
            prep_weight("mix_out%d" % li, w_in["da_w_o"][j], D, D, 128, 128)
        elif kind == 1:
            prep_weight("mix_in%d" % li, w_in["wa_w_qkv"][j], D, 1536, 128, 128, g_of)
            prep_weight("mix_out%d" % li, w_in["wa_w_o"][j], D, D, 128, 128)
        else:
            prep_weight("mix_in%d" % li, w_in["rg_w_in"][j], D, 3072, 128, 96, g_of)
            prep_weight("mix_out%d" % li, w_in["rg_w_out"][j], DRNN, D, 96, 128)
        gf_of = (lambda li: (lambda kc: gffn[:, kc, li:li + 1]))(li)
        prep_weight("up%d" % li, w_in["ffn_w_up"][li], D, 2 * DFF, 128, 128, gf_of)
        prep_weight("down%d" % li, w_in["ffn_w_down"][li], DFF, D, 128, 128)
    kb.end_phase()

    WSLOT = 4096
    NWS = 4

    class WStream:
        def __init__(self):
            self.slots = [kb.sb([128, WSLOT], BF16, "wslot") for _ in range(NWS)]
            self.i = 0

        def get(self, name, m0, nm):
            dst, KC, MC, ksz, msz = wbf[name]
            assert nm * KC * msz <= WSLOT
            s = self.slots[self.i % NWS]
            self.i += 1
            view = s[0:ksz, 0:nm * KC * msz].rearrange("p (m k i) -> p m k i", m=nm, k=KC)
            kb.load(s, view, dst[m0:m0 + nm].rearrange("m p k i -> p m k i"), dsrc=[wdram_b])
            return s, view

    def rmsnorm_fm(xt, ht, W, sq, rinv, psb, KCH=8):
        sp_ = splits(W)
        for c in range(KCH):
            s = sq[c % 2]
            kb.op(act, lambda: A.activation(out=s[:, 0:W], in_=xt[:, c, 0:W], func=AF.Square, scale=1.0 / 32.0),
                  reads=[xt], writes=[s])
            for bi, (a, b) in enumerate(sp_):
                kb.mm(psb[bi], psb[bi][:, 0:b - a], ones_b[:], s[:, a:b], c == 0, c == KCH - 1, reads=[ones_b, s],
                      mark=True)
        for bi, (a, b) in enumerate(sp_):
            kb.op(dve, lambda: V.tensor_scalar(out=rinv[:, a:b], in0=psb[bi][:, 0:b - a], scalar1=EPS, scalar2=None,
                                               op0=ALU.add), reads=[psb[bi]], writes=[rinv])
        kb.op(act, lambda: A.activation(out=rinv[:, 0:W], in_=rinv[:, 0:W], func=AF.Sqrt), reads=[rinv], writes=[rinv])
        kb.op(dve, lambda: V.reciprocal(out=rinv[:, 0:W], in_=rinv[:, 0:W]), reads=[rinv], writes=[rinv])
        for c in range(KCH):
            E, EE = (dve, V) if c % 2 == 0 else (pool, P_)
            kb.op(E, lambda: EE.tensor_tensor(out=ht[:, c, 0:W], in0=xt[:, c, 0:W], in1=rinv[:, 0:W], op=ALU.mult),
                  reads=[xt, rinv], writes=[ht])

    def phase_transpose_in():
        XT, xt_b = XTS[0], xt_bs[0]
        xin = [kb.sb([128, D], F32, "xin") for _ in range(3)]
        stg_ = [kb.sb([128, 8, NT], F32, "tstg") for _ in range(2)]
        for ti in range(ntiles):
            st_ = stg_[ti % 2]
            for sbk in range(4):
                blk = ti * 4 + sbk
                xi = xin[blk % 3]
                kb.load(xi, xi[:], x_in[blk * 128:(blk + 1) * 128, :])
                for half in range(2):
                    pt = ps[(blk * 2 + half) % 8]
                    for c4 in range(4):
                        c = half * 4 + c4
                        kb.op(pe, lambda: nc.tensor.transpose(out=pt[:, c4 * 128:(c4 + 1) * 128],
                                                              in_=xi[:, c * 128:(c + 1) * 128], identity=ident_f[:]),
                              reads=[xi, ident_f], writes=[pt], mark=(c4 == 3))
                    dstv = st_[:, half * 4:(half + 1) * 4, sbk * 128:(sbk + 1) * 128]
                    srcv = pt[:, :].rearrange("p (c t) -> p c t", c=4)
                    if half == 0:
                        kb.op(act, lambda: A.copy(out=dstv, in_=srcv), reads=[pt], writes=[st_])
                    else:
                        kb.op(dve, lambda: V.tensor_copy(out=dstv, in_=srcv), reads=[pt], writes=[st_])
            kb.store(st_, XT[:, :, PADC + ti * NT:PADC + (ti + 1) * NT].rearrange("c p t -> p c t"), st_[:],
                     ddst=[xt_b[ti]])
        kb.end_phase()

    def phase_transpose_out():
        XT, xt_b = XTS[nlayers % 2], xt_bs[nlayers % 2]
        xl = [kb.sb([128, 8, NT], F32, "xl") for _ in range(2)]
        yo = [kb.sb([128, D], F32, "yo") for _ in range(3)]
        for ti in range(ntiles):
            xt = xl[ti % 2]
            kb.load(xt, xt[:], XT[:, :, PADC + ti * NT:PADC + (ti + 1) * NT].rearrange("c p t -> p c t"),
                    dsrc=[xt_b[ti]])
            for sbk in range(4):
                blk = ti * 4 + sbk
                yt = yo[blk % 3]
                for half in range(2):
                    pt = ps[(blk * 2 + half) % 8]
                    for c4 in range(4):
                        c = half * 4 + c4
                        kb.op(pe, lambda: nc.tensor.transpose(out=pt[:, c4 * 128:(c4 + 1) * 128],
                                                              in_=xt[:, c, sbk * 128:(sbk + 1) * 128],
                                                              identity=ident_f[:]),
                              reads=[xt, ident_f], writes=[pt], mark=(c4 == 3))
                    if half == 0:
                        kb.op(act, lambda: A.copy(out=yt[:, 0:512], in_=pt[:, :]), reads=[pt], writes=[yt])
                    else:
                        kb.op(dve, lambda: V.tensor_copy(out=yt[:, 512:1024], in_=pt[:, :]), reads=[pt], writes=[yt])
                kb.store(yt, y_out[blk * 128:(blk + 1) * 128, :], yt[:], ddst=[y_b[ti]])
        kb.end_phase()

    def phase_ffn(li, aksz, akc):
        XT, xt_b = XTS[li % 2], xt_bs[li % 2]
        XTO, xto_b = XTS[(li + 1) % 2], xt_bs[(li + 1) % 2]
        W = NT + 2
        sp_ = splits(W)
        pi = [0]

        def nextps():
            p = ps[pi[0] % 8]
            pi[0] += 1
            return p

        for si in range(len(kb.seqs)):
            tl = seq_tiles(si)
            for ti in tl:
                kb.begin_scope("ffn")
                ps_alloc()
                ws = WStream()
                xt_s = [kb.sb([128, 8, W], F32, "xt") for _ in range(2)]
                ao_s = [kb.sb([128, akc, W], BF16, "ao") for _ in range(2)]
                ht = kb.sb([128, 8, W], BF16, "ht")
                sq = [kb.sb([128, W], BF16, "sq") for _ in range(2)]
                rinv = kb.sb([128, W], F32, "rinv")
                actT = kb.sb([128, 22, NT], BF16, "actT")
                gev = [kb.sb([128, W], F32, "gev") for _ in range(2)]
                cv1 = [kb.sb([128, NT], F32, "cv1") for _ in range(2)]
                cv2 = [kb.sb([128, NT], F32, "cv2") for _ in range(2)]
                xt = xt_s[ti % 2]
                ao = ao_s[ti % 2]
                c0 = PADC + ti * NT - 1
                nb = [xt_b[t] for t in (ti - 1, ti, ti + 1) if 0 <= t < ntiles]
                nab = [ao_b[t] for t in (ti - 1, ti, ti + 1) if 0 <= t < ntiles]
                kb.load(xt, xt[:], XT[:, :, c0:c0 + W].rearrange("c p t -> p c t"), dsrc=nb)
                kb.load(ao, ao[0:aksz], AOT[0:akc, 0:aksz, c0:c0 + W].rearrange("c p t -> p c t"), dsrc=nab)
                for mc in range(8):
                    if mc % 2 == 0:
                        wsl, wv = ws.get("mix_out%d" % li, mc, 2)
                    pp = [nextps() for _ in sp_]
                    for kc in range(akc):
                        for bi, (a, b) in enumerate(sp_):
                            kb.mm(pp[bi], pp[bi][:, 0:b - a], wv[0:aksz, mc % 2, kc, :], ao[0:aksz, kc, a:b], kc == 0,
                                  kc == akc - 1, reads=[wsl, ao])
                    for bi, (a, b) in enumerate(sp_):
                        kb.op(dve, lambda: V.tensor_tensor(out=xt[:, mc, a:b], in0=xt[:, mc, a:b],
                                                           in1=pp[bi][:, 0:b - a], op=ALU.add),
                              reads=[pp[bi], xt], writes=[xt])
                pp = [nextps() for _ in sp_]
                rmsnorm_fm(xt, ht, W, sq, rinv, pp)
                if ti == tl[0]:
                    kb.op(pool, lambda: P_.memset(ht[:, :, 0:1], 0.0), writes=[ht])
                if ti == tl[-1]:
                    kb.op(pool, lambda: P_.memset(ht[:, :, W - 1:W], 0.0), writes=[ht])
                def fc_chunk(fc, wsg, wvg, wsu, wvu):
                    pg = [nextps() for _ in sp_]
                    for kc in range(8):
                        for bi, (a, b) in enumerate(sp_):
                            kb.mm(pg[bi], pg[bi][:, 0:b - a], wvg[:, fc % 2, kc, :], ht[:, kc, a:b], kc == 0, kc == 7,
                                  reads=[wsg, ht])
                    pu = nextps()
                    for kc in range(8):
                        kb.mm(pu, pu[:, 0:NT], wvu[:, fc % 2, kc, :], ht[:, kc, 1:1 + NT], kc == 0, kc == 7,
                              reads=[wsu, ht])
                    yield
                    ge = gev[fc % 2]
                    for bi, (a, b) in enumerate(sp_):
                        kb.op(act, lambda: A.copy(out=ge[:, a:b], in_=pg[bi][:, 0:b - a]), reads=[pg[bi]], writes=[ge])
                    yield
                    t1 = cv1[fc % 2]
                    t2 = cv2[fc % 2]
                    kb.op(pool, lambda: P_.tensor_scalar(out=t1[:], in0=ge[:, 0:NT], scalar1=fcw[:, fc, li * 3:li * 3 + 1],
                                                         scalar2=fcb[:, fc, li:li + 1], op0=ALU.mult, op1=ALU.add),
                          reads=[ge, fcw, fcb], writes=[t1])
                    yield
                    kb.op(dve, lambda: V.scalar_tensor_tensor(out=t2[:], in0=ge[:, 1:1 + NT],
                                                              scalar=fcw[:, fc, li * 3 + 1:li * 3 + 2], in1=t1[:],
                                                              op0=ALU.mult, op1=ALU.add),
                          reads=[ge, fcw, t1], writes=[t2])
                    kb.op(dve, lambda: V.scalar_tensor_tensor(out=t1[:], in0=ge[:, 2:2 + NT],
                                                              scalar=fcw[:, fc, li * 3 + 2:li * 3 + 3], in1=t2[:],
                                                              op0=ALU.mult, op1=ALU.add),
                          reads=[ge, fcw, t2], writes=[t1])
                    yield
                    kb.op(act, lambda: A.activation(out=t2[:], in_=t1[:], func=AF.Silu), reads=[t1], writes=[t2])
                    yield
                    kb.op(dve, lambda: V.tensor_tensor(out=actT[:, fc, :], in0=t2[:], in1=pu[:, 0:NT], op=ALU.mult),
                          reads=[t2, pu], writes=[actT])

                for fc in range(0, 22, 2):
                    wsg, wvg = ws.get("up%d" % li, fc, 2)
                    wsu, wvu = ws.get("up%d" % li, 22 + fc, 2)
                    interleave([fc_chunk(fc, wsg, wvg, wsu, wvu), fc_chunk(fc + 1, wsg, wvg, wsu, wvu)])
                for mc in range(8):
                    wsl, wv = ws.get("down%d" % li, mc, 1)
                    pd = nextps()
                    for kc in range(22):
                        kb.mm(pd, pd[:, 0:NT], wv[:, 0, kc, :], actT[:, kc, :], kc == 0, kc == 21, reads=[wsl, actT])
                    kb.op(dve, lambda: V.tensor_tensor(out=xt[:, mc, 1:1 + NT], in0=xt[:, mc, 1:1 + NT], in1=pd[:, 0:NT],
                                                       op=ALU.add), reads=[pd, xt], writes=[xt])
                kb.store(xt, XTO[:, :, PADC + ti * NT:PADC + (ti + 1) * NT].rearrange("c p t -> p c t"),
                         xt[:, :, 1:1 + NT], ddst=[xto_b[ti]])
                kb.end_scope()
        kb.end_phase()

    def qk_epilogue(pq, gap, cos_t, sin_t, outv, tmp):
        sqb, t_s, qn, t1, t2, _ob = tmp
        psq = pq[1]
        prq = pq[2]
        p0 = pq[0]
        kb.op(act, lambda: A.activation(out=sqb[:], in_=p0[:, 0:NT], func=AF.Square, scale=0.125),
              reads=[p0], writes=[sqb])
        yield
        kb.mm(psq, psq[:, 0:NT], bones_b[:], sqb[:], True, True, reads=[bones_b, sqb])
        yield
        kb.op(dve, lambda: V.tensor_scalar(out=t_s[:], in0=psq[:, 0:NT], scalar1=EPS, scalar2=None, op0=ALU.add),
              reads=[psq], writes=[t_s])
        yield
        kb.op(act, lambda: A.activation(out=t_s[:], in_=t_s[:], func=AF.Sqrt), reads=[t_s], writes=[t_s])
        yield
        kb.op(dve, lambda: V.reciprocal(out=t_s[:], in_=t_s[:]), reads=[t_s], writes=[t_s])
        kb.op(dve, lambda: V.scalar_tensor_tensor(out=qn[:], in0=p0[:, 0:NT], scalar=gap, in1=t_s[:], op0=ALU.mult,
                                                  op1=ALU.mult), reads=[p0, t_s], writes=[qn])
        yield
        kb.mm(prq, prq[:, 0:NT], rmat_f[:], qn[:], True, True, reads=[rmat_f, qn])
        kb.op(pool, lambda: P_.tensor_tensor(out=t1[:], in0=qn[:], in1=cos_t[:], op=ALU.mult),
              reads=[qn, cos_t], writes=[t1])
        yield
        kb.op(dve, lambda: V.tensor_tensor(out=t2[:], in0=prq[:, 0:NT], in1=sin_t[:], op=ALU.mult),
              reads=[prq, sin_t], writes=[t2])
        yield
        kb.op(pool, lambda: P_.tensor_tensor(out=outv, in0=t1[:], in1=t2[:], op=ALU.add),
              reads=[t1, t2], writes=[tmp[5]])

    def phase_qkv(li, kind, j):
        XT, xt_b = XTS[li % 2], xt_bs[li % 2]
        nq = 8
        nk = 8 if kind == 0 else 2
        nvcols = 1024 if kind == 0 else 256
        qn_src = w_in["da_q_norm"] if kind == 0 else w_in["wa_q_norm"]
        kn_src = w_in["da_k_norm"] if kind == 0 else w_in["wa_k_norm"]
        gq = kb.sb([128, 1], F32, "gq")
        gk = kb.sb([128, 1], F32, "gk")
        for half in range(2):
            kb.load(gq, gq[half * 64:(half + 1) * 64, :], qn_src[j:j + 1, :].rearrange("a d -> d a"),
                    allow_slow_non_contiguous=True)
            kb.load(gk, gk[half * 64:(half + 1) * 64, :], kn_src[j:j + 1, :].rearrange("a d -> d a"),
                    allow_slow_non_contiguous=True)
        pi = [0]

        def nextps():
            p = ps[pi[0] % 8]
            pi[0] += 1
            return p

        wname = "mix_in%d" % li
        for si in range(len(kb.seqs)):
            for n_, ti in enumerate(seq_tiles(si)):
                kb.begin_scope("qkv")
                ps_alloc()
                ws = WStream()
                xt_s = [kb.sb([128, 8, NT], F32, "xt") for _ in range(2)]
                ht = kb.sb([128, 8, NT], BF16, "ht")
                sq = [kb.sb([128, NT], BF16, "sq") for _ in range(2)]
                rinv = kb.sb([128, NT], F32, "rinv")
                cos_s = [kb.sb([128, NT], F32, "cos") for _ in range(2)]
                sin_s = [kb.sb([128, NT], F32, "sin") for _ in range(2)]
                sqb = [kb.sb([128, NT], BF16, "sqb") for _ in range(2)]
                t_s = [kb.sb([128, NT], F32, "ts") for _ in range(2)]
                qn = [kb.sb([128, NT], F32, "qn") for _ in range(2)]
                t1 = [kb.sb([128, NT], F32, "t1") for _ in range(2)]
                t2 = [kb.sb([128, NT], F32, "t2") for _ in range(2)]
                qo_s = [kb.sb([128, nq, NT], BF16, "qo") for _ in range(2)]
                ko_s = [kb.sb([128, nk, NT], BF16, "ko") for _ in range(2)]
                vo_s = [kb.sb([128, 4, nvcols], BF16, "vo") for _ in range(2)]
                xt = xt_s[ti % 2]
                qo, ko, vo = qo_s[ti % 2], ko_s[ti % 2], vo_s[ti % 2]
                cs, sn = cos_s[ti % 2], sin_s[ti % 2]
                kb.load(xt, xt[:], XT[:, :, PADC + ti * NT:PADC + (ti + 1) * NT].rearrange("c p t -> p c t"),
                        dsrc=[xt_b[ti]])
                kb.load(cs, cs[:], c_cos[:, n_ * NT:(n_ + 1) * NT])
                kb.load(sn, sn[:], c_sin[:, n_ * NT:(n_ + 1) * NT])
                rmsnorm_fm(xt, ht, NT, sq, rinv, [nextps()])
                def qk_chunk(mc, wsl, wv):
                    pq = [nextps(), nextps(), nextps()]
                    for kc in range(8):
                        kb.mm(pq[0], pq[0][:, 0:NT], wv[:, mc % 2, kc, :], ht[:, kc, :], kc == 0, kc == 7,
                              reads=[wsl, ht])
                    yield
                    isq = mc < nq
                    outv = qo[:, mc, :] if isq else ko[:, mc - nq, :]
                    tmp = (sqb[mc % 2], t_s[mc % 2], qn[mc % 2], t1[mc % 2], t2[mc % 2], qo if isq else ko)
                    yield from qk_epilogue(pq, (gq if isq else gk)[:, 0:1], cs, sn, outv, tmp)

                for mc in range(0, nq + nk, 2):
                    wsl, wv = ws.get(wname, mc, 2)
                    interleave([qk_chunk(mc, wsl, wv), qk_chunk(mc + 1, wsl, wv)])
                vm0 = nq + nk
                nvg = nvcols // 256
                for vg in range(nvg):
                    wsl, wv = ws.get(wname, vm0 + vg * 2, 2)
                    for sbk in range(4):
                        pv = nextps()
                        for kc in range(8):
                            kb.mm(pv, pv[:, 0:256], ht[:, kc, sbk * 128:(sbk + 1) * 128], wv[:, :, kc, :], kc == 0,
                                  kc == 7, reads=[wsl, ht])
                        if sbk % 2 == 0:
                            kb.op(act, lambda: A.copy(out=vo[:, sbk, vg * 256:(vg + 1) * 256], in_=pv[:, 0:256]),
                                  reads=[pv], writes=[vo])
                        else:
                            kb.op(dve, lambda: V.tensor_copy(out=vo[:, sbk, vg * 256:(vg + 1) * 256], in_=pv[:, 0:256]),
                                  reads=[pv], writes=[vo])
                kb.store(qo, QT[0:nq, :, ti * NT:(ti + 1) * NT].rearrange("c p t -> p c t"), qo[:], ddst=[q_b[ti]])
                kb.store(ko, KT[0:nk, :, ti * NT:(ti + 1) * NT].rearrange("c p t -> p c t"), ko[:], ddst=[k_b[ti]])
                nh = nvcols // (128 if kind == 0 else 64)
                dv = nvcols // nh
                vview = VV[0:nh * 128 * NB * dv].rearrange("(h p b d) -> p h b d", h=nh, p=128, b=NB)
                for sbk in range(4):
                    kb.store(vo, vview[:, :, ti * 4 + sbk, :], vo[:, sbk, :].rearrange("p (h d) -> p h d", h=nh),
                             ddst=[v_b[ti]])
                kb.end_scope()
        kb.end_phase()

    def phase_da_attn(li, j):
        lam_init = 0.8 - 0.6 * math.exp(-0.3 * li)
        Lm = max(kb.seqs)
        lv = kb.sb([128, 4, 64], F32, "lv")
        for i_, nm in enumerate(["da_lambda_q1", "da_lambda_k1", "da_lambda_q2", "da_lambda_k2"]):
            kb.load(lv, lv[:, i_, :], w_in[nm][j:j + 1, :].partition_broadcast(128))
        lp = kb.sb([128, 2, 64], F32, "lp")
        ls = kb.sb([128, 4], F32, "ls")
        kb.op(dve, lambda: V.tensor_tensor(out=lp[:, 0, :], in0=lv[:, 0, :], in1=lv[:, 1, :], op=ALU.mult),
              reads=[lv], writes=[lp])
        kb.op(dve, lambda: V.tensor_tensor(out=lp[:, 1, :], in0=lv[:, 2, :], in1=lv[:, 3, :], op=ALU.mult),
              reads=[lv], writes=[lp])
        kb.op(dve, lambda: V.reduce_sum(out=ls[:, 0:2], in_=lp[:], axis=mybir.AxisListType.X), reads=[lp], writes=[ls])
        kb.op(act, lambda: A.activation(out=ls[:, 0:2], in_=ls[:, 0:2], func=AF.Exp), reads=[ls], writes=[ls])
        kb.op(dve, lambda: V.tensor_tensor(out=ls[:, 2:3], in0=ls[:, 1:2], in1=ls[:, 0:1], op=ALU.subtract),
              reads=[ls], writes=[ls])
        kb.op(dve, lambda: V.tensor_scalar(out=ls[:, 3:4], in0=ls[:, 2:3], scalar1=-lam_init, scalar2=None,
                                           op0=ALU.add), reads=[ls], writes=[ls])
        neg_lam = ls[:, 3:4]
        sg = kb.sb([128, 2], F32, "sg")
        kb.load(sg, sg[:, 0:1], w_in["da_sub_norm"][j:j + 1, :].rearrange("a d -> d a"), allow_slow_non_contiguous=True)
        kb.op(dve, lambda: V.tensor_scalar(out=sg[:, 1:2], in0=sg[:, 0:1], scalar1=(1.0 - lam_init), scalar2=None,
                                           op0=ALU.mult), reads=[sg], writes=[sg])
        it = 0
        qi = 0
        for si in range(len(kb.seqs)):
            L = kb.seqs[si]
            off = kb.offs[si]
            tl = seq_tiles(si)
            nkt = L // 128
            for h in range(8):
                vview = VV[0:8 * 128 * NB * 128].rearrange("(h p b d) -> p h b d", h=8, p=128, b=NB)
                for ti in tl:
                    kb.begin_scope("da")
                    ps_alloc()
                    kt_t = kb.sb([128, Lm], BF16, "ktt")
                    v_t = kb.sb([128, Lm // 128, 128], BF16, "vtt")
                    q_s = [kb.sb([128, NT], BF16, "qs") for _ in range(2)]
                    p_s = [kb.sb([128, 2, NT], BF16, "pts") for _ in range(3)]
                    r0 = kb.sb([128, NT], F32, "r0")
                    r1 = kb.sb([128, NT], F32, "r1")
                    o_t = kb.sb([128, NT], F32, "ot")
                    sqb = kb.sb([128, NT], BF16, "sqb")
                    rs = kb.sb([128, NT], F32, "rs")
                    on_s = [kb.sb([128, NT], BF16, "on") for _ in range(2)]
                    acc = [kb.sb([128, NT], F32, "acc") for _ in range(2)]
                    S_ps = [[ps[0], ps[1]], [ps[2], ps[3]]]
                    O_ps = [ps[4], ps[5]]
                    L_ps = [ps[6], ps[7]]
                    if ti == tl[0]:
                        kb.load(kt_t, kt_t[:, 0:L], KT[h, :, off:off + L], dsrc=[k_b[t] for t in tl])
                        kb.load(v_t, v_t[:, 0:nkt, :], vview[:, h, off // 128:off // 128 + nkt, :],
                                dsrc=[v_b[t] for t in tl])
                    qt = q_s[qi % 2]
                    on = on_s[qi % 2]
                    qi += 1
                    kb.load(qt, qt[:], QT[h, :, ti * NT:(ti + 1) * NT], dsrc=[q_b[ti]])
                    it0 = it
                    for kk in range(nkt + 1):
                        if kk < nkt:
                            kt = kk
                            pr_ = (it0 + kt) % 2
                            sp2 = S_ps[pr_]
                            pt = p_s[(it0 + kt) % 3]
                            for m in range(2):
                                kb.mm(sp2[m], sp2[m][:, 0:NT], kt_t[m * 64:(m + 1) * 64, kt * 128:(kt + 1) * 128],
                                      qt[m * 64:(m + 1) * 64, :], True, True, reads=[kt_t, qt])
                            kb.op(act, lambda: A.activation(out=pt[:].rearrange("p a b -> p (a b)"),
                                                            in_=pspair[pr_][:, 0:1024], func=AF.Exp,
                                                            scale=0.125), reads=[sp2[0], sp2[1]], writes=[pt])
                        if kk >= 1:
                            kt = kk - 1
                            pt = p_s[(it0 + kt) % 3]
                            last = kt == nkt - 1
                            for m in range(2):
                                kb.mm(O_ps[m], O_ps[m][:, 0:NT], v_t[:, kt, :], pt[:, m, :], kt == 0, last,
                                      reads=[v_t, pt], mark=True)
                            kb.mm(L_ps[1], L_ps[1][:, 0:NT], ones_b[:], pt[:, 1, :], kt == 0, last,
                                  reads=[ones_b, pt], mark=True)
                            ac = acc[kt % 2]
                            if kt < 2:
                                kb.op(dve, lambda: V.tensor_copy(out=ac[:], in_=pt[:, 0, :]), reads=[pt], writes=[ac],
                                      no_self=True)
                            else:
                                kb.op(dve, lambda: V.tensor_tensor(out=ac[:], in0=ac[:], in1=pt[:, 0, :], op=ALU.add),
                                      reads=[pt, ac], writes=[ac], no_self=True)
                    it = it0 + nkt
                    kb.op(dve, lambda: V.tensor_tensor(out=acc[0][:], in0=acc[0][:], in1=acc[1][:], op=ALU.add),
                          reads=[acc[0], acc[1]], writes=[acc[0]])
                    kb.mm(L_ps[0], L_ps[0][:, 0:NT], ones_f[:], acc[0][:], True, True,
                          reads=[ones_f, acc[0]], mark=True)
                    kb.op(dve, lambda: V.reciprocal(out=r0[:], in_=L_ps[0][:, 0:NT]), reads=[L_ps[0]], writes=[r0])
                    kb.op(dve, lambda: V.reciprocal(out=r1[:], in_=L_ps[1][:, 0:NT]), reads=[L_ps[1]], writes=[r1])
                    kb.op(dve, lambda: V.tensor_tensor(out=r0[:], in0=O_ps[0][:, 0:NT], in1=r0[:], op=ALU.mult),
                          reads=[O_ps[0], r0], writes=[r0])
                    kb.op(dve, lambda: V.tensor_tensor(out=r1[:], in0=O_ps[1][:, 0:NT], in1=r1[:], op=ALU.mult),
                          reads=[O_ps[1], r1], writes=[r1])
                    kb.op(dve, lambda: V.scalar_tensor_tensor(out=o_t[:], in0=r1[:], scalar=neg_lam, in1=r0[:],
                                                              op0=ALU.mult, op1=ALU.add),
                          reads=[r0, r1, ls], writes=[o_t])
                    kb.op(act, lambda: A.activation(out=sqb[:], in_=o_t[:], func=AF.Square,
                                                    scale=1.0 / math.sqrt(128.0)), reads=[o_t], writes=[sqb])
                    pss = S_ps[it % 2][0]
                    kb.mm(pss, pss[:, 0:NT], ones_b[:], sqb[:], True, True, reads=[ones_b, sqb])
                    kb.op(dve, lambda: V.tensor_scalar(out=rs[:], in0=pss[:, 0:NT], scalar1=EPS, scalar2=None,
                                                       op0=ALU.add), reads=[pss], writes=[rs])
                    kb.op(act, lambda: A.activation(out=rs[:], in_=rs[:], func=AF.Sqrt), reads=[rs], writes=[rs])
                    kb.op(dve, lambda: V.reciprocal(out=rs[:], in_=rs[:]), reads=[rs], writes=[rs])
                    kb.op(dve, lambda: V.scalar_tensor_tensor(out=on[:], in0=o_t[:], scalar=sg[:, 1:2], in1=rs[:],
                                                              op0=ALU.mult, op1=ALU.mult),
                          reads=[o_t, sg, rs], writes=[on])
                    kb.store(on, AOT[h, :, PADC + ti * NT:PADC + (ti + 1) * NT], on[:], ddst=[ao_b[ti]])
                    kb.end_scope()
        kb.end_phase()

    def phase_wa_attn(li, j):
        Lm = max(kb.seqs)
        sk = kb.sb([64, 16], F32, "sk")
        kb.load(sk, sk[:], w_in["wa_sink"][j:j + 1, :].partition_broadcast(64))
        kb.op(act, lambda: A.activation(out=sk[:], in_=sk[:], func=AF.Exp), reads=[sk], writes=[sk])
        es = kb.sb([64, 16, 128], F32, "es")
        kb.op(dve, lambda: V.memset(es[:], 0.0), writes=[es])
        for hq in range(16):
            kb.op(dve, lambda: V.tensor_scalar(out=es[:, hq, :], in0=es[:, hq, :], scalar1=sk[:, hq:hq + 1],
                                               scalar2=None, op0=ALU.add), reads=[sk, es], writes=[es])
        it = 0
        qi = 0
        vview = VV[0:4 * 128 * NB * 64].rearrange("(h p b d) -> p h b d", h=4, p=128, b=NB)
        for si in range(len(kb.seqs)):
            L = kb.seqs[si]
            off = kb.offs[si]
            tl = seq_tiles(si)
            nkb = L // 128
            for g in range(4):
                for n_, ti in enumerate(tl):
                    kb.begin_scope("wa")
                    ps_alloc()
                    kt_t = kb.sb([64, Lm], BF16, "ktt")
                    v_t = kb.sb([128, Lm // 128, 64], BF16, "vtt")
                    q_s = [kb.sb([64, 4, NT], BF16, "qs") for _ in range(2)]
                    p_s = [kb.sb([128, NT], BF16, "pts") for _ in range(4)]
                    lt = kb.sb([64, NT], F32, "lt")
                    o_s = [kb.sb([64, 4, NT], BF16, "os") for _ in range(2)]
                    if n_ == 0:
                        kb.load(kt_t, kt_t[:, 0:L], KT[g // 2, (g % 2) * 64:(g % 2) * 64 + 64, off:off + L],
                                dsrc=[k_b[t] for t in tl])
                        kb.load(v_t, v_t[:, 0:nkb, :], vview[:, g, off // 128:off // 128 + nkb, :],
                                dsrc=[v_b[t] for t in tl])
                    qt = q_s[qi % 2]
                    ot = o_s[qi % 2]
                    qi += 1
                    for c2 in range(2):
                        kb.load(qt, qt[:, c2 * 2:c2 * 2 + 2, :],
                                QT[g * 2 + c2, :, ti * NT:(ti + 1) * NT].rearrange("(a d) t -> d a t", a=2),
                                dsrc=[q_b[ti]])
                    for qb in range(4):
                        jb = n_ * 4 + qb
                        kbs = [b for b in (jb - 1, jb, jb + 1) if 0 <= b < nkb]
                        Ops = ps[4 + (it % 2)]
                        Lps = ps[6 + (it % 2)]
                        qrhs = qt[:, :, qb * 128:(qb + 1) * 128]
                        for n2, kblk in enumerate(kbs):
                            sp_ = ps[it % 4]
                            pt = p_s[it % 4]
                            it += 1
                            rel = kblk - jb
                            kb.mm(sp_, sp_[0:128, 0:NT], kt_t[:, kblk * 128:(kblk + 1) * 128], qrhs, True, rel == 0,
                                  reads=[kt_t, qt], mark=True)
                            if rel != 0:
                                mi = 0 if rel == 1 else 1
                                kb.mm(sp_, sp_[0:128, 0:NT], ident_b[:], mask_b[:, mi * 512:(mi + 1) * 512], False, True,
                                      reads=[ident_b, mask_b], mark=True)
                            kb.op(act, lambda: A.activation(out=pt[:], in_=sp_[:, 0:NT], func=AF.Exp, scale=0.125),
                                  reads=[sp_], writes=[pt])
                            first, last = n2 == 0, n2 == len(kbs) - 1
                            kb.mm(Ops, Ops[0:64, 0:NT], v_t[:, kblk, :], pt[:], first, last, reads=[v_t, pt], mark=True)
                            kb.mm(Lps, Lps[0:64, 0:NT], ones_b[:, 0:64], pt[:], first, last, reads=[ones_b, pt],
                                  mark=True)
                        kb.op(dve, lambda: V.tensor_tensor(out=lt[:], in0=Lps[0:64, 0:NT],
                                                           in1=es[:, g * 4:(g + 1) * 4, :].rearrange("p a b -> p (a b)"),
                                                           op=ALU.add), reads=[Lps, es], writes=[lt])
                        kb.op(dve, lambda: V.reciprocal(out=lt[:], in_=lt[:]), reads=[lt], writes=[lt])
                        kb.op(dve, lambda: V.tensor_tensor(out=ot[:, :, qb * 128:(qb + 1) * 128],
                                                           in0=Ops[0:64, 0:NT].rearrange("p (a b) -> p a b", a=4),
                                                           in1=lt[:].rearrange("p (a b) -> p a b", a=4), op=ALU.mult),
                              reads=[Ops, lt], writes=[ot])
                    for c2 in range(2):
                        kb.store(ot, AOT[g * 2 + c2, :, PADC + ti * NT:PADC + (ti + 1) * NT].rearrange(
                            "(a d) t -> d a t", a=2), ot[:, c2 * 2:c2 * 2 + 2, :], ddst=[ao_b[ti]])
                    kb.end_scope()
        kb.end_phase()

    def phase_rg(li, j):
        XT, xt_b = XTS[li % 2], xt_bs[li % 2]
        W = NT + 3
        sp_ = splits(W)
        wname = "mix_in%d" % li
        cw = load_T(w_in["rg_conv_w"][j], 4, 16, csz=96, persistent=False)
        cb = load_T(w_in["rg_conv_b"][j:j + 1, :], 1, 16, csz=96, persistent=False)
        gb = load_T(w_in["rg_gate_b"][j].rearrange("d n e -> (d n) e"), 32, 2, csz=96, persistent=False)
        lam = load_T(w_in["rg_lambda"][j], 2, 16, csz=96, persistent=False)
        kb.op(act, lambda: A.activation(out=lam[:], in_=lam[:], func=AF.Exp, scale=-1.0), reads=[lam], writes=[lam])
        kb.op(dve, lambda: V.tensor_scalar(out=lam[:], in0=lam[:], scalar1=1.0, scalar2=None, op0=ALU.add),
              reads=[lam], writes=[lam])
        kb.op(act, lambda: A.activation(out=lam[:], in_=lam[:], func=AF.Ln), reads=[lam], writes=[lam])
        nsp = kb.sb([96, 16, 2], F32, "nsp")
        kb.op(dve, lambda: V.tensor_scalar(out=nsp[:], in0=lam[:], scalar1=-8.0, scalar2=None, op0=ALU.mult),
              reads=[lam], writes=[nsp])
        nsp2 = kb.sb([96, 16, 2], F32, "nsp2")
        kb.op(dve, lambda: V.tensor_scalar(out=nsp2[:], in0=lam[:], scalar1=-16.0, scalar2=None, op0=ALU.mult),
              reads=[lam], writes=[nsp2])
        gwf = kb.sb([96, 16, 192], F32, "gwf")
        gw = kb.sb([96, 2, 16, 192], BF16, "gw")
        for d in range(2):
            kb.load(gwf, gwf[:], w_in["rg_gate_w"][j, d].rearrange("n c e -> c n e"))
            kb.op(dve, lambda: V.tensor_copy(out=gw[:, d], in_=gwf[:]), reads=[gwf], writes=[gw])

        carry = kb.sb([96, 16], F32, "carry")
        pi = [0]
        cnt = [0]

        def nextps():
            p = ps[pi[0] % 8]
            pi[0] += 1
            return p

        def rev(a, n):
            st = a.ap[-1][0]
            return bass.AP(a.tensor, a.offset + (n - 1) * st, [list(x) for x in a.ap[:-1]] + [[-st, n]])

        for d in range(2):
            for si in range(len(kb.seqs)):
                tl = seq_tiles(si)
                order = tl if d == 0 else tl[::-1]
                kb.op(dve, lambda: V.memset(carry[:], 0.0), reads=[], writes=[carry])
                for ti in order:
                    kb.begin_scope("rg")
                    ps_alloc()
                    ws = WStream()
                    xt_s = [kb.sb([128, 8, W], F32, "xt") for _ in range(2)]
                    ht = kb.sb([128, 8, W], BF16, "ht")
                    sq = [kb.sb([128, W], BF16, "sq") for _ in range(2)]
                    rinv = kb.sb([128, W], F32, "rinv")
                    uev = [kb.sb([96, W], F32, "uev") for _ in range(2)]
                    c1 = [kb.sb([96, NT], F32, "c1") for _ in range(2)]
                    c2 = [kb.sb([96, NT], F32, "c2") for _ in range(2)]
                    uc = [kb.sb([96, NT], F32, "uc") for _ in range(2)]
                    ucb = [kb.sb([96, NT], BF16, "ucb") for _ in range(2)]
                    rr = [kb.sb([96, NT], F32, "rr") for _ in range(2)]
                    ig = [kb.sb([96, NT], F32, "ig") for _ in range(2)]
                    aa = [kb.sb([96, NT], F32, "aa") for _ in range(2)]
                    bb = [kb.sb([96, NT], F32, "bb") for _ in range(2)]
                    hs_s = [kb.sb([96, NT], F32, "hs") for _ in range(3)]
                    hf_s = [kb.sb([96, NT], F32, "hfl") for _ in range(3)]
                    gg = [kb.sb([96, NT], F32, "gg") for _ in range(2)]
                    g2 = [kb.sb([96, NT], F32, "g2") for _ in range(2)]
                    yo_s = [kb.sb([96, NT], BF16, "yo") for _ in range(3)]
                    xt = xt_s[ti % 2]
                    c0 = PADC + ti * NT - 2
                    nb = [xt_b[t] for t in (ti - 1, ti, ti + 1) if 0 <= t < ntiles]
                    kb.load(xt, xt[:], XT[:, :, c0:c0 + W].rearrange("c p t -> p c t"), dsrc=nb)
                    pp = [nextps() for _ in sp_]
                    rmsnorm_fm(xt, ht, W, sq, rinv, pp)
                    if ti == tl[0]:
                        kb.op(pool, lambda: P_.memset(ht[:, :, 0:2], 0.0), writes=[ht])
                    if ti == tl[-1]:
                        kb.op(pool, lambda: P_.memset(ht[:, :, W - 1:W], 0.0), writes=[ht])
                    def rg_block(n):
                        k2 = n % 2
                        k3 = cnt[0] % 3
                        cnt[0] += 1
                        hs = hs_s[k3]
                        dn = d * 16 + n
                        wsl, wv = ws.get(wname, 16 + n, 1)
                        pu = [nextps() for _ in sp_]
                        for kc in range(8):
                            for bi, (a, b) in enumerate(sp_):
                                kb.mm(pu[bi], pu[bi][0:96, 0:b - a], wv[:, 0, kc, :], ht[:, kc, a:b], kc == 0, kc == 7,
                                      reads=[wsl, ht])
                        ue = uev[k2]
                        yield
                        for bi, (a, b) in enumerate(sp_):
                            kb.op(act, lambda: A.copy(out=ue[:, a:b], in_=pu[bi][0:96, 0:b - a]), reads=[pu[bi]],
                                  writes=[ue])
                        kb.op(pool, lambda: P_.tensor_scalar(out=c1[k2][:], in0=ue[:, 0:NT], scalar1=cw[:, n, 0:1],
                                                             scalar2=cb[:, n, 0:1], op0=ALU.mult, op1=ALU.add),
                              reads=[ue, cw, cb], writes=[c1[k2]])
                        kb.op(dve, lambda: V.scalar_tensor_tensor(out=c2[k2][:], in0=ue[:, 1:1 + NT],
                                                                    scalar=cw[:, n, 1:2], in1=c1[k2][:],
                                                                    op0=ALU.mult, op1=ALU.add),
                              reads=[ue, cw, c1[k2]], writes=[c2[k2]])
                        kb.op(dve, lambda: V.scalar_tensor_tensor(out=c1[k2][:], in0=ue[:, 2:2 + NT],
                                                                    scalar=cw[:, n, 2:3], in1=c2[k2][:],
                                                                    op0=ALU.mult, op1=ALU.add),
                              reads=[ue, cw, c2[k2]], writes=[c1[k2]])
                        kb.op(dve, lambda: V.scalar_tensor_tensor(out=uc[k2][:], in0=ue[:, 3:3 + NT],
                                                                  scalar=cw[:, n, 3:4], in1=c1[k2][:],
                                                                  op0=ALU.mult, op1=ALU.add),
                              reads=[ue, cw, c1[k2]], writes=[uc[k2]])
                        kb.op(act, lambda: A.copy(out=ucb[k2][:], in_=uc[k2][:]), reads=[uc[k2]], writes=[ucb[k2]])
                        yield
                        pr = nextps()
                        kb.mm(pr, pr[0:96, 0:NT], gw[:, d, n, 0:96], ucb[k2][:], True, True, reads=[gw, ucb[k2]])
                        pg_ = nextps()
                        kb.mm(pg_, pg_[0:96, 0:NT], gw[:, d, n, 96:192], ucb[k2][:], True, True, reads=[gw, ucb[k2]])
                        yield
                        kb.op(act, lambda: A.activation(out=rr[k2][:], in_=pr[0:96, 0:NT], func=AF.Sigmoid,
                                                        bias=gb[:, 0, dn:dn + 1]), reads=[pr, gb], writes=[rr[k2]])
                        kb.op(act, lambda: A.activation(out=ig[k2][:], in_=pg_[0:96, 0:NT], func=AF.Sigmoid,
                                                        bias=gb[:, 1, dn:dn + 1]), reads=[pg_, gb], writes=[ig[k2]])
                        kb.op(act, lambda: A.activation(out=aa[k2][:], in_=rr[k2][:], func=AF.Exp,
                                                        scale=nsp[:, n, d:d + 1]), reads=[rr[k2], nsp], writes=[aa[k2]])
                        kb.op(act, lambda: A.activation(out=bb[k2][:], in_=rr[k2][:], func=AF.Exp,
                                                        scale=nsp2[:, n, d:d + 1]), reads=[rr[k2], nsp2],
                              writes=[bb[k2]])
                        yield
                        kb.op(dve, lambda: V.tensor_scalar(out=bb[k2][:], in0=bb[k2][:], scalar1=-1.0, scalar2=1.0,
                                                           op0=ALU.mult, op1=ALU.add), reads=[bb[k2]], writes=[bb[k2]])
                        kb.op(act, lambda: A.activation(out=bb[k2][:], in_=bb[k2][:], func=AF.Sqrt), reads=[bb[k2]],
                              writes=[bb[k2]])
                        yield
                        kb.op(pool, lambda: P_.tensor_tensor(out=ig[k2][:], in0=ig[k2][:], in1=uc[k2][:], op=ALU.mult),
                              reads=[ig[k2], uc[k2]], writes=[ig[k2]])
                        kb.op(dve, lambda: V.tensor_tensor(out=bb[k2][:], in0=bb[k2][:], in1=ig[k2][:], op=ALU.mult),
                              reads=[bb[k2], ig[k2]], writes=[bb[k2]])
                        yield
                        if d == 0:
                            kb.op(dve, lambda: V.tensor_tensor_scan(out=hs[:], data0=aa[k2][:], data1=bb[k2][:],
                                                                    initial=carry[:, n:n + 1], op0=ALU.mult,
                                                                    op1=ALU.add),
                                  reads=[aa[k2], bb[k2], carry], writes=[hs])
                            kb.op(dve, lambda: V.tensor_copy(out=carry[:, n:n + 1], in_=hs[:, NT - 1:NT]),
                                  reads=[hs], writes=[carry])
                            kb.store(hs, HF[n, :, ti * NT:(ti + 1) * NT], hs[:], ddst=[hf_b[ti]])
                        else:
                            hfl = hf_s[k3]
                            yo = yo_s[k3]
                            kb.load(hfl, hfl[:], HF[n, :, ti * NT:(ti + 1) * NT], dsrc=[hf_b[ti]])
                            kb.op(dve, lambda: V.tensor_tensor_scan(out=rev(hs[:], NT), data0=rev(aa[k2][:], NT),
                                                                    data1=rev(bb[k2][:], NT),
                                                                    initial=carry[:, n:n + 1], op0=ALU.mult,
                                                                    op1=ALU.add),
                                  reads=[aa[k2], bb[k2], carry], writes=[hs])
                            kb.op(dve, lambda: V.tensor_copy(out=carry[:, n:n + 1], in_=hs[:, 0:1]),
                                  reads=[hs], writes=[carry])
                            wsl2, wv2 = ws.get(wname, n, 1)
                            pgt = nextps()
                            for kc in range(8):
                                kb.mm(pgt, pgt[0:96, 0:NT], wv2[:, 0, kc, :], ht[:, kc, 2:2 + NT], kc == 0, kc == 7,
                                      reads=[wsl2, ht])
                            yield
                            G_, G2 = gg[k2], g2[k2]
                            kb.op(act, lambda: A.copy(out=G_[:], in_=pgt[0:96, 0:NT]), reads=[pgt], writes=[G_])
                            kb.op(pool, lambda: P_.tensor_tensor(out=G2[:], in0=G_[:], in1=G_[:], op=ALU.mult),
                                  reads=[G_], writes=[G2])
                            kb.op(pool, lambda: P_.tensor_scalar(out=G2[:], in0=G2[:], scalar1=0.044715, scalar2=1.0,
                                                                 op0=ALU.mult, op1=ALU.add), reads=[G2], writes=[G2])
                            kb.op(pool, lambda: P_.tensor_tensor(out=G2[:], in0=G2[:], in1=G_[:], op=ALU.mult),
                                  reads=[G_, G2], writes=[G2])
                            kb.op(act, lambda: A.activation(out=G2[:], in_=G2[:], func=AF.Sigmoid,
                                                            scale=2.0 * math.sqrt(2.0 / math.pi)), reads=[G2],
                                  writes=[G2])
                            kb.op(pool, lambda: P_.tensor_tensor(out=G_[:], in0=G_[:], in1=G2[:], op=ALU.mult),
                                  reads=[G_, G2], writes=[G_])
                            kb.op(pool, lambda: P_.tensor_tensor(out=hfl[:], in0=hfl[:], in1=hs[:], op=ALU.add),
                                  reads=[hfl, hs], writes=[hfl])
                            kb.op(dve, lambda: V.tensor_tensor(out=yo[:], in0=hfl[:], in1=G_[:], op=ALU.mult),
                                  reads=[hfl, G_], writes=[yo])
                            kb.store(yo, AOT[n, 0:96, PADC + ti * NT:PADC + (ti + 1) * NT], yo[:], ddst=[ao_b[ti]])
                    for n in range(0, 16, 2):
                        interleave([rg_block(n), rg_block(n + 1)])
                    kb.end_scope()
        kb.end_phase()

    phase_transpose_in()
    for li in range(nlayers):
        kind, j = li % 3, li // 3
        if kind == 0:
            phase_qkv(li, 0, j)
            phase_da_attn(li, j)
            phase_ffn(li, 128, 8)
        elif kind == 1:
            phase_qkv(li, 1, j)
            phase_wa_attn(li, j)
            phase_ffn(li, 128, 8)
        else:
            phase_rg(li, j)
            phase_ffn(li, 96, 16)
    phase_transpose_out()
    kb.finish()
    for g in reversed(psg):
        g.__exit__(None, None, None)
    return nc


def make_consts(Lmax):
    ident = np.eye(128, dtype=np.float32)
    rmat = np.zeros((128, 128), np.float32)
    for m in range(128):
        d = m % 64
        if d < 8:
            rmat[m + 8, m] = 1.0
        elif d < 16:
            rmat[m - 8, m] = 1.0
    bones = np.zeros((128, 128), np.float32)
    bones[:64, :64] = 1.0
    bones[64:, 64:] = 1.0
    i = np.arange(128)[:, None]
    c = np.arange(128)[None, :]
    m0 = np.where(i <= c, 0.0, -30000.0).astype(np.float32)
    m1 = np.where(c <= i, 0.0, -30000.0).astype(np.float32)
    mask = np.stack([np.tile(m0, (1, 4)), np.tile(m1, (1, 4))], axis=1).astype(np.float32)
    half = 8
    inv = (ROPE_THETA ** (-np.arange(half, dtype=np.float32) * 2.0 / 16.0)).astype(np.float32)
    pos = np.arange(Lmax, dtype=np.float32)
    ang = pos[None, :] * inv[:, None]
    cosv = np.cos(ang).astype(np.float32)
    sinv = np.sin(ang).astype(np.float32)
    cos_t = np.ones((128, Lmax), np.float32)
    sin_t = np.zeros((128, Lmax), np.float32)
    for p in range(128):
        d = p % 64
        if d < 8:
            cos_t[p] = cosv[d]
            sin_t[p] = -sinv[d]
        elif d < 16:
            cos_t[p] = cosv[d - 8]
            sin_t[p] = sinv[d - 8]
    return {"c_ident": ident, "c_rmat": rmat, "c_bones": bones, "c_mask": mask, "c_cos": cos_t, "c_sin": sin_t}


_WNAMES = ["norm_mix", "norm_ffn", "ffn_w_up", "ffn_conv_w", "ffn_conv_b", "ffn_w_down", "da_w_qkv", "da_q_norm",
           "da_k_norm", "da_lambda_q1", "da_lambda_k1", "da_lambda_q2", "da_lambda_k2", "da_sub_norm", "da_w_o",
           "wa_w_qkv", "wa_q_norm", "wa_k_norm", "wa_sink", "wa_w_o", "rg_w_in", "rg_conv_w", "rg_conv_b",
           "rg_gate_w", "rg_gate_b", "rg_lambda", "rg_w_out"]


def run(x_prompt, x_sample, weights, nlayers=4):
    Bp, Sp, _ = x_prompt.shape
    Bs, Ss, _ = x_sample.shape
    ncores = 8
    per = Bs // ncores
    seqs = [Sp] + [Ss] * per
    nc = build(seqs, nlayers)
    consts = make_consts(max(seqs))
    wts = {k: np.ascontiguousarray(np.asarray(weights[k], dtype=np.float32)) for k in _WNAMES}
    in_maps = []
    for c in range(ncores):
        xp = x_prompt[c % Bp]
        xs = [x_sample[c * per + i] for i in range(per)]
        xc = np.ascontiguousarray(np.concatenate([xp] + xs, axis=0).astype(np.float32))
        m = {"x": xc}
        m.update(wts)
        m.update(consts)
        in_maps.append(m)
    res = run_bass_kernel_spmd(nc, in_maps, core_ids=list(range(ncores)))
    yp = np.stack([res.results[c]["y"][:Sp] for c in range(Bp)], axis=0)
    ys = np.stack([res.results[c]["y"][Sp + i * Ss:Sp + (i + 1) * Ss] for c in range(ncores) for i in range(per)], axis=0)
    return yp.astype(np.float32), ys.astype(np.float32)


def kernel(**inputs):
    x_prompt = np.asarray(inputs["x_prompt"])
    x_sample = np.asarray(inputs["x_sample"])
    return run(x_prompt, x_sample, inputs, 4)
```
